# Optimizing a Trainium2 kernel written in Bass

```python
import math
import jax
import jax.numpy as jnp
from jax import lax
import numpy as np

D_MODEL = 1024
BATCH = 16
SEQ = 2048
DEPTH = 4

CTX_LEN = 256
GRID_W = 64
N_EVEN = (DEPTH + 1) // 2
N_ODD = DEPTH // 2
D_FF = 4 * D_MODEL
EPS = 1e-6
ROPE_BASE = 10000.0
MIX_A = D_MODEL // 2
MLSTM_HEADS = 4
MLSTM_HD = MIX_A // MLSTM_HEADS
MLSTM_CHUNK = 64
N_GATES = 4 * MLSTM_HEADS
FORGET_BIAS_LO = 3.0
FORGET_BIAS_HI = 6.0
MIX_B = D_MODEL - MIX_A
HYENA_ORDER = 2
HYENA_SHORT = 3
HYENA_EMB = 33
HYENA_FFN = 64
HYENA_TARGET = 1e-2
HYENA_FAST = 0.3
HYENA_SLOW = 1.5
E_IN = 4 * MIX_A + N_GATES + (HYENA_ORDER + 1) * MIX_B
NA_HEADS = 16
NA_HD = D_MODEL // NA_HEADS
NA_WIN_ROWS = 8
NA_WIN_COLS = 16

kernel_name = "hybrid_mlstm_hyena_natten_dit_trunk"


def rms_norm(x, gain=None):
    x32 = x.astype(jnp.float32)
    y = x32 * lax.rsqrt(jnp.mean(x32 * x32, axis=-1, keepdims=True) + EPS)
    if gain is not None:
        y = y * gain.astype(jnp.float32)
    return y.astype(x.dtype)


def rope_1d(xa, pos):
    nf = xa.shape[-1] // 2
    inv = ROPE_BASE ** (-jnp.arange(nf, dtype=jnp.float32) / nf)
    ang = pos.astype(jnp.float32)[:, None] * inv[None, :]
    cos, sin = jnp.cos(ang), jnp.sin(ang)
    x1, x2 = xa[..., :nf], xa[..., nf:]
    return jnp.concatenate([x1 * cos - x2 * sin, x1 * sin + x2 * cos], axis=-1)


def axial_rope(x, rows, cols):
    half = x.shape[-1] // 2
    return jnp.concatenate([rope_1d(x[..., :half], rows), rope_1d(x[..., half:], cols)], axis=-1)


def mlstm_zero_state(b):
    f32 = jnp.float32
    return (jnp.zeros((b, MLSTM_HEADS, MLSTM_HD, MLSTM_HD), f32),
            jnp.zeros((b, MLSTM_HEADS, MLSTM_HD), f32),
            jnp.zeros((b, MLSTM_HEADS), f32))


def mlstm_chunkwise(q, k, v, ig, lf, state):
    B, H, L, dh = q.shape
    nc = L // MLSTM_CHUNK

    def to_chunks(a):
        a = a.reshape(a.shape[:2] + (nc, MLSTM_CHUNK) + a.shape[3:])
        return jnp.moveaxis(a, 2, 0)

    tri = jnp.tril(jnp.ones((MLSTM_CHUNK, MLSTM_CHUNK), dtype=bool))

    def step(carry, inp):
        C, n, m = carry
        qc, kc, vc, ic, fc = inp
        b = jnp.cumsum(fc, axis=-1)
        dmat = b[..., :, None] - b[..., None, :] + ic[..., None, :]
        dmat = jnp.where(tri, dmat, -jnp.inf)
        inter = b + m[..., None]
        m_t = jnp.maximum(inter, jnp.max(dmat, axis=-1))
        w = jnp.exp(dmat - m_t[..., None])
        sc = jnp.exp(inter - m_t)
        qk = jnp.einsum('bhtd,bhsd->bhts', qc, kc) * w
        num = sc[..., None] * jnp.einsum('bhvk,bhtk->bhtv', C, qc) + jnp.einsum('bhts,bhsv->bhtv', qk, vc)
        den = sc * jnp.einsum('bhk,bhtk->bht', n, qc) + jnp.sum(qk, axis=-1)
        h = num / jnp.maximum(jnp.abs(den), jnp.exp(-m_t))[..., None]
        bl = b[..., -1]
        g = bl[..., None] - b + ic
        m_new = jnp.maximum(bl + m, jnp.max(g, axis=-1))
        decay = jnp.exp(bl + m - m_new)
        wk = jnp.exp(g - m_new[..., None])
        C_new = decay[..., None, None] * C + jnp.einsum('bhs,bhsv,bhsk->bhvk', wk, vc, kc)
        n_new = decay[..., None] * n + jnp.einsum('bhs,bhsk->bhk', wk, kc)
        return (C_new, n_new, m_new), h

    state, hs = lax.scan(step, state, tuple(to_chunks(a) for a in (q, k, v, ig, lf)))
    h = jnp.moveaxis(hs, 0, 2).reshape(B, H, L, dh)
    return h, state


def mlstm_bidir(q, k, v, gpre, init):
    B, L, _ = gpre.shape
    g = gpre.reshape(B, L, 4, MLSTM_HEADS).transpose(2, 0, 3, 1)
    h_f, s_f = mlstm_chunkwise(q, k, v, g[0], jax.nn.log_sigmoid(g[1]), init[0])
    fl = lambda a: jnp.flip(a, axis=2)
    h_b, s_b = mlstm_chunkwise(fl(q), fl(k), fl(v), fl(g[2]), fl(jax.nn.log_sigmoid(g[3])), init[1])
    return h_f + fl(h_b), (s_f, s_b)


def short_conv(x, w, b):
    L = x.shape[1]
    pad = HYENA_SHORT // 2
    xp = jnp.pad(x, ((0, 0), (pad, pad), (0, 0)))
    y = b
    for j in range(HYENA_SHORT):
        y = y + xp[:, j:j + L] * w[j]
    return y


def hyena_filters(L, w1, b1, w2, b2, w3, freq):
    f32 = jnp.float32
    t = jnp.linspace(0.0, 1.0, L, dtype=f32)[:, None]
    bands = (HYENA_EMB - 1) // 2
    wpos = 2.0 * math.pi * jnp.arange(L, dtype=f32) / L
    fr = jnp.linspace(1e-4, bands - 1, bands, dtype=f32)
    ang = wpos[:, None] * fr[None, :]
    emb = jnp.concatenate([t, jnp.cos(ang), -jnp.sin(ang)], axis=-1)
    h = jnp.sin(freq * (emb @ w1 + b1))
    h = jnp.sin(freq * (h @ w2 + b2))
    h = (h @ w3).astype(f32).reshape(L, HYENA_ORDER, 2, MIX_B)
    deltas = jnp.abs(jnp.linspace(math.log(HYENA_TARGET) / HYENA_SLOW, math.log(HYENA_TARGET) / HYENA_FAST, MIX_B, dtype=f32))
    h = h * jnp.exp(-t * deltas)[:, None, None, :]
    return h * lax.rsqrt(jnp.sum(h * h, axis=0, keepdims=True) + EPS)


def fft_conv(z, h):
    L = z.shape[1]
    n = 2 * L
    zf = jnp.fft.rfft(z, n=n, axis=1)
    hf = jnp.fft.rfft(h, n=n, axis=0)
    return jnp.fft.irfft(zf * hf[None], n=n, axis=1)[:, :L]


def bidir_long_conv(z, h_fwd, h_bwd):
    return fft_conv(z, h_fwd) + jnp.flip(fft_conv(jnp.flip(z, axis=1), h_bwd), axis=1)


def hyena(u, x1, x2, filt, dskip):
    z = u.astype(jnp.float32)
    for o, gate in enumerate((x1, x2)):
        z = gate.astype(jnp.float32) * (bidir_long_conv(z, filt[:, o, 0], filt[:, o, 1]) + dskip[o] * z)
    return z


def even_mixer(hx, hc, rows, cols, w_in, gate_b, hnorm, conv_w, conv_b, f_w1, f_b1, f_w2, f_b2, f_w3, f_freq, hy_d, w_out, ctx_out):
    f32 = jnp.float32
    cuts = [MIX_A, 2 * MIX_A, 3 * MIX_A, 4 * MIX_A, 4 * MIX_A + N_GATES]

    def heads(a):
        b_, l_, _ = a.shape
        return a.reshape(b_, l_, MLSTM_HEADS, MLSTM_HD).transpose(0, 2, 1, 3).astype(f32)

    def project(h, rope):
        q, k, v, o, g, hy = jnp.split(h @ w_in, cuts, axis=-1)
        q, k, v = heads(q), heads(k) * MLSTM_HD ** -0.5, heads(v)
        if rope:
            q, k = axial_rope(q, rows, cols), axial_rope(k, rows, cols)
        return q, k, v, o, g.astype(f32) + gate_b, hy

    def combine(hm, o, hy, dtype):
        b_, h_, l_, d_ = hm.shape
        hm = rms_norm(hm, hnorm.reshape(MLSTM_HEADS, 1, MLSTM_HD))
        a_out = hm.transpose(0, 2, 1, 3).reshape(b_, l_, h_ * d_) * jax.nn.sigmoid(o.astype(f32))
        u, x1, x2 = jnp.split(short_conv(hy, conv_w, conv_b), 3, axis=-1)
        filt = hyena_filters(l_, f_w1, f_b1, f_w2, f_b2, f_w3, f_freq)
        b_out = hyena(u, x1, x2, filt, hy_d)
        return (jnp.concatenate([a_out, b_out], axis=-1) @ w_out).astype(dtype)

    qc, kc, vc, oc, gc, hyc = project(hc, False)
    hm_c, ctx_states = mlstm_bidir(qc, kc, vc, gc, (mlstm_zero_state(hc.shape[0]), mlstm_zero_state(hc.shape[0])))
    q, k, v, o, g, hy = project(hx, True)
    hm_x, _ = mlstm_bidir(q, k, v, g, ctx_states)
    out_x = combine(hm_x, o, hy, hx.dtype)
    out_c = combine(hm_c, oc, hyc, hc.dtype) if ctx_out else None
    return out_x, out_c


def odd_mixer(hx, hc, w_qkv, qn, kn, rpb, w_out, ctx_out):
    f32 = jnp.float32

    def project(h):
        b_, l_, _ = h.shape
        p = (h @ w_qkv).reshape(b_, l_, 3, NA_HEADS, NA_HD)
        q = rms_norm(p[:, :, 0], qn) * NA_HD ** -0.5
        k = rms_norm(p[:, :, 1], kn)
        v = p[:, :, 2]
        return tuple(a.transpose(0, 2, 1, 3) for a in (q, k, v))

    qc, kc, vc = project(hc)
    q, k, v = project(hx)
    B, H, S, d = q.shape
    rows_n = S // GRID_W
    wr = min(NA_WIN_ROWS, rows_n)
    nk = wr * GRID_W
    q_g = q.reshape(B, H, rows_n, GRID_W, d)
    k_g = k.reshape(B, H, rows_n, GRID_W, d)
    v_g = v.reshape(B, H, rows_n, GRID_W, d)
    cidx = jnp.arange(GRID_W)
    cstart = jnp.clip(cidx - NA_WIN_COLS // 2, 0, GRID_W - NA_WIN_COLS)
    cmask = (cidx[None, :] >= cstart[:, None]) & (cidx[None, :] < cstart[:, None] + NA_WIN_COLS)
    blk_mask = jnp.broadcast_to(cmask[:, None, :], (GRID_W, wr, GRID_W)).reshape(GRID_W, nk)
    dc_idx = jnp.clip(cidx[None, :] - cidx[:, None] + NA_WIN_COLS - 1, 0, 2 * NA_WIN_COLS - 2)

    def row_block(args):
        q_r, r = args
        rs = jnp.clip(r - wr // 2, 0, rows_n - wr)
        k_blk = lax.dynamic_slice_in_dim(k_g, rs, wr, axis=2).reshape(B, H, nk, d)
        v_blk = lax.dynamic_slice_in_dim(v_g, rs, wr, axis=2).reshape(B, H, nk, d)
        dr_idx = rs + jnp.arange(wr) - r + NA_WIN_ROWS - 1
        bias = rpb[:, dr_idx[None, :, None], dc_idx[:, None, :]].reshape(H, GRID_W, nk)
        s_lat = jnp.einsum('bhqd,bhkd->bhqk', q_r, k_blk).astype(f32) + bias.astype(f32)
        s_lat = jnp.where(blk_mask, s_lat, -jnp.inf)
        s_ctx = jnp.einsum('bhqd,bhkd->bhqk', q_r, kc).astype(f32)
        p = jax.nn.softmax(jnp.concatenate([s_lat, s_ctx], axis=-1), axis=-1).astype(v.dtype)
        return (jnp.einsum('bhqk,bhkd->bhqd', p[..., :nk], v_blk)
                + jnp.einsum('bhqk,bhkd->bhqd', p[..., nk:], vc))

    o = lax.map(row_block, (jnp.moveaxis(q_g, 2, 0), jnp.arange(rows_n)))
    o = o.transpose(1, 0, 3, 2, 4).reshape(B, S, H * d)
    out_x = (o @ w_out).astype(hx.dtype)
    out_c = None
    if ctx_out:
        s = jnp.einsum('bhqd,bhkd->bhqk', qc, kc).astype(f32)
        pc = jax.nn.softmax(s, axis=-1).astype(vc.dtype)
        oc = jnp.einsum('bhqk,bhkd->bhqd', pc, vc).transpose(0, 2, 1, 3).reshape(B, -1, H * d)
        out_c = (oc @ w_out).astype(hc.dtype)
    return out_x, out_c


def sq_relu_mlp(h, w1, w2):
    return jnp.square(jax.nn.relu(h @ w1)) @ w2


def setup_inputs(seed: int = 0) -> dict:
    key = jax.random.key(seed)
    ks = iter(jax.random.split(key, 32))
    nrm = lambda shape, s: jax.random.normal(next(ks), shape, jnp.float32) * s
    D = D_MODEL
    H = MLSTM_HEADS
    f_lin = jnp.linspace(FORGET_BIAS_LO, FORGET_BIAS_HI, H, dtype=jnp.float32)
    gate_base = jnp.concatenate([jnp.zeros((H,), jnp.float32), f_lin, jnp.zeros((H,), jnp.float32), f_lin])
    return {
        "x": nrm((BATCH, SEQ, D), 1.0),
        "c": nrm((BATCH, D), 1.0),
        "ctx": nrm((BATCH, CTX_LEN, D), 1.0),
        "c_ctx": nrm((D,), 1.0),
        "w_mod": nrm((DEPTH, D, 6 * D), 0.5 * D ** -0.5),
        "b_mod": nrm((DEPTH, 6 * D), 0.01),
        "w_mlp_in": nrm((DEPTH, D, D_FF), D ** -0.5),
        "w_mlp_out": nrm((DEPTH, D_FF, D), D_FF ** -0.5),
        "e_w_in": nrm((N_EVEN, D, E_IN), D ** -0.5),
        "e_gate_b": gate_base[None, :] + nrm((N_EVEN, N_GATES), 0.1),
        "e_hnorm": 1.0 + nrm((N_EVEN, MIX_A), 0.02),
        "e_conv_w": nrm((N_EVEN, HYENA_SHORT, (HYENA_ORDER + 1) * MIX_B), HYENA_SHORT ** -0.5),
        "e_conv_b": nrm((N_EVEN, (HYENA_ORDER + 1) * MIX_B), 0.01),
        "e_f_w1": nrm((N_EVEN, HYENA_EMB, HYENA_FFN), HYENA_EMB ** -0.5),
        "e_f_b1": nrm((N_EVEN, HYENA_FFN), 0.1),
        "e_f_w2": nrm((N_EVEN, HYENA_FFN, HYENA_FFN), HYENA_FFN ** -0.5),
        "e_f_b2": nrm((N_EVEN, HYENA_FFN), 0.1),
        "e_f_w3": nrm((N_EVEN, HYENA_FFN, HYENA_ORDER * 2 * MIX_B), HYENA_FFN ** -0.5),
        "e_f_freq": 1.0 + nrm((N_EVEN, HYENA_FFN), 0.02),
        "e_hy_d": nrm((N_EVEN, HYENA_ORDER, MIX_B), 0.5),
        "e_w_out": nrm((N_EVEN, MIX_A + MIX_B, D), (MIX_A + MIX_B) ** -0.5),
        "o_w_qkv": nrm((N_ODD, D, 3 * D), D ** -0.5),
        "o_qn": 1.0 + nrm((N_ODD, NA_HD), 0.02),
        "o_kn": 1.0 + nrm((N_ODD, NA_HD), 0.02),
        "o_rpb": nrm((N_ODD, NA_HEADS, 2 * NA_WIN_ROWS - 1, 2 * NA_WIN_COLS - 1), 0.02),
        "o_w_out": nrm((N_ODD, D, D), D ** -0.5),
    }


def reference(x, c, ctx, c_ctx, w_mod, b_mod, w_mlp_in, w_mlp_out, e_w_in, e_gate_b, e_hnorm, e_conv_w, e_conv_b, e_f_w1, e_f_b1, e_f_w2, e_f_b2, e_f_w3, e_f_freq, e_hy_d, e_w_out, o_w_qkv, o_qn, o_kn, o_rpb, o_w_out):
    S = x.shape[1]
    t = jnp.arange(S)
    rows, cols = t // GRID_W, t % GRID_W
    silu_c = jax.nn.silu(c)
    silu_cc = jax.nn.silu(c_ctx)
    xc = ctx
    for l in range(DEPTH):
        last = l == DEPTH - 1
        mod_x = (silu_c @ w_mod[l] + b_mod[l])[:, None, :]
        mod_c = silu_cc @ w_mod[l] + b_mod[l]
        sh1, sc1, g1, sh2, sc2, g2 = jnp.split(mod_x, 6, axis=-1)
        csh1, csc1, cg1, csh2, csc2, cg2 = jnp.split(mod_c, 6, axis=-1)
        hx = rms_norm(x) * (1.0 + sc1) + sh1
        hc = rms_norm(xc) * (1.0 + csc1) + csh1
        i = l // 2
        if l % 2 == 0:
            mx, mc = even_mixer(hx, hc, rows, cols, e_w_in[i], e_gate_b[i], e_hnorm[i], e_conv_w[i], e_conv_b[i],
                                e_f_w1[i], e_f_b1[i], e_f_w2[i], e_f_b2[i], e_f_w3[i], e_f_freq[i], e_hy_d[i],
                                e_w_out[i], not last)
        else:
            mx, mc = odd_mixer(hx, hc, o_w_qkv[i], o_qn[i], o_kn[i], o_rpb[i], o_w_out[i], not last)
        x = x + g1 * mx
        x = x + g2 * sq_relu_mlp(rms_norm(x) * (1.0 + sc2) + sh2, w_mlp_in[l], w_mlp_out[l])
        if not last:
            xc = xc + cg1 * mc
            xc = xc + cg2 * sq_relu_mlp(rms_norm(xc) * (1.0 + csc2) + csh2, w_mlp_in[l], w_mlp_out[l])
    return x
```

```python
import os
import math
import numpy as np
import ml_dtypes
from contextlib import ExitStack
import concourse.bass as bass
import concourse.mybir as mybir
from concourse.bass_utils import run_bass_kernel_spmd

F32 = mybir.dt.float32
BF16 = mybir.dt.bfloat16
ALU = mybir.AluOpType
AF = mybir.ActivationFunctionType
AX = mybir.AxisListType

NCORES = 8
NB = 2
D = 1024
SEQ = 2048
CTX = 256
TT = SEQ + CTX
NTILE = TT // 128
DEPTH = 4
DFF = 4096
EPS = 1e-6
E_IN = 3600
GRID_W = 64
PI = math.pi


class T:
    __slots__ = ("ap", "lw", "rd", "name")

    def __init__(self, ap, name=""):
        self.ap = ap
        self.lw = None
        self.rd = []
        self.name = name

    def __getitem__(self, idx):
        return self.ap[idx]


class Sched:
    NDMA = 10

    def __init__(self, nc):
        self.nc = nc
        self.es = ExitStack()
        self.engs = {"pe": nc.tensor, "act": nc.scalar, "dve": nc.vector, "pool": nc.gpsimd, "sp": nc.sync}
        self.sem = {}
        self.cnt = {}
        for k in self.engs:
            self.sem[k] = self.es.enter_context(nc.semaphore("s_" + k))
            self.cnt[k] = 0
        self.dq = {}
        for q in ("sp", "act", "pool"):
            sems = [self.es.enter_context(nc.semaphore(f"d_{q}{i}")) for i in range(self.NDMA)]
            self.dq[q] = {"sems": sems, "n": 0}
        self.known = {k: {} for k in self.engs}
        self.n_wait = 0
        self.n_inst = 0
        self.pe_pend = None
        self.pe_pend_w = None

    def _flush_pe(self):
        if self.pe_pend is not None:
            self.cnt["pe"] += 1
            self.pe_pend.then_inc(self.sem["pe"], 1)
            self.pe_pend = None
            self.pe_pend_w = None

    def _wait(self, e, tk):
        if tk is None:
            return
        key, sem, val, src = tk
        if src == e and e == "pe":
            return
        if src == "pe" and val > self.cnt["pe"]:
            self._flush_pe()
        kn = self.known[e]
        if kn.get(key, 0) >= val:
            return
        kn[key] = val
        self.engs[e].wait_ge(sem, val)
        self.n_wait += 1

    def _deps(self, e, reads, writes):
        for b in reads:
            self._wait(e, b.lw)
        for b in writes:
            self._wait(e, b.lw)
            for t in b.rd:
                self._wait(e, t)

    def _commit(self, tk, reads, writes):
        for b in reads:
            b.rd.append(tk)
            if len(b.rd) > 48:
                best = {}
                for t in b.rd:
                    if t[0] not in best or best[t[0]][2] < t[2]:
                        best[t[0]] = t
                b.rd = list(best.values())
        for b in writes:
            b.lw = tk
            b.rd = []

    def op(self, e, fn, reads=(), writes=()):
        self._deps(e, reads, writes)
        if e == "pe":
            w0 = writes[0] if writes else None
            if self.pe_pend is not None and self.pe_pend_w is not w0:
                self._flush_pe()
            ins = fn(self.engs[e])
            self.pe_pend = ins
            self.pe_pend_w = w0
            tk = (e, self.sem[e], self.cnt[e] + 1, e)
        else:
            ins = fn(self.engs[e])
            self.cnt[e] += 1
            ins.then_inc(self.sem[e], 1)
            tk = (e, self.sem[e], self.cnt[e], e)
        self._commit(tk, reads, writes)
        self.n_inst += 1
        return tk

    def dma(self, q, out, in_, reads=(), writes=(), **kw):
        d = self.dq[q]
        j = d["n"]
        i = j % self.NDMA
        rnd = j // self.NDMA
        sem = d["sems"][i]
        key = f"d_{q}{i}"
        if rnd > 0:
            self._wait(q, (key, sem, 16 * rnd, None))
        self._deps(q, reads, writes)
        ins = self.engs[q].dma_start(out=out, in_=in_, **kw)
        ins.then_inc(sem, 16)
        d["n"] = j + 1
        tk = (key, sem, 16 * (rnd + 1), None)
        self._commit(tk, reads, writes)
        self.n_inst += 1
        return tk

    def all_tickets(self):
        self._flush_pe()
        tks = []
        for k in self.engs:
            if self.cnt[k] > 0:
                tks.append((k, self.sem[k], self.cnt[k], k))
        for q, d in self.dq.items():
            j = d["n"]
            for i in range(min(j, self.NDMA)):
                last = ((j - 1 - i) // self.NDMA) * self.NDMA + i
                tks.append((f"d_{q}{i}", d["sems"][i], 16 * (last // self.NDMA + 1), None))
        return tks

    def barrier(self, engines=None):
        tks = self.all_tickets()
        for e in (engines or self.engs):
            for tk in tks:
                if tk[3] == e:
                    continue
                self._wait(e, tk)


class Phase:
    def __init__(self, S, name):
        self.S = S
        self.nc = S.nc
        self.name = name
        self.es = ExitStack()
        self.k = 0

    def __enter__(self):
        return self

    def sb(self, shape, dt=F32, name=None):
        self.k += 1
        h = self.es.enter_context(self.nc.sbuf_tensor(f"{self.name}_{name or 't'}{self.k}", list(shape), dt))
        return T(h.ap() if hasattr(h, "ap") and callable(h.ap) else h, name or "")

    def ps(self, shape, dt=F32, name=None):
        self.k += 1
        h = self.es.enter_context(self.nc.psum_tensor(f"{self.name}_{name or 'p'}{self.k}", list(shape), dt))
        return T(h.ap() if hasattr(h, "ap") and callable(h.ap) else h, name or "")

    def __exit__(self, *a):
        self.S.barrier()
        self.es.close()
        return False


def _bf(a):
    return np.asarray(a, np.float32).astype(ml_dtypes.bfloat16)


def host_consts():
    c = {}
    c["ident_b"] = _bf(np.eye(128))
    c["ident_f"] = np.eye(128, dtype=np.float32)
    c["jrev_b"] = _bf(np.eye(128)[::-1])
    s = np.arange(128)
    mf = (s[:, None] <= s[None, :]).astype(np.float32)
    mb = (s[:, None] >= s[None, :]).astype(np.float32)
    c["tri_f"] = np.stack([mf, mb, np.ones((128, 128), np.float32)], 0)
    c["tri_b"] = _bf(np.stack([mf, mb], 0))
    t = np.arange(SEQ)
    rows, cols = t // GRID_W, t % GRID_W
    nf = 32
    inv = (10000.0 ** (-np.arange(nf, dtype=np.float32) / nf)).astype(np.float32)
    ang_r = rows.astype(np.float32)[:, None] * inv[None, :]
    ang_c = cols.astype(np.float32)[:, None] * inv[None, :]
    cosT = np.concatenate([np.cos(ang_r), np.cos(ang_c)], 1).astype(np.float32)
    sinT = np.concatenate([np.sin(ang_r), np.sin(ang_c)], 1).astype(np.float32)
    ks = np.float32(128 ** -0.5)
    c["rope"] = np.stack([cosT, sinT, cosT * ks, sinT * ks], 0).astype(np.float32)
    for nm, L in (("L", SEQ), ("C", CTX)):
        tt = np.linspace(0.0, 1.0, L, dtype=np.float32)[:, None]
        bands = 16
        wpos = (2.0 * np.pi * np.arange(L, dtype=np.float32) / L).astype(np.float32)
        fr = np.linspace(1e-4, bands - 1, bands, dtype=np.float32)
        ang = wpos[:, None] * fr[None, :]
        emb = np.concatenate([tt, np.cos(ang), -np.sin(ang)], -1).astype(np.float32)
        c["embT_" + nm] = np.ascontiguousarray(emb.T)
        deltas = np.abs(np.linspace(math.log(1e-2) / 1.5, math.log(1e-2) / 0.3, 512, dtype=np.float32))
        c["dec_" + nm] = np.ascontiguousarray(np.exp(-tt * deltas[None, :]).T.astype(np.float32))
        c["embTr_" + nm] = np.ascontiguousarray(c["embT_" + nm][:, ::-1])
        c["decr_" + nm] = np.ascontiguousarray(c["dec_" + nm][:, ::-1])
    cidx = np.arange(GRID_W)
    cstart = np.clip(cidx - 8, 0, GRID_W - 16)
    cmask = (cidx[None, :] >= cstart[:, None]) & (cidx[None, :] < cstart[:, None] + 16)
    c["cmaskT"] = np.ascontiguousarray(cmask.T.astype(np.float32))
    return c


def rpb_expand(o_rpb):
    cidx = np.arange(GRID_W)
    dc = np.clip(cidx[:, None] - cidx[None, :] + 15, 0, 30)
    r = o_rpb[:, :, :, dc]
    return np.ascontiguousarray(r.transpose(0, 1, 3, 2, 4))


class Builder:
    def __init__(self, depth=DEPTH, debug=False, hy_pool_frac=0.0, hy_pe=True, hy_kb=64):
        self.depth = depth
        self.debug = debug
        self.hy_pool_frac = hy_pool_frac
        self.hy_pe = hy_pe
        self.hy_kb = hy_kb
        self.nc = bass.Bass("TRN2", target_bir_lowering=False)
        self.S = Sched(self.nc)
        self.inp = {}
        self.scr = {}
        self.feed = set()
        self.only = None

    IN_SHAPES = {
        "x": ([NB, SEQ, D], F32), "ctx": ([NB, CTX, D], F32), "cT": ([128, 8, 3], F32),
        "w_mod": ([DEPTH, D, 6 * D], F32), "b_mod": ([DEPTH, 6 * D], F32),
        "w_mlp_in": ([DEPTH, D, DFF], F32), "w_mlp_out": ([DEPTH, DFF, D], F32),
        "e_w_in": ([2, D, E_IN], F32), "e_gate_b": ([2, 16], F32), "e_hnorm": ([2, 512], F32),
        "e_conv_w": ([2, 3, 1536], F32), "e_conv_b": ([2, 1536], F32),
        "e_f_w1": ([2, 33, 64], F32), "e_f_b1": ([2, 64], F32), "e_f_w2": ([2, 64, 64], F32), "e_f_b2": ([2, 64], F32),
        "e_f_w3": ([2, 64, 2048], F32), "e_f_freq": ([2, 64], F32), "e_hy_d": ([2, 2, 512], F32),
        "e_w_out": ([2, D, D], F32), "o_w_qkv": ([2, D, 3 * D], F32), "o_qn": ([2, 64], F32), "o_kn": ([2, 64], F32),
        "rpbx": ([2, 16, 64, 15, 64], F32), "o_w_out": ([2, D, D], F32),
        "jrev_b": ([128, 128], BF16), "ident_b": ([128, 128], BF16), "ident_f": ([128, 128], F32), "tri_f": ([3, 128, 128], F32),
        "tri_b": ([2, 128, 128], BF16), "rope": ([4, SEQ, 64], F32),
        "embT_L": ([33, SEQ], F32), "embT_C": ([33, CTX], F32), "dec_L": ([512, SEQ], F32), "dec_C": ([512, CTX], F32),
        "cmaskT": ([64, 64], F32),
        "embTr_L": ([33, SEQ], F32), "embTr_C": ([33, CTX], F32), "decr_L": ([512, SEQ], F32), "decr_C": ([512, CTX], F32),
    }
    SCR_SHAPES = {
        "xs": ([NB, TT, D], F32), "modrow": ([DEPTH, 3, 6 * D], F32),
        "qT": ([NB, 4, 128, TT], BF16), "kT": ([NB, 4, 128, TT], BF16),
        "ktm": ([NB, TT, 512], BF16), "vtm": ([NB, TT, 512], BF16), "osig": ([NB, TT, 512], BF16),
        "gates": ([NB, TT, 16], F32), "hy": ([NB, 1536, TT], F32), "mixT": ([NB, D, TT], BF16),
        "filtL": ([2, 2, 512, SEQ], F32), "filtC": ([2, 2, 512, CTX], F32),
        "gpL": ([2, 512, 2 * SEQ], BF16), "gpC": ([2, 512, 2 * CTX], BF16),
        "oqT": ([NB, 8, 128, TT], BF16), "okT": ([NB, 8, 128, TT], BF16), "ov": ([NB, TT, 16, 64], BF16),
    }

    def declare(self):
        bld = self

        class LazyIn(dict):
            def __missing__(s, name):
                shape, dt = bld.IN_SHAPES[name]
                s[name] = bld.nc.dram_tensor(name, list(shape), dt, kind="ExternalInput").ap()
                return s[name]

        class LazyScr(dict):
            def __missing__(s, name):
                shape, dt = bld.SCR_SHAPES[name]
                if name in bld.feed:
                    kind = "ExternalInput"
                else:
                    kind = "ExternalOutput" if bld.debug else "Internal"
                s[name] = T(bld.nc.dram_tensor(name, list(shape), dt, kind=kind).ap(), name)
                return s[name]

        self.inp = LazyIn()
        self.scr = LazyScr()
        self.out = self.nc.dram_tensor("out", [NB, SEQ, D], F32, kind="ExternalOutput").ap()

    def load_consts(self, P):
        S, I = self.S, self.inp
        self.ident_b = P.sb([128, 128], BF16, "identb")
        S.dma("sp", self.ident_b[:, :], I["ident_b"][:, :], writes=[self.ident_b])

    def phase_init(self):
        S, I = self.S, self.inp
        xs = self.scr["xs"]
        for b in range(NB):
            S.dma("sp", xs[b, 0:SEQ, :], I["x"][b, :, :], writes=[xs])
            S.dma("pool", xs[b, SEQ:TT, :], I["ctx"][b, :, :], writes=[xs])
        S.barrier()

    def phase_mod(self):
        S, I = self.S, self.inp
        modrow = self.scr["modrow"]
        with Phase(S, "mod") as P:
            sc = P.sb([128, 8, 3], F32, "sc")
            S.dma("sp", sc[:, :, :], I["cT"][:, :, :], writes=[sc])
            S.op("act", lambda e: e.activation(sc[:, :, :], sc[:, :, :], AF.Silu), reads=[sc], writes=[sc])
            wst = [P.sb([128, 8, 512], F32, "wst") for _ in range(2)]
            pss = [P.ps([3, 512], F32, "ps") for _ in range(2)]
            bias = P.sb([3, 6 * D], F32, "bias")
            msb = P.sb([3, 6 * D], F32, "msb")
            k = 0
            for l in range(self.depth):
                S.dma("pool", bias[:, :], I["b_mod"][l:l + 1, :].partition_broadcast(3), writes=[bias])
                wv = I["w_mod"][l].rearrange("(j p) n -> p j n", p=128)
                for cg in range(12):
                    w = wst[k % 2]; ps = pss[k % 2]; k += 1
                    S.dma("sp" if cg % 2 == 0 else "act", w[:, :, :], wv[:, :, cg * 512:(cg + 1) * 512], writes=[w])
                    for j in range(8):
                        S.op("pe", lambda e, j=j, w=w, ps=ps: e.matmul(ps[:, :], sc[:, j, :], w[:, j, :], start=(j == 0), stop=(j == 7)),
                             reads=[sc, w], writes=[ps])
                    S.op("dve", lambda e, ps=ps, cg=cg: e.tensor_tensor(msb[:, cg * 512:(cg + 1) * 512], ps[:, :], bias[:, cg * 512:(cg + 1) * 512], ALU.add),
                         reads=[ps, bias], writes=[msb])
                S.dma("sp", modrow[l, :, :], msb[:, :], reads=[msb], writes=[modrow])

    def load_mod_fm(self, P, l, which):
        S = self.S
        modrow = self.scr["modrow"]
        off_sh = (0 if which == 0 else 3) * D
        off_sc = off_sh + D
        scp = P.sb([128, 3, 8], F32, "scp")
        shp = P.sb([128, 3, 8], F32, "shp")
        for r in range(3):
            S.dma("pool", scp[:, r, :], modrow[l, r, off_sc:off_sc + D].rearrange("(j p) -> p j", p=128),
                  reads=[modrow], writes=[scp], allow_slow_non_contiguous=True)
            S.dma("pool", shp[:, r, :], modrow[l, r, off_sh:off_sh + D].rearrange("(j p) -> p j", p=128),
                  reads=[modrow], writes=[shp], allow_slow_non_contiguous=True)
        S.op("dve", lambda e: e.tensor_scalar(scp[:, :, :], scp[:, :, :], 1.0, None, ALU.add), reads=[scp], writes=[scp])
        return scp, shp

    def load_gate_bc(self, P, l, which):
        S = self.S
        modrow = self.scr["modrow"]
        off = (2 if which == 0 else 5) * D
        g = P.sb([128, 3, D], F32, "gbc")
        for r in range(3):
            S.dma("pool", g[:, r, :], modrow[l, r:r + 1, off:off + D].partition_broadcast(128), reads=[modrow], writes=[g])
        return g

    def load_weight_bf16(self, P, w_ap, K, N, name, stage, cw=512):
        S = self.S
        kc = K // 128
        wb = P.sb([128, kc, N], BF16, name)
        wv = w_ap.rearrange("(j p) n -> p j n", p=128)
        i = 0
        for c0 in range(0, N, cw):
            c1 = min(N, c0 + cw)
            for j0 in range(0, kc, 8):
                st = stage[i % len(stage)]
                q = ("sp", "act")[i % 2]
                S.dma(q, st[:, 0:8, 0:c1 - c0], wv[:, j0:j0 + 8, c0:c1], writes=[st])
                eng = ("pool", "dve")[i % 2] if (i % 4 != 3) else "act"
                if eng == "act":
                    S.op("act", lambda e, st=st, j0=j0, c0=c0, c1=c1: e.copy(wb[:, j0:j0 + 8, c0:c1], st[:, 0:8, 0:c1 - c0]), reads=[st], writes=[wb])
                else:
                    S.op(eng, lambda e, st=st, j0=j0, c0=c0, c1=c1: e.tensor_copy(wb[:, j0:j0 + 8, c0:c1], st[:, 0:8, 0:c1 - c0]), reads=[st], writes=[wb])
                i += 1
        return wb

    def rsqrt(self, t, sl, mul, add):
        S = self.S
        S.op("dve", lambda e: e.tensor_scalar(t.ap[sl], t.ap[sl], float(mul), float(add), ALU.mult, ALU.add), reads=[t], writes=[t])
        S.op("act", lambda e: e.activation(t.ap[sl], t.ap[sl], AF.Sqrt), reads=[t], writes=[t])
        S.op("dve", lambda e: e.reciprocal(t.ap[sl], t.ap[sl]), reads=[t], writes=[t])

    def norm_to_hT(self, xt, scp, shp, r, hT, col, tmp):
        S = self.S
        junk, ss, rstd, xn, pT = tmp["junk"], tmp["ss"], tmp["rstd"], tmp["xn"], tmp["pT"]
        S.op("dve", lambda e: e.memset(ss[:, :], 0.0), writes=[ss])
        S.op("act", lambda e: e.activation(junk[:, :], xt[:, :], AF.Square, accum_out=ss[:, :]), reads=[xt, ss], writes=[junk, ss])
        S.op("dve", lambda e: e.tensor_copy(rstd[:, :], ss[:, :]), reads=[ss], writes=[rstd])
        self.rsqrt(rstd, (slice(None), slice(None)), 1.0 / D, EPS)
        S.op("act", lambda e: e.activation(xn[:, :], xt[:, :], AF.Copy, scale=rstd[:, :]), reads=[xt, rstd], writes=[xn])
        for j in range(8):
            S.op("pe", lambda e, j=j: e.transpose(pT[:, j * 128:(j + 1) * 128], xn[:, j * 128:(j + 1) * 128], self.ident_b[:, :]),
                 reads=[xn, self.ident_b], writes=[pT])
        for j in range(8):
            if j % 2 == 0:
                S.op("dve", lambda e, j=j: e.tensor_scalar(hT[:, j, col * 128:(col + 1) * 128], pT[:, j * 128:(j + 1) * 128],
                                                            scp[:, r, j:j + 1], shp[:, r, j:j + 1], ALU.mult, ALU.add),
                     reads=[pT, scp, shp], writes=[hT])
            else:
                S.op("act", lambda e, j=j: e.activation(hT[:, j, col * 128:(col + 1) * 128], pT[:, j * 128:(j + 1) * 128], AF.Identity,
                                                         bias=shp[:, r, j:j + 1], scale=scp[:, r, j:j + 1]),
                     reads=[pT, scp, shp], writes=[hT])

    def norm_tmp(self, P):
        return {"junk": P.sb([128, D], BF16, "junk"), "ss": P.sb([128, 1], F32, "ss"), "rstd": P.sb([128, 1], F32, "rstd"),
                "xn": P.sb([128, D], BF16, "xn"), "pT": P.ps([128, D], BF16, "pT")}

    def groups(self, with_ctx=True):
        g = []
        for b in range(NB):
            for k in range(4):
                g.append((b, k * 512, 4, b))
            if with_ctx:
                g.append((b, SEQ, 2, 2))
        return g

    def phase_even_proj(self, l):
        S, I, sc = self.S, self.inp, self.scr
        i = l // 2
        with Phase(S, f"ep{l}") as P:
            self.load_consts(P)
            scp, shp = self.load_mod_fm(P, l, 0)
            stage = [P.sb([128, 8, 512], F32, "stg") for _ in range(2)]
            W = self.load_weight_bf16(P, I["e_w_in"][i], D, E_IN, "Win", stage)
            gb = P.sb([128, 16], F32, "gb")
            S.dma("sp", gb[:, :], I["e_gate_b"][i:i + 1, :].partition_broadcast(128), writes=[gb])
            tmp = self.norm_tmp(P)
            xts = [P.sb([128, D], F32, "xt") for _ in range(2)]
            hTs = [P.sb([128, 8, 512], BF16, "hT") for _ in range(2)]
            ropes = [P.sb([128, 4, 64], F32, "rope") for _ in range(2)]
            psq = [P.ps([128, 512], F32, "psq") for _ in range(4)]
            psg = P.ps([128, 16], F32, "psg")
            pT2 = P.ps([128, 1024], BF16, "pT2")
            qf = P.sb([128, 512], F32, "qf")
            t1 = P.sb([128, 256], F32, "t1"); t2 = P.sb([128, 256], F32, "t2")
            qk_b = [P.sb([128, 1024], BF16, "qkb") for _ in range(2)]
            vo_b = [P.sb([128, 1024], BF16, "vob") for _ in range(2)]
            gsb = [P.sb([128, 16], F32, "gsb") for _ in range(2)]
            qkT = [P.sb([128, 8, 128], BF16, "qkT") for _ in range(2)]
            hyo = [P.sb([128, 512], F32, "hyo") for _ in range(2)]
            n = 0
            nh = 0
            for gi, (b, tok0, nt, r) in enumerate(self.groups()):
                hT = hTs[gi % 2]
                for ti in range(nt):
                    t0 = tok0 + ti * 128
                    xt = xts[n % 2]; rp = ropes[n % 2]; qkb = qk_b[n % 2]; vob = vo_b[n % 2]; gs = gsb[n % 2]; qT_s = qkT[n % 2]
                    n += 1
                    S.dma("sp", xt[:, :], sc["xs"][b, t0:t0 + 128, :], reads=[sc["xs"]], writes=[xt])
                    latent = tok0 < SEQ
                    if latent:
                        S.dma("pool", rp[:, :, :], I["rope"][:, t0:t0 + 128, :].rearrange("k p c -> p k c"), writes=[rp])
                    self.norm_to_hT(xt, scp, shp, r, hT, ti, tmp)
                    for cgp in range(4):
                        ps = psq[cgp]
                        for j in range(8):
                            S.op("pe", lambda e, j=j, ps=ps, cgp=cgp: e.matmul(ps[:, :], hT[:, j, ti * 128:(ti + 1) * 128], W[:, j, cgp * 512:(cgp + 1) * 512],
                                                                               start=(j == 0), stop=(j == 7)), reads=[hT, W], writes=[ps])
                    for j in range(8):
                        S.op("pe", lambda e, j=j: e.matmul(psg[:, :], hT[:, j, ti * 128:(ti + 1) * 128], W[:, j, 2048:2064], start=(j == 0), stop=(j == 7)),
                             reads=[hT, W], writes=[psg])
                    for which in range(2):
                        ps = psq[which]
                        dst = qkb
                        dc0 = which * 512
                        if latent:
                            S.op("act", lambda e, ps=ps: e.copy(qf[:, :], ps[:, :]), reads=[ps], writes=[qf])
                            qv = qf.ap.rearrange("p (h a b c) -> p h a b c", h=4, a=2, b=2)
                            dv = dst.ap[:, dc0:dc0 + 512].rearrange("p (h a b c) -> p h a b c", h=4, a=2, b=2)
                            cos = rp.ap[:, 2 * which + 0, :].rearrange("p (a c) -> p a c", a=2).unsqueeze(1).broadcast_to([128, 4, 2, 32])
                            sin = rp.ap[:, 2 * which + 1, :].rearrange("p (a c) -> p a c", a=2).unsqueeze(1).broadcast_to([128, 4, 2, 32])
                            t1v = t1.ap.rearrange("p (h a c) -> p h a c", h=4, a=2)
                            t2v = t2.ap.rearrange("p (h a c) -> p h a c", h=4, a=2)
                            x1 = qv[:, :, :, 0, :]; x2 = qv[:, :, :, 1, :]
                            S.op("dve", lambda e: e.tensor_tensor(t1v, x1, cos, ALU.mult), reads=[qf, rp], writes=[t1])
                            S.op("pool", lambda e: e.tensor_tensor(t2v, x2, sin, ALU.mult), reads=[qf, rp], writes=[t2])
                            S.op("dve", lambda e: e.tensor_tensor(dv[:, :, :, 0, :], t1v, t2v, ALU.subtract), reads=[t1, t2], writes=[dst])
                            S.op("dve", lambda e: e.tensor_tensor(t1v, x1, sin, ALU.mult), reads=[qf, rp], writes=[t1])
                            S.op("pool", lambda e: e.tensor_tensor(t2v, x2, cos, ALU.mult), reads=[qf, rp], writes=[t2])
                            S.op("dve", lambda e: e.tensor_tensor(dv[:, :, :, 1, :], t1v, t2v, ALU.add), reads=[t1, t2], writes=[dst])
                        else:
                            scale = 1.0 if which == 0 else float(np.float32(128 ** -0.5))
                            S.op("act", lambda e, ps=ps: e.mul(dst[:, dc0:dc0 + 512], ps[:, :], scale), reads=[ps], writes=[dst])
                    for j in range(8):
                        S.op("pe", lambda e, j=j: e.transpose(pT2[:, j * 128:(j + 1) * 128], qkb[:, j * 128:(j + 1) * 128], self.ident_b[:, :]),
                             reads=[qkb, self.ident_b], writes=[pT2])
                    S.op("dve", lambda e: e.tensor_copy(qT_s[:, 0:4, :], pT2[:, 0:512].rearrange("p (j t) -> p j t", j=4)), reads=[pT2], writes=[qT_s])
                    S.op("act", lambda e: e.copy(qT_s[:, 4:8, :], pT2[:, 512:1024].rearrange("p (j t) -> p j t", j=4)), reads=[pT2], writes=[qT_s])
                    S.dma("sp", sc["qT"][b, :, :, t0:t0 + 128].rearrange("h d t -> d h t"), qT_s[:, 0:4, :], reads=[qT_s], writes=[sc["qT"]])
                    S.dma("sp", sc["kT"][b, :, :, t0:t0 + 128].rearrange("h d t -> d h t"), qT_s[:, 4:8, :], reads=[qT_s], writes=[sc["kT"]])
                    S.dma("pool", sc["ktm"][b, t0:t0 + 128, :], qkb[:, 512:1024], reads=[qkb], writes=[sc["ktm"]])
                    S.op("dve", lambda e: e.tensor_copy(vob[:, 0:512], psq[2][:, :]), reads=[psq[2]], writes=[vob])
                    S.op("act", lambda e: e.activation(vob[:, 512:1024], psq[3][:, :], AF.Sigmoid), reads=[psq[3]], writes=[vob])
                    S.op("dve", lambda e: e.tensor_tensor(gs[:, :], psg[:, :], gb[:, :], ALU.add), reads=[psg, gb], writes=[gs])
                    S.dma("pool", sc["vtm"][b, t0:t0 + 128, :], vob[:, 0:512], reads=[vob], writes=[sc["vtm"]])
                    S.dma("pool", sc["osig"][b, t0:t0 + 128, :], vob[:, 512:1024], reads=[vob], writes=[sc["osig"]])
                    S.dma("pool", sc["gates"][b, t0:t0 + 128, :], gs[:, :], reads=[gs], writes=[sc["gates"]])
                ntok = nt * 128
                for cc in range(12):
                    ps = psq[cc % 4]
                    ho = hyo[nh % 2]; nh += 1
                    for j in range(8):
                        S.op("pe", lambda e, j=j, ps=ps, cc=cc: e.matmul(ps[:, 0:ntok], W[:, j, 2064 + cc * 128:2064 + (cc + 1) * 128], hT[:, j, 0:ntok],
                                                                         start=(j == 0), stop=(j == 7)), reads=[hT, W], writes=[ps])
                    if cc % 2 == 0:
                        S.op("act", lambda e, ps=ps, ho=ho: e.copy(ho[:, 0:ntok], ps[:, 0:ntok]), reads=[ps], writes=[ho])
                    else:
                        S.op("dve", lambda e, ps=ps, ho=ho: e.tensor_copy(ho[:, 0:ntok], ps[:, 0:ntok]), reads=[ps], writes=[ho])
                    S.dma("act", sc["hy"][b, cc * 128:(cc + 1) * 128, tok0:tok0 + ntok], ho[:, 0:ntok], reads=[ho], writes=[sc["hy"]])

    def phase_mlstm(self, l):
        S, I, sc = self.S, self.inp, self.scr
        i = l // 2
        with Phase(S, f"ml{l}") as P:
            self.load_consts(P)
            trif = P.sb([128, 3, 128], F32, "trif")
            trib = P.sb([128, 2, 128], BF16, "trib")
            S.dma("sp", trif[:, :, :], I["tri_f"].rearrange("k p c -> p k c"), writes=[trif])
            S.dma("sp", trib[:, :, :], I["tri_b"].rearrange("k p c -> p k c"), writes=[trib])
            hn = P.sb([128, 512], F32, "hn")
            S.dma("pool", hn[:, :], I["e_hnorm"][i:i + 1, :].partition_broadcast(128), writes=[hn])
            qT = P.sb([128, 4, TT], BF16, "qT"); kT = P.sb([128, 4, TT], BF16, "kT")
            ktm = P.sb([128, NTILE, 512], BF16, "ktm")
            v1 = P.sb([128, NTILE, 4, 129], BF16, "v1")
            G = P.sb([128, NTILE, 16], F32, "G")
            hm = P.sb([128, NTILE, 512], F32, "hm")
            lf = P.sb([128, NTILE, 8], F32, "lf")
            cum = P.sb([128, NTILE, 8], F32, "cum")
            es = P.sb([128, NTILE, 8], F32, "es"); eb = P.sb([128, NTILE, 8], F32, "eb"); dec = P.sb([128, NTILE, 8], F32, "dec")
            psG = P.ps([128, 16], F32, "psG")
            NSET = 4
            psS = [P.ps([128, 3, 132], F32, "psS") for _ in range(NSET)]
            psA = [T(p.ap[:, 0, 0:128], "psA") for p in psS]
            psB = [T(p.ap[:, 1, 0:129], "psB") for p in psS]
            psC = [T(p.ap[:, 2, 0:129], "psC") for p in psS]
            pTo = P.ps([128, 512], BF16, "pTo")
            Sf = [P.sb([128, 129], F32, "Sf") for _ in range(8)]
            Sb = [P.sb([128, 129], BF16, "Sb") for _ in range(8)]
            PT = [P.sb([128, 128], BF16, "PT") for _ in range(4)]
            vE = [P.sb([128, 129], BF16, "vE") for _ in range(4)]
            tmpS = [P.sb([128, 129], F32, "tmpS") for _ in range(4)]
            ne = [P.sb([128, 129], F32, "ne") for _ in range(4)]
            dn = [P.sb([128, 1], F32, "dn") for _ in range(4)]
            sq = P.sb([128, 512], F32, "sq"); ss4 = P.sb([128, 4], F32, "ss4"); rs4 = P.sb([128, 4], F32, "rs4")
            osg = [P.sb([128, 512], BF16, "osg") for _ in range(2)]
            ab = [P.sb([128, 512], BF16, "ab") for _ in range(2)]
            aT = [P.sb([128, 4, 128], BF16, "aT") for _ in range(2)]
            for b in range(NB):
                S.dma("sp", qT[:, :, :], sc["qT"][b].rearrange("h d t -> d h t"), reads=[sc["qT"]], writes=[qT])
                S.dma("act", kT[:, :, :], sc["kT"][b].rearrange("h d t -> d h t"), reads=[sc["kT"]], writes=[kT])
                S.dma("sp", ktm[:, :, :], sc["ktm"][b].rearrange("(n p) c -> p n c", p=128), reads=[sc["ktm"]], writes=[ktm])
                S.op("pool", lambda e: e.memset(v1[:, :, :, :], 1.0), writes=[v1])
                for h in range(4):
                    S.dma("pool", v1[:, :, h, 0:128], sc["vtm"][b, :, h * 128:(h + 1) * 128].rearrange("(n p) c -> p n c", p=128), reads=[sc["vtm"]], writes=[v1])
                S.dma("sp", G[:, :, :], sc["gates"][b].rearrange("(n p) c -> p n c", p=128), reads=[sc["gates"]], writes=[G])
                S.op("pool", lambda e: e.memset(hm[:, :, :], 0.0), writes=[hm])
                Gv = G.ap.rearrange("p n (k h) -> p n k h", k=4)
                lfv = lf.ap.rearrange("p n (k h) -> p n k h", k=2)
                S.op("act", lambda e: e.activation(lfv, Gv[:, :, 1::2, :], AF.Exp, scale=-1.0), reads=[G], writes=[lf])
                S.op("act", lambda e: e.activation(lf[:, :, :], lf[:, :, :], AF.Ln, bias=1.0), reads=[lf], writes=[lf])
                S.op("dve", lambda e: e.tensor_scalar(lf[:, :, :], lf[:, :, :], -1.0, None, ALU.mult), reads=[lf], writes=[lf])
                for n in range(NTILE):
                    S.op("pe", lambda e, n=n: e.matmul(psG[:, 0:4], trif[:, 0, :], lf[:, n, 0:4], start=True, stop=True), reads=[trif, lf], writes=[psG])
                    S.op("pe", lambda e, n=n: e.matmul(psG[:, 4:8], trif[:, 1, :], lf[:, n, 4:8], start=True, stop=True), reads=[trif, lf], writes=[psG])
                    S.op("pe", lambda e, n=n: e.matmul(psG[:, 8:16], trif[:, 2, :], lf[:, n, 0:8], start=True, stop=True), reads=[trif, lf], writes=[psG])
                    S.op("dve", lambda e, n=n: e.tensor_copy(cum[:, n, :], psG[:, 0:8]), reads=[psG], writes=[cum])
                    S.op("act", lambda e, n=n: e.activation(dec[:, n, :], psG[:, 8:16], AF.Exp), reads=[psG], writes=[dec])
                esv = es.ap.rearrange("p n (k h) -> p n k h", k=2)
                cumv = cum.ap.rearrange("p n (k h) -> p n k h", k=2)
                S.op("dve", lambda e: e.tensor_tensor(esv, Gv[:, :, 0::2, :], cumv, ALU.subtract), reads=[G, cum], writes=[es])
                S.op("act", lambda e: e.activation(es[:, :, :], es[:, :, :], AF.Exp), reads=[es], writes=[es])
                S.op("act", lambda e: e.activation(eb[:, :, :], cum[:, :, :], AF.Exp), reads=[cum], writes=[eb])
                for c_ in range(8):
                    S.op("pool", lambda e, c_=c_: e.memset(Sf[c_][:, :], 0.0), writes=[Sf[c_]])
                    S.op("pool", lambda e, c_=c_: e.memset(Sb[c_][:, :], 0.0), writes=[Sb[c_]])
                order = {0: [16, 17] + list(range(16)), 1: [17, 16] + list(range(15, -1, -1))}
                k = 0
                for step in range(NTILE):
                    for dr in range(2):
                        n = order[dr][step]
                        tsl = slice(n * 128, (n + 1) * 128)
                        for h in range(4):
                            ci = dr * 4 + h
                            gi = dr * 4 + h
                            a = k % NSET; k += 1
                            pA, pB, pC = psA[a], psB[a], psC[a]
                            pt, ve, tS, nE, dN = PT[a], vE[a], tmpS[a], ne[a], dn[a]
                            sF, sB = Sf[ci], Sb[ci]
                            S.op("pe", lambda e: e.matmul(pA[:, :], kT[:, h, tsl], qT[:, h, tsl], start=True, stop=True), reads=[kT, qT], writes=[pA])
                            S.op("dve", lambda e: e.scalar_tensor_tensor(pt[:, :], pA[:, :], es[:, n, gi:gi + 1], trib[:, dr, :], ALU.mult, ALU.mult),
                                 reads=[pA, es, trib], writes=[pt])
                            S.op("pool", lambda e: e.tensor_scalar(ve[:, :], v1[:, n, h, :], es[:, n, gi:gi + 1], None, ALU.mult), reads=[v1, es], writes=[ve])
                            S.op("pe", lambda e: e.matmul(pB[:, :], pt[:, :], v1[:, n, h, :], start=True, stop=False), reads=[pt, v1], writes=[pB])
                            S.op("pe", lambda e: e.matmul(pB[:, :], qT[:, h, tsl], sB[:, :], start=False, stop=True), reads=[qT, sB], writes=[pB])
                            S.op("pe", lambda e: e.matmul(pC[:, :], ktm[:, n, h * 128:(h + 1) * 128], ve[:, :], start=True, stop=True), reads=[ktm, ve], writes=[pC])
                            S.op("act", lambda e: e.activation(tS[:, :], pC[:, :], AF.Copy, scale=dec[:, n, gi:gi + 1]), reads=[pC, dec], writes=[tS])
                            S.op("dve", lambda e: e.scalar_tensor_tensor(sF[:, :], sF[:, :], dec[:, n, gi:gi + 1], tS[:, :], ALU.mult, ALU.add),
                                 reads=[sF, dec, tS], writes=[sF])
                            S.op("act", lambda e: e.copy(sB[:, :], sF[:, :]), reads=[sF], writes=[sB])
                            S.op("dve", lambda e: e.tensor_scalar(nE[:, :], pB[:, :], eb[:, n, gi:gi + 1], None, ALU.mult), reads=[pB, eb], writes=[nE])
                            S.op("act", lambda e: e.activation(dN[:, :], nE[:, 128:129], AF.Abs), reads=[nE], writes=[dN])
                            S.op("dve", lambda e: e.tensor_scalar(dN[:, :], dN[:, :], 1.0, None, ALU.max), reads=[dN], writes=[dN])
                            S.op("dve", lambda e: e.reciprocal(dN[:, :], dN[:, :]), reads=[dN], writes=[dN])
                            S.op("dve", lambda e: e.scalar_tensor_tensor(hm[:, n, h * 128:(h + 1) * 128], nE[:, 0:128], dN[:, :], hm[:, n, h * 128:(h + 1) * 128],
                                                                         ALU.mult, ALU.add), reads=[nE, dN, hm], writes=[hm])
                for n in range(NTILE):
                    og = osg[n % 2]; abt = ab[n % 2]; at = aT[n % 2]
                    S.dma("sp", og[:, :], sc["osig"][b, n * 128:(n + 1) * 128, :], reads=[sc["osig"]], writes=[og])
                    S.op("pool", lambda e, n=n: e.tensor_tensor(sq[:, :], hm[:, n, :], hm[:, n, :], ALU.mult), reads=[hm], writes=[sq])
                    S.op("dve", lambda e: e.tensor_reduce(ss4[:, :], sq.ap.rearrange("p (h d) -> p h d", h=4), AX.X, ALU.add), reads=[sq], writes=[ss4])
                    S.op("dve", lambda e: e.tensor_copy(rs4[:, :], ss4[:, :]), reads=[ss4], writes=[rs4])
                    self.rsqrt(rs4, (slice(None), slice(None)), 1.0 / 128, EPS)
                    S.op("dve", lambda e, n=n: e.tensor_tensor(sq.ap.rearrange("p (h d) -> p h d", h=4), hm[:, n, :].rearrange("p (h d) -> p h d", h=4),
                                                                rs4.ap.unsqueeze(2).broadcast_to([128, 4, 128]), ALU.mult), reads=[hm, rs4], writes=[sq])
                    S.op("pool", lambda e: e.tensor_tensor(sq[:, :], sq[:, :], hn[:, :], ALU.mult), reads=[sq, hn], writes=[sq])
                    S.op("dve", lambda e, og=og, abt=abt: e.tensor_tensor(abt[:, :], sq[:, :], og[:, :], ALU.mult), reads=[sq, og], writes=[abt])
                    for j in range(4):
                        S.op("pe", lambda e, j=j, abt=abt: e.transpose(pTo[:, j * 128:(j + 1) * 128], abt[:, j * 128:(j + 1) * 128], self.ident_b[:, :]),
                             reads=[abt, self.ident_b], writes=[pTo])
                    S.op("act", lambda e, at=at: e.copy(at[:, :, :], pTo.ap.rearrange("p (j t) -> p j t", j=4)), reads=[pTo], writes=[at])
                    S.dma("pool", sc["mixT"][b, 0:512, n * 128:(n + 1) * 128].rearrange("(j p) t -> p j t", p=128), at[:, :, :], reads=[at], writes=[sc["mixT"]])

    def phase_filters(self, l):
        S, I, sc = self.S, self.inp, self.scr
        i = l // 2
        with Phase(S, f"hf{l}") as P:
            w1 = P.sb([33, 64], F32, "w1"); w2 = P.sb([64, 64], F32, "w2"); w3 = P.sb([64, 2048], F32, "w3")
            S.dma("sp", w1[:, :], I["e_f_w1"][i], writes=[w1])
            S.dma("sp", w2[:, :], I["e_f_w2"][i], writes=[w2])
            S.dma("sp", w3[:, :], I["e_f_w3"][i], writes=[w3])
            fb = P.sb([64, 3], F32, "fb")
            S.dma("pool", fb[:, 0:1], I["e_f_freq"][i].rearrange("(p o) -> p o", o=1), writes=[fb])
            S.dma("pool", fb[:, 1:2], I["e_f_b1"][i].rearrange("(p o) -> p o", o=1), writes=[fb])
            S.dma("pool", fb[:, 2:3], I["e_f_b2"][i].rearrange("(p o) -> p o", o=1), writes=[fb])
            fbb = P.sb([64, 2], F32, "fbb")
            S.op("dve", lambda e: e.tensor_scalar(fbb[:, :], fb[:, 1:3], fb[:, 0:1], None, ALU.mult), reads=[fb], writes=[fbb])
            embT = P.sb([33, SEQ], F32, "embT")
            h1 = P.sb([64, SEQ], F32, "h1"); h2 = P.sb([64, SEQ], F32, "h2")
            pre = P.sb([64, 512], F32, "pre")
            kk = P.sb([64, 512], F32, "kk")
            ps = [P.ps([128, 512], F32, "ps") for _ in range(2)]
            hd = P.sb([128, SEQ], F32, "hd"); dcy = P.sb([128, SEQ], F32, "dcy")
            junk = P.sb([128, SEQ], F32, "junk")
            ssq = P.sb([128, 1], F32, "ssq")
            for nm, L, filt in (("L", SEQ, sc["filtL"]), ("C", CTX, sc["filtC"])):
                cw = min(512, L)
                S.dma("sp", embT[:, 0:L], I["embT_" + nm][:, :], writes=[embT])
                for (src, wgt, dst, K, bi) in ((embT, w1, h1, 33, 0), (h1, w2, h2, 64, 1)):
                    for c0 in range(0, L, cw):
                        p = ps[(c0 // cw) % 2]
                        S.op("pe", lambda e, p=p, src=src, wgt=wgt, c0=c0, K=K: e.matmul(p[0:64, 0:cw], wgt[0:K, :], src[0:K, c0:c0 + cw], start=True, stop=True),
                             reads=[src, wgt], writes=[p])
                        S.op("dve", lambda e, p=p, bi=bi: e.tensor_scalar(pre[:, 0:cw], p[0:64, 0:cw], fb[:, 0:1], fbb[:, bi:bi + 1], ALU.mult, ALU.add),
                             reads=[p, fb, fbb], writes=[pre])
                        MAGIC = 12582912.0
                        S.op("dve", lambda e: e.tensor_scalar(kk[:, 0:cw], pre[:, 0:cw], float(1.0 / (2 * PI)), MAGIC, ALU.mult, ALU.add), reads=[pre], writes=[kk])
                        S.op("dve", lambda e: e.tensor_scalar(kk[:, 0:cw], kk[:, 0:cw], -MAGIC, None, ALU.add), reads=[kk], writes=[kk])
                        S.op("dve", lambda e: e.scalar_tensor_tensor(pre[:, 0:cw], kk[:, 0:cw], float(-2 * PI), pre[:, 0:cw], ALU.mult, ALU.add), reads=[kk, pre], writes=[pre])
                        S.op("dve", lambda e: e.tensor_scalar(kk[:, 0:cw], pre[:, 0:cw], float(PI), None, ALU.is_gt), reads=[pre], writes=[kk])
                        S.op("dve", lambda e: e.scalar_tensor_tensor(pre[:, 0:cw], kk[:, 0:cw], float(-2 * PI), pre[:, 0:cw], ALU.mult, ALU.add), reads=[kk, pre], writes=[pre])
                        S.op("dve", lambda e: e.tensor_scalar(kk[:, 0:cw], pre[:, 0:cw], float(-PI), None, ALU.is_lt), reads=[pre], writes=[kk])
                        S.op("dve", lambda e: e.scalar_tensor_tensor(pre[:, 0:cw], kk[:, 0:cw], float(2 * PI), pre[:, 0:cw], ALU.mult, ALU.add), reads=[kk, pre], writes=[pre])
                        S.op("dve", lambda e: e.tensor_scalar(pre[:, 0:cw], pre[:, 0:cw], 3.141592, -3.141592, ALU.min, ALU.max), reads=[pre], writes=[pre])
                        S.op("act", lambda e, dst=dst, c0=c0: e.activation(dst[:, c0:c0 + cw], pre[:, 0:cw], AF.Sin), reads=[pre], writes=[dst])
                for ch in range(16):
                    o_, d_, cq = ch // 8, (ch // 4) % 2, ch % 4
                    S.dma("sp", dcy[:, 0:L], I["dec_" + nm][cq * 128:(cq + 1) * 128, :], writes=[dcy])
                    for c0 in range(0, L, cw):
                        p = ps[(c0 // cw) % 2]
                        S.op("pe", lambda e, p=p, ch=ch, c0=c0: e.matmul(p[:, 0:cw], w3[:, ch * 128:(ch + 1) * 128], h2[:, c0:c0 + cw], start=True, stop=True),
                             reads=[w3, h2], writes=[p])
                        S.op("dve", lambda e, p=p, c0=c0: e.tensor_tensor(hd[:, c0:c0 + cw], p[:, 0:cw], dcy[:, c0:c0 + cw], ALU.mult), reads=[p, dcy], writes=[hd])
                    S.op("dve", lambda e: e.memset(ssq[:, :], 0.0), writes=[ssq])
                    S.op("act", lambda e: e.activation(junk[:, 0:L], hd[:, 0:L], AF.Square, accum_out=ssq[:, :]), reads=[hd, ssq], writes=[junk, ssq])
                    self.rsqrt(ssq, (slice(None), slice(None)), 1.0, EPS)
                    S.op("act", lambda e: e.activation(junk[:, 0:L], hd[:, 0:L], AF.Copy, scale=ssq[:, :]), reads=[hd, ssq], writes=[junk])
                    S.dma("pool", filt[o_, d_, cq * 128:(cq + 1) * 128, :], junk[:, 0:L], reads=[junk], writes=[filt])

    def phase_hyena(self, l):
        S, I, sc = self.S, self.inp, self.scr
        i = l // 2
        with Phase(S, f"hy{l}") as P:
            cw = P.sb([128, 3, 4, 3], F32, "cw")
            cb = P.sb([128, 3, 4], F32, "cb")
            hd = P.sb([128, 2, 4], F32, "hd")
            for part in range(3):
                for tap in range(3):
                    S.dma("pool", cw[:, part, :, tap], I["e_conv_w"][i, tap, part * 512:(part + 1) * 512].rearrange("(q p) -> p q", p=128),
                          writes=[cw], allow_slow_non_contiguous=True)
                S.dma("pool", cb[:, part, :], I["e_conv_b"][i, part * 512:(part + 1) * 512].rearrange("(q p) -> p q", p=128), writes=[cb], allow_slow_non_contiguous=True)
            for o_ in range(2):
                S.dma("pool", hd[:, o_, :], I["e_hy_d"][i, o_, :].rearrange("(q p) -> p q", p=128), writes=[hd], allow_slow_non_contiguous=True)
            for (L, tok0, filt) in ((SEQ, 0, sc["filtL"]), (CTX, SEQ, sc["filtC"])):
                raw = P.sb([128, NB, L + 2], F32, "raw")
                ys = [P.sb([128, NB, L], F32, "ys") for _ in range(3)]
                accd = P.sb([128, NB, L], F32, "accd"); accp = P.sb([128, NB, L], F32, "accp")
                gf = P.sb([128, L], F32, "gf"); gbk = P.sb([128, L], F32, "gbk")
                ob = P.sb([128, NB, L], BF16, "ob")
                S.op("pool", lambda e: e.memset(raw[:, :, :], 0.0), writes=[raw])
                for cq in range(4):
                    for part in range(3):
                        r0 = part * 512 + cq * 128
                        S.dma("sp", raw[:, :, 1:L + 1], sc["hy"][:, r0:r0 + 128, tok0:tok0 + L].rearrange("b p t -> p b t"), reads=[sc["hy"]], writes=[raw])
                        y = ys[part]
                        S.op("dve", lambda e, y=y, part=part: e.tensor_scalar(y[:, :, :], raw[:, :, 0:L], cw[:, part, cq, 0:1], cb[:, part, cq:cq + 1], ALU.mult, ALU.add),
                             reads=[raw, cw, cb], writes=[y])
                        S.op("dve", lambda e, y=y, part=part: e.scalar_tensor_tensor(y[:, :, :], raw[:, :, 1:L + 1], cw[:, part, cq, 1:2], y[:, :, :], ALU.mult, ALU.add),
                             reads=[raw, cw, y], writes=[y])
                        S.op("dve", lambda e, y=y, part=part: e.scalar_tensor_tensor(y[:, :, :], raw[:, :, 2:L + 2], cw[:, part, cq, 2:3], y[:, :, :], ALU.mult, ALU.add),
                             reads=[raw, cw, y], writes=[y])
                    z = ys[0]
                    for o_ in range(2):
                        S.dma("sp", gf[:, :], filt[o_, 0, cq * 128:(cq + 1) * 128, :], reads=[filt], writes=[gf])
                        S.dma("act", gbk[:, :], filt[o_, 1, cq * 128:(cq + 1) * 128, :], reads=[filt], writes=[gbk])
                        S.op("dve", lambda e: e.tensor_scalar(accd[:, :, :], z[:, :, :], hd[:, o_, cq:cq + 1], None, ALU.mult), reads=[z, hd], writes=[accd])
                        S.op("pool", lambda e: e.memset(accp[:, :, :], 0.0), writes=[accp])
                        taps = [(0, s) for s in range(L)] + [(1, s) for s in range(L)]
                        total = sum(L - s for _, s in taps)
                        pool_budget = total * self.hy_pool_frac
                        acc_cost = 0.0
                        for (dr, s) in taps:
                            g = gf if dr == 0 else gbk
                            use_pool = False
                            if acc_cost < pool_budget and ((s % 3) == 1 or self.hy_pool_frac >= 0.5):
                                use_pool = True
                                acc_cost += (L - s)
                            eng, acc = ("pool", accp) if use_pool else ("dve", accd)
                            if dr == 0:
                                oa = acc[:, :, s:L]; za = z[:, :, 0:L - s]
                            else:
                                oa = acc[:, :, 0:L - s]; za = z[:, :, s:L]
                            S.op(eng, lambda e, oa=oa, za=za, g=g, s=s: e.scalar_tensor_tensor(oa, za, g[:, s:s + 1], oa, ALU.mult, ALU.add),
                                 reads=[z, g, acc], writes=[acc])
                        S.op("dve", lambda e: e.tensor_tensor(accd[:, :, :], accd[:, :, :], accp[:, :, :], ALU.add), reads=[accd, accp], writes=[accd])
                        gate = ys[1 + o_]
                        if o_ == 0:
                            S.op("dve", lambda e: e.tensor_tensor(z[:, :, :], accd[:, :, :], gate[:, :, :], ALU.mult), reads=[accd, gate], writes=[z])
                        else:
                            S.op("dve", lambda e: e.tensor_tensor(ob[:, :, :], accd[:, :, :], gate[:, :, :], ALU.mult), reads=[accd, gate], writes=[ob])
                    S.dma("sp", sc["mixT"][:, 512 + cq * 128:512 + (cq + 1) * 128, tok0:tok0 + L].rearrange("b p t -> p b t"), ob[:, :, :], reads=[ob], writes=[sc["mixT"]])

    def phase_filters_pe(self, l):
        S, I, sc = self.S, self.inp, self.scr
        i = l // 2
        with Phase(S, f"hg{l}") as P:
            w1 = P.sb([33, 64], F32, "w1"); w2 = P.sb([64, 64], F32, "w2"); w3 = P.sb([64, 2048], F32, "w3")
            S.dma("sp", w1[:, :], I["e_f_w1"][i], writes=[w1])
            S.dma("sp", w2[:, :], I["e_f_w2"][i], writes=[w2])
            S.dma("sp", w3[:, :], I["e_f_w3"][i], writes=[w3])
            fb = P.sb([64, 3], F32, "fb")
            S.dma("pool", fb[:, 0:1], I["e_f_freq"][i].rearrange("(p o) -> p o", o=1), writes=[fb])
            S.dma("pool", fb[:, 1:2], I["e_f_b1"][i].rearrange("(p o) -> p o", o=1), writes=[fb])
            S.dma("pool", fb[:, 2:3], I["e_f_b2"][i].rearrange("(p o) -> p o", o=1), writes=[fb])
            fbb = P.sb([64, 2], F32, "fbb")
            S.op("dve", lambda e: e.tensor_scalar(fbb[:, :], fb[:, 1:3], fb[:, 0:1], None, ALU.mult), reads=[fb], writes=[fbb])
            embT = P.sb([33, SEQ], F32, "embT")
            h1 = P.sb([64, SEQ], F32, "h1")
            h2s = [P.sb([64, SEQ], F32, "h2") for _ in range(2)]
            pre = P.sb([64, 512], F32, "pre"); kk = P.sb([64, 512], F32, "kk")
            ps = [P.ps([128, 512], F32, "ps") for _ in range(2)]
            hd = P.sb([128, SEQ], F32, "hd"); dcy = P.sb([128, SEQ], F32, "dcy")
            junk = P.sb([128, SEQ], F32, "junk")
            ssq = P.sb([128, 1], F32, "ssq")
            gt = P.sb([128, 2 * SEQ], F32, "gt")
            gbf = P.sb([128, 2 * SEQ], BF16, "gbf")
            MAGIC = 12582912.0
            for nm, L, gp in (("L", SEQ, sc["gpL"]), ("C", CTX, sc["gpC"])):
                cw = min(512, L)
                for rv in range(2):
                    h2 = h2s[rv]
                    S.dma("sp", embT[:, 0:L], I[("embTr_" if rv else "embT_") + nm][:, :], writes=[embT])
                    for (src_, wgt, dst, K, bi) in ((embT, w1, h1, 33, 0), (h1, w2, h2, 64, 1)):
                        for c0 in range(0, L, cw):
                            p = ps[(c0 // cw) % 2]
                            S.op("pe", lambda e: e.matmul(p[0:64, 0:cw], wgt[0:K, :], src_[0:K, c0:c0 + cw], start=True, stop=True), reads=[src_, wgt], writes=[p])
                            S.op("dve", lambda e: e.tensor_scalar(pre[:, 0:cw], p[0:64, 0:cw], fb[:, 0:1], fbb[:, bi:bi + 1], ALU.mult, ALU.add), reads=[p, fb, fbb], writes=[pre])
                            S.op("dve", lambda e: e.tensor_scalar(kk[:, 0:cw], pre[:, 0:cw], float(1.0 / (2 * PI)), MAGIC, ALU.mult, ALU.add), reads=[pre], writes=[kk])
                            S.op("dve", lambda e: e.tensor_scalar(kk[:, 0:cw], kk[:, 0:cw], -MAGIC, None, ALU.add), reads=[kk], writes=[kk])
                            S.op("dve", lambda e: e.scalar_tensor_tensor(pre[:, 0:cw], kk[:, 0:cw], float(-2 * PI), pre[:, 0:cw], ALU.mult, ALU.add), reads=[kk, pre], writes=[pre])
                            S.op("dve", lambda e: e.tensor_scalar(kk[:, 0:cw], pre[:, 0:cw], float(PI), None, ALU.is_gt), reads=[pre], writes=[kk])
                            S.op("dve", lambda e: e.scalar_tensor_tensor(pre[:, 0:cw], kk[:, 0:cw], float(-2 * PI), pre[:, 0:cw], ALU.mult, ALU.add), reads=[kk, pre], writes=[pre])
                            S.op("dve", lambda e: e.tensor_scalar(kk[:, 0:cw], pre[:, 0:cw], float(-PI), None, ALU.is_lt), reads=[pre], writes=[kk])
                            S.op("dve", lambda e: e.scalar_tensor_tensor(pre[:, 0:cw], kk[:, 0:cw], float(2 * PI), pre[:, 0:cw], ALU.mult, ALU.add), reads=[kk, pre], writes=[pre])
                            S.op("dve", lambda e: e.tensor_scalar(pre[:, 0:cw], pre[:, 0:cw], 3.141592, -3.141592, ALU.min, ALU.max), reads=[pre], writes=[pre])
                            S.op("act", lambda e: e.activation(dst[:, c0:c0 + cw], pre[:, 0:cw], AF.Sin), reads=[pre], writes=[dst])
                for o_ in range(2):
                    for cq in range(4):
                        S.op("pool", lambda e: e.memset(gt[:, 0:1], 0.0), writes=[gt])
                        for d_ in (1, 0):
                            ch = o_ * 8 + d_ * 4 + cq
                            h2 = h2s[d_]
                            S.dma("sp", dcy[:, 0:L], I[("decr_" if d_ else "dec_") + nm][cq * 128:(cq + 1) * 128, :], writes=[dcy])
                            for c0 in range(0, L, cw):
                                p = ps[(c0 // cw) % 2]
                                S.op("pe", lambda e: e.matmul(p[:, 0:cw], w3[:, ch * 128:(ch + 1) * 128], h2[:, c0:c0 + cw], start=True, stop=True), reads=[w3, h2], writes=[p])
                                S.op("dve", lambda e: e.tensor_tensor(hd[:, c0:c0 + cw], p[:, 0:cw], dcy[:, c0:c0 + cw], ALU.mult), reads=[p, dcy], writes=[hd])
                            S.op("dve", lambda e: e.memset(ssq[:, :], 0.0), writes=[ssq])
                            S.op("act", lambda e: e.activation(junk[:, 0:L], hd[:, 0:L], AF.Square, accum_out=ssq[:, :]), reads=[hd, ssq], writes=[junk, ssq])
                            self.rsqrt(ssq, (slice(None), slice(None)), 1.0, EPS)
                            if d_ == 1:
                                S.op("act", lambda e: e.activation(gt[:, 1:L + 1], hd[:, 0:L], AF.Copy, scale=ssq[:, :]), reads=[hd, ssq], writes=[gt])
                            else:
                                S.op("act", lambda e: e.activation(junk[:, 0:L], hd[:, 0:L], AF.Copy, scale=ssq[:, :]), reads=[hd, ssq], writes=[junk])
                                S.op("dve", lambda e: e.tensor_copy(gt[:, L + 1:2 * L], junk[:, 1:L]), reads=[junk], writes=[gt])
                                S.op("dve", lambda e: e.tensor_tensor(gt[:, L:L + 1], gt[:, L:L + 1], junk[:, 0:1], ALU.add), reads=[junk, gt], writes=[gt])
                        S.op("act", lambda e: e.copy(gbf[:, 0:2 * L], gt[:, 0:2 * L]), reads=[gt], writes=[gbf])
                        S.dma("pool", gp[o_, cq * 128:(cq + 1) * 128, :], gbf[:, 0:2 * L], reads=[gbf], writes=[gp])

    def phase_hyena_pe(self, l):
        S, I, sc = self.S, self.inp, self.scr
        nc = self.nc
        i = l // 2
        with Phase(S, f"hp{l}") as P:
            self.load_consts(P)
            ident_f = P.sb([128, 128], F32, "identf")
            S.dma("sp", ident_f[:, :], I["ident_f"][:, :], writes=[ident_f])
            jrev = P.sb([128, 128], BF16, "jrev")
            S.dma("sp", jrev[:, :], I["jrev_b"][:, :], writes=[jrev])
            cw = P.sb([128, 3, 4, 3], F32, "cw")
            cb = P.sb([128, 3, 4], F32, "cb")
            for part in range(3):
                for tap in range(3):
                    S.dma("pool", cw[:, part, :, tap], I["e_conv_w"][i, tap, part * 512:(part + 1) * 512].rearrange("(q p) -> p q", p=128),
                          writes=[cw], allow_slow_non_contiguous=True)
                S.dma("pool", cb[:, part, :], I["e_conv_b"][i, part * 512:(part + 1) * 512].rearrange("(q p) -> p q", p=128), writes=[cb], allow_slow_non_contiguous=True)
            dbc = P.sb([128, 2, 512], F32, "dbc")
            for o_ in range(2):
                S.dma("pool", dbc[:, o_, :], I["e_hy_d"][i, o_:o_ + 1, :].partition_broadcast(128), writes=[dbc])
            psT = [P.ps([128, 512], F32, "psT") for _ in range(2)]
            psB = P.ps([128, 1024], BF16, "psB")
            for (L, tok0, gp, nb) in ((SEQ, 0, sc["gpL"], SEQ // 128), (CTX, SEQ, sc["gpC"], CTX // 128)):
              with Phase(S, f"hp{l}_{L}") as P:
                  GW = 2 * L - 128
                  raw = P.sb([128, NB, L + 2], F32, "raw")
                  ybuf = P.sb([128, NB, L], F32, "ys")
                  tm = [P.sb([128, NB * nb, 128], F32, "tm") for _ in range(3)]
                  zb = P.sb([128, NB * nb * 128], BF16, "zb")
                  KB = self.hy_kb
                  NQ = 128 // KB
                  zq = P.sb([KB, NQ, NB, nb, 128], BF16, "zq")
                  Yt = P.sb([128, NB * nb, 128], F32, "Yt")
                  outb = P.sb([128, NB * nb, 128], BF16, "outb")
                  WW = 2 * L - KB
                  ring = [P.sb([KB, WW], BF16, "G") for _ in range(3)]
                  psY = [P.ps([128, 16, NB, nb], F32, "psY") for _ in range(2)]
                  ob = P.sb([128, NB, L], BF16, "ob")
                  S.op("pool", lambda e: e.memset(raw[:, :, :], 0.0), writes=[raw])
                  gtensor = gp.ap.tensor
                  nG = 0
                  for cq in range(4):
                      for part in range(3):
                          r0 = part * 512 + cq * 128
                          S.dma("sp", raw[:, :, 1:L + 1], sc["hy"][:, r0:r0 + 128, tok0:tok0 + L].rearrange("b p t -> p b t"), reads=[sc["hy"]], writes=[raw])
                          y = ybuf
                          S.op("dve", lambda e: e.tensor_scalar(y[:, :, :], raw[:, :, 0:L], cw[:, part, cq, 0:1], cb[:, part, cq:cq + 1], ALU.mult, ALU.add),
                               reads=[raw, cw, cb], writes=[y])
                          S.op("dve", lambda e: e.scalar_tensor_tensor(y[:, :, :], raw[:, :, 1:L + 1], cw[:, part, cq, 1:2], y[:, :, :], ALU.mult, ALU.add),
                               reads=[raw, cw, y], writes=[y])
                          S.op("dve", lambda e: e.scalar_tensor_tensor(y[:, :, :], raw[:, :, 2:L + 2], cw[:, part, cq, 2:3], y[:, :, :], ALU.mult, ALU.add),
                               reads=[raw, cw, y], writes=[y])
                          k = 0
                          blocks = [(b, j) for b in range(NB) for j in range(nb)]
                          for g0 in range(0, len(blocks), 4):
                              grp = blocks[g0:g0 + 4]
                              pt = psT[(g0 // 4) % 2]
                              for q, (b, j) in enumerate(grp):
                                  S.op("pe", lambda e: e.transpose(pt[:, q * 128:(q + 1) * 128], y[:, b, j * 128:(j + 1) * 128], ident_f[:, :]),
                                       reads=[y, ident_f], writes=[pt])
                              n_ = len(grp)
                              S.op("act" if (g0 // 4) % 2 else "dve",
                                   (lambda e: e.copy(tm[part][:, g0:g0 + n_, :], pt[:, 0:n_ * 128].rearrange("p (q c) -> p q c", q=n_))) if (g0 // 4) % 2 else
                                   (lambda e: e.tensor_copy(tm[part][:, g0:g0 + n_, :], pt[:, 0:n_ * 128].rearrange("p (q c) -> p q c", q=n_))),
                                   reads=[pt], writes=[tm[part]])
                      zt = tm[0]
                      for o_ in range(2):
                          S.op("act", lambda e: e.copy(zb.ap.rearrange("p (n c) -> p n c", c=128), zt[:, :, :]), reads=[zt], writes=[zb])
                          NEL = NB * nb * 128
                          zqv = zq.ap.rearrange("p q b j c -> p q (b j c)")
                          kq = 0
                          for q0 in range(0, NEL, 512):
                              q1 = min(NEL, q0 + 512)
                              for hi in range(NQ):
                                  pt = psT[kq % 2]; kq += 1
                                  S.op("pe", lambda e: e.matmul(pt[0:KB, 0:q1 - q0], jrev[:, KB * hi:KB * (hi + 1)], zb[:, q0:q1], start=True, stop=True),
                                       reads=[jrev, zb], writes=[pt])
                                  if kq % 2:
                                      S.op("dve", lambda e: e.tensor_copy(zqv[:, hi, q0:q1], pt[0:KB, 0:q1 - q0]), reads=[pt], writes=[zq])
                                  else:
                                      S.op("act", lambda e: e.copy(zqv[:, hi, q0:q1], pt[0:KB, 0:q1 - q0]), reads=[pt], writes=[zq])
                          for c0 in range(0, 128, 16):
                              py = psY[(c0 // 16) % 2]
                              for cs in range(16):
                                  c = c0 + cs
                                  G = ring[nG % 3]
                                  q_ = ("sp", "pool")[nG % 2]
                                  nG += 1
                                  src_ap = bass.AP(tensor=gtensor, offset=(o_ * 512 + cq * 128 + c) * (2 * L) + 1, ap=[[1, KB], [1, WW]])
                                  S.dma(q_, G[:, :], src_ap, reads=[gp], writes=[G])
                                  dlist = [0]
                                  for dd in range(1, nb):
                                      dlist += [dd, -dd]
                                  nmm = len(dlist) * NQ
                                  km = 0
                                  for di, d_ in enumerate(dlist):
                                      j0, j1 = max(0, -d_), min(nb, nb - d_)
                                      i0, i1 = j0 + d_, j1 + d_
                                      for hi in range(NQ):
                                          w0 = (d_ + nb - 1) * 128 + KB * hi
                                          S.op("pe", lambda e: e.matmul(py[:, cs, :, i0:i1], G[:, w0:w0 + 128], zq[:, hi, :, j0:j1, c],
                                                                        start=(km == 0), stop=(km == nmm - 1)), reads=[G, zq], writes=[py])
                                          km += 1
                              S.op("dve", lambda e: e.tensor_copy(Yt[:, :, c0:c0 + 16].rearrange("p (b j) c -> p b j c", b=NB), py.ap.rearrange("p c b j -> p b j c")),
                                   reads=[py], writes=[Yt])
                          dsl = dbc[:, o_, cq * 128:(cq + 1) * 128].unsqueeze(1).broadcast_to([128, NB * nb, 128])
                          t1v = ybuf.ap.rearrange("p b (j c) -> p (b j) c", c=128)
                          S.op("pool", lambda e: e.tensor_tensor(t1v, zt[:, :, :], dsl, ALU.mult), reads=[zt, dbc], writes=[ybuf])
                          S.op("dve", lambda e: e.tensor_tensor(Yt[:, :, :], Yt[:, :, :], t1v, ALU.add), reads=[Yt, ybuf], writes=[Yt])
                          gate = tm[1 + o_]
                          if o_ == 0:
                              S.op("dve", lambda e: e.tensor_tensor(zt[:, :, :], Yt[:, :, :], gate[:, :, :], ALU.mult), reads=[Yt, gate], writes=[zt])
                          else:
                              S.op("dve", lambda e: e.tensor_tensor(outb[:, :, :], Yt[:, :, :], gate[:, :, :], ALU.mult), reads=[Yt, gate], writes=[outb])
                      for b in range(NB):
                          for j0 in range(0, nb, 8):
                              n_ = min(8, nb - j0)
                              for q in range(n_):
                                  S.op("pe", lambda e: e.transpose(psB[:, q * 128:(q + 1) * 128], outb[:, b * nb + j0 + q, :], self.ident_b[:, :]),
                                       reads=[outb, self.ident_b], writes=[psB])
                              S.op("act", lambda e: e.copy(ob[:, b, j0 * 128:(j0 + n_) * 128], psB[:, 0:n_ * 128]), reads=[psB], writes=[ob])
                      S.dma("sp", sc["mixT"][:, 512 + cq * 128:512 + (cq + 1) * 128, tok0:tok0 + L].rearrange("b p t -> p b t"), ob[:, :, :], reads=[ob], writes=[sc["mixT"]])

    def phase_outproj(self, l, w_ap, last):
        S, I, sc = self.S, self.inp, self.scr
        with Phase(S, f"op{l}") as P:
            stage = [P.sb([128, 8, 512], F32, "stg") for _ in range(2)]
            W = self.load_weight_bf16(P, w_ap, D, D, "Wo", stage)
            g1 = self.load_gate_bc(P, l, 0)
            mT = [P.sb([128, 8, 512], BF16, "mT") for _ in range(2)]
            xts = [P.sb([128, D], F32, "xt") for _ in range(2)]
            ps = [P.ps([128, 512], F32, "ps") for _ in range(4)]
            tmp = [P.sb([128, 512], F32, "tmp") for _ in range(2)]
            n = 0
            for gi, (b, tok0, nt, r) in enumerate(self.groups(with_ctx=not last)):
                m = mT[gi % 2]
                ntok = nt * 128
                S.dma("sp", m[:, :, 0:ntok], sc["mixT"][b, :, tok0:tok0 + ntok].rearrange("(j p) t -> p j t", p=128), reads=[sc["mixT"]], writes=[m])
                for ti in range(nt):
                    t0 = tok0 + ti * 128
                    xt = xts[n % 2]
                    S.dma("act", xt[:, :], sc["xs"][b, t0:t0 + 128, :], reads=[sc["xs"]], writes=[xt])
                    for half in range(2):
                        p = ps[(2 * n + half) % 4]; tm = tmp[half]
                        for j in range(8):
                            S.op("pe", lambda e, j=j, p=p, half=half: e.matmul(p[:, :], m[:, j, ti * 128:(ti + 1) * 128], W[:, j, half * 512:(half + 1) * 512],
                                                                               start=(j == 0), stop=(j == 7)), reads=[m, W], writes=[p])
                        S.op("dve", lambda e, p=p, tm=tm, half=half: e.tensor_tensor(tm[:, :], p[:, :], g1[:, r, half * 512:(half + 1) * 512], ALU.mult), reads=[p, g1], writes=[tm])
                        S.op("pool", lambda e, tm=tm, half=half, xt=xt: e.tensor_tensor(xt[:, half * 512:(half + 1) * 512], xt[:, half * 512:(half + 1) * 512], tm[:, :], ALU.add),
                             reads=[xt, tm], writes=[xt])
                    S.dma("pool", sc["xs"][b, t0:t0 + 128, :], xt[:, :], reads=[xt], writes=[sc["xs"]])
                    n += 1

    def phase_mlp(self, l, last):
        S, I, sc = self.S, self.inp, self.scr
        GT = 2
        with Phase(S, f"mlp{l}") as P:
            self.load_consts(P)
            scp, shp = self.load_mod_fm(P, l, 1)
            g2 = self.load_gate_bc(P, l, 1)
            stage = [P.sb([128, 8, 256], F32, "stg") for _ in range(2)]
            W1 = self.load_weight_bf16(P, I["w_mlp_in"][l], D, DFF, "W1", stage, cw=256)
            W2 = self.load_weight_bf16(P, I["w_mlp_out"][l], DFF, D, "W2", stage, cw=256)
            tmp = self.norm_tmp(P)
            xts = [P.sb([128, D], F32, "xt") for _ in range(3)]
            hT = P.sb([128, 8, GT * 128], BF16, "hT")
            aT = P.sb([128, 32, GT * 128], BF16, "aT")
            ps = [P.ps([128, 512], F32, "ps") for _ in range(4)]
            tm = [P.sb([128, 512], F32, "tm") for _ in range(2)]
            rl = [P.sb([128, GT * 128], F32, "rl") for _ in range(2)]
            grp = []
            for b in range(NB):
                for k in range(0, SEQ // 128, GT):
                    grp.append((b, k * 128, GT, b))
                if not last:
                    grp.append((b, SEQ, 2, 2))
            n = 0
            for gi, (b, tok0, nt, r) in enumerate(grp):
                ntok = nt * 128
                gx = []
                for ti in range(nt):
                    xt = xts[n % 3]; n += 1
                    gx.append(xt)
                    t0 = tok0 + ti * 128
                    S.dma("sp", xt[:, :], sc["xs"][b, t0:t0 + 128, :], reads=[sc["xs"]], writes=[xt])
                    self.norm_to_hT(xt, scp, shp, r, hT, ti, tmp)
                for fc in range(32):
                    p = ps[fc % 2]; rr = rl[fc % 2]
                    for j in range(8):
                        S.op("pe", lambda e, j=j, p=p, fc=fc: e.matmul(p[:, 0:ntok], W1[:, j, fc * 128:(fc + 1) * 128], hT[:, j, 0:ntok], start=(j == 0), stop=(j == 7)),
                             reads=[W1, hT], writes=[p])
                    S.op("act", lambda e, p=p, rr=rr: e.activation(rr[:, 0:ntok], p[:, 0:ntok], AF.Relu), reads=[p], writes=[rr])
                    eng = "dve" if fc % 2 == 0 else "pool"
                    S.op(eng, lambda e, rr=rr, fc=fc: e.tensor_tensor(aT[:, fc, 0:ntok], rr[:, 0:ntok], rr[:, 0:ntok], ALU.mult), reads=[rr], writes=[aT])
                for ti in range(nt):
                    xt = gx[ti]
                    t0 = tok0 + ti * 128
                    for half in range(2):
                        p = ps[2 + half]; t_ = tm[half]
                        for fc in range(32):
                            S.op("pe", lambda e, fc=fc, p=p, half=half: e.matmul(p[:, :], aT[:, fc, ti * 128:(ti + 1) * 128], W2[:, fc, half * 512:(half + 1) * 512],
                                                                                 start=(fc == 0), stop=(fc == 31)), reads=[aT, W2], writes=[p])
                        S.op("dve", lambda e, p=p, t_=t_, half=half: e.tensor_tensor(t_[:, :], p[:, :], g2[:, r, half * 512:(half + 1) * 512], ALU.mult), reads=[p, g2], writes=[t_])
                        S.op("pool", lambda e, t_=t_, half=half, xt=xt: e.tensor_tensor(xt[:, half * 512:(half + 1) * 512], xt[:, half * 512:(half + 1) * 512], t_[:, :], ALU.add),
                             reads=[xt, t_], writes=[xt])
                    if last:
                        S.dma("pool", self.out[b, t0:t0 + 128, :], xt[:, :], reads=[xt])
                    else:
                        S.dma("pool", sc["xs"][b, t0:t0 + 128, :], xt[:, :], reads=[xt], writes=[sc["xs"]])

    def phase_odd_proj(self, l):
        S, I, sc = self.S, self.inp, self.scr
        i = l // 2
        with Phase(S, f"oq{l}") as P:
            self.load_consts(P)
            scp, shp = self.load_mod_fm(P, l, 0)
            stage = [P.sb([128, 8, 256], F32, "stg") for _ in range(2)]
            W = self.load_weight_bf16(P, I["o_w_qkv"][i], D, 3 * D, "Wqkv", stage, cw=256)
            gn = P.sb([128, 2, 64], F32, "gn")
            S.dma("sp", gn[:, 0, :], I["o_qn"][i:i + 1, :].partition_broadcast(128), writes=[gn])
            S.dma("sp", gn[:, 1, :], I["o_kn"][i:i + 1, :].partition_broadcast(128), writes=[gn])
            S.op("dve", lambda e: e.tensor_scalar(gn[:, 0, :], gn[:, 0, :], 0.125, None, ALU.mult), reads=[gn], writes=[gn])
            tmp = self.norm_tmp(P)
            xts = [P.sb([128, D], F32, "xt") for _ in range(2)]
            hT = P.sb([128, 8, 128], BF16, "hT")
            psq = [P.ps([128, 512], F32, "psq") for _ in range(6)]
            pT2 = P.ps([128, 1024], BF16, "pT2")
            sq = P.sb([128, D], F32, "sq")
            ss16 = P.sb([128, 16], F32, "ss16")
            qn = [P.sb([128, D], BF16, "qn") for _ in range(2)]
            vb = [P.sb([128, D], BF16, "vb") for _ in range(2)]
            qT_s = [P.sb([128, 8, 128], BF16, "qTs") for _ in range(2)]
            n = 0
            for b in range(NB):
                for ti in range(NTILE):
                    t0 = ti * 128
                    r = b if t0 < SEQ else 2
                    xt = xts[n % 2]; vbt = vb[n % 2]; n += 1
                    S.dma("sp", xt[:, :], sc["xs"][b, t0:t0 + 128, :], reads=[sc["xs"]], writes=[xt])
                    self.norm_to_hT(xt, scp, shp, r, hT, 0, tmp)
                    for cg in range(6):
                        ps = psq[cg]
                        for j in range(8):
                            S.op("pe", lambda e, j=j, ps=ps, cg=cg: e.matmul(ps[:, :], hT[:, j, :], W[:, j, cg * 512:(cg + 1) * 512], start=(j == 0), stop=(j == 7)),
                                 reads=[hT, W], writes=[ps])
                    for which in range(2):
                        qnt = qn[which]
                        for half in range(2):
                            ps = psq[which * 2 + half]
                            S.op("act", lambda e, ps=ps, half=half: e.activation(sq[:, half * 512:(half + 1) * 512], ps[:, :], AF.Square), reads=[ps], writes=[sq])
                        S.op("dve", lambda e: e.tensor_reduce(ss16[:, :], sq.ap.rearrange("p (h d) -> p h d", h=16), AX.X, ALU.add), reads=[sq], writes=[ss16])
                        self.rsqrt(ss16, (slice(None), slice(None)), 1.0 / 64, EPS)
                        for half in range(2):
                            ps = psq[which * 2 + half]
                            S.op("dve", lambda e, ps=ps, half=half: e.tensor_tensor(sq[:, half * 512:(half + 1) * 512].rearrange("p (h d) -> p h d", h=8),
                                                                                    ps[:, :].rearrange("p (h d) -> p h d", h=8),
                                                                                    ss16[:, half * 8:(half + 1) * 8].unsqueeze(2).broadcast_to([128, 8, 64]), ALU.mult),
                                 reads=[ps, ss16], writes=[sq])
                        S.op("pool", lambda e, qnt=qnt, which=which: e.tensor_tensor(qnt.ap.rearrange("p (h d) -> p h d", h=16), sq.ap.rearrange("p (h d) -> p h d", h=16),
                                                                                    gn[:, which, :].unsqueeze(1).broadcast_to([128, 16, 64]), ALU.mult),
                             reads=[sq, gn], writes=[qnt])
                        for j in range(8):
                            S.op("pe", lambda e, j=j, qnt=qnt: e.transpose(pT2[:, j * 128:(j + 1) * 128], qnt[:, j * 128:(j + 1) * 128], self.ident_b[:, :]),
                                 reads=[qnt, self.ident_b], writes=[pT2])
                        qs = qT_s[which]
                        S.op("act", lambda e, qs=qs: e.copy(qs[:, :, :], pT2.ap.rearrange("p (j t) -> p j t", j=8)), reads=[pT2], writes=[qs])
                        dst = sc["oqT"] if which == 0 else sc["okT"]
                        S.dma("sp", dst[b, :, :, t0:t0 + 128].rearrange("h d t -> d h t"), qs[:, :, :], reads=[qs], writes=[dst])
                    S.op("dve", lambda e, vbt=vbt: e.tensor_copy(vbt[:, 0:512], psq[4][:, :]), reads=[psq[4]], writes=[vbt])
                    S.op("act", lambda e, vbt=vbt: e.copy(vbt[:, 512:1024], psq[5][:, :]), reads=[psq[5]], writes=[vbt])
                    S.dma("pool", sc["ov"][b, t0:t0 + 128, :, :].rearrange("p h d -> p (h d)"), vbt[:, :], reads=[vbt], writes=[sc["ov"]])

    def phase_natten(self, l, last=False):
        S, I, sc = self.S, self.inp, self.scr
        i = l // 2
        NR = SEQ // GRID_W
        with Phase(S, f"na{l}") as P:
            self.load_consts(P)
            cm = P.sb([64, 64], F32, "cm")
            S.dma("sp", cm[:, :], I["cmaskT"][:, :], writes=[cm])
            rp = P.sb([64, 2, 15, 64], F32, "rp")
            ebm = P.sb([64, 2, 15, 64], BF16, "ebm")
            qT = [P.sb([128, TT], BF16, "qT") for _ in range(2)]
            kT = [P.sb([128, TT], BF16, "kT") for _ in range(2)]
            v64 = [P.sb([64, 36, 2, 65], BF16, "v64") for _ in range(2)]
            psL = [P.ps([64, 8, 64], F32, "psL") for _ in range(2)]
            psX = [P.ps([64, 4, 64], F32, "psX") for _ in range(2)]
            psO = [P.ps([64, 65], F32, "psO") for _ in range(2)]
            pT = P.ps([128, 512], BF16, "pT")
            eL = [P.sb([64, 8, 64], BF16, "eL") for _ in range(2)]
            pL = [P.sb([64, 8, 64], BF16, "pL") for _ in range(2)]
            eX = [P.sb([64, 4, 64], BF16, "eX") for _ in range(2)]
            ao = [P.sb([64, 36, 128], BF16, "ao") for _ in range(2)]
            oT = [P.sb([128, 512], BF16, "oT") for _ in range(2)]
            rcs = [P.sb([64, 1], F32, "rc") for _ in range(2)]
            k = 0
            it = 0
            for hp in range(8):
                S.dma("sp", rp[:, :, :, :], I["rpbx"][i, 2 * hp:2 * hp + 2].rearrange("h k r q -> k h r q"), writes=[rp])
                S.op("act", lambda e: e.activation(rp[:, :, :, :], rp[:, :, :, :], AF.Exp), reads=[rp], writes=[rp])
                S.op("dve", lambda e: e.tensor_tensor(ebm.ap.rearrange("k h r q -> k (h r) q"), rp.ap.rearrange("k h r q -> k (h r) q"),
                                                     cm.ap.unsqueeze(1).broadcast_to([64, 30, 64]), ALU.mult), reads=[rp, cm], writes=[ebm])
                for b in range(NB):
                    q_, k_, v_, a_ = qT[it % 2], kT[it % 2], v64[it % 2], ao[it % 2]
                    it += 1
                    S.dma("sp", q_[:, :], sc["oqT"][b, hp, :, :], reads=[sc["oqT"]], writes=[q_])
                    S.dma("act", k_[:, :], sc["okT"][b, hp, :, :], reads=[sc["okT"]], writes=[k_])
                    S.op("pool", lambda e, v_=v_: e.memset(v_[:, :, :, :], 1.0), writes=[v_])
                    for hl_ in range(2):
                        S.dma("pool", v_[:, :, hl_, 0:64], sc["ov"][b, :, 2 * hp + hl_, :].rearrange("(n p) d -> p n d", p=64), reads=[sc["ov"]], writes=[v_])
                    nq = 32 if last else 36
                    for r in range(nq):
                        lat = r < NR
                        for hl in range(2):
                            a = k % 2; k += 1
                            hs = slice(hl * 64, (hl + 1) * 64)
                            pl_, px_, po_ = psL[a], psX[a], psO[a]
                            el_, pp_, ex_ = eL[a], pL[a], eX[a]
                            rc_ = rcs[a]
                            qs = slice(r * 64, (r + 1) * 64)
                            if lat:
                                rs = min(max(r - 4, 0), NR - 8)
                                for j in range(8):
                                    S.op("pe", lambda e, j=j: e.matmul(pl_[:, j, :], k_[hs, (rs + j) * 64:(rs + j + 1) * 64], q_[hs, qs], start=True, stop=True),
                                         reads=[k_, q_], writes=[pl_])
                            for j in range(4):
                                S.op("pe", lambda e, j=j: e.matmul(px_[:, j, :], k_[hs, SEQ + j * 64:SEQ + (j + 1) * 64], q_[hs, qs], start=True, stop=True),
                                     reads=[k_, q_], writes=[px_])
                            if lat:
                                S.op("act", lambda e: e.activation(el_[:, :, :], pl_[:, :, :], AF.Exp), reads=[pl_], writes=[el_])
                                d0 = rs - r + 7
                                S.op("dve", lambda e: e.tensor_tensor(pp_[:, :, :], el_[:, :, :], ebm[:, hl, d0:d0 + 8, :], ALU.mult), reads=[el_, ebm], writes=[pp_])
                            S.op("act", lambda e: e.activation(ex_[:, :, :], px_[:, :, :], AF.Exp), reads=[px_], writes=[ex_])
                            first = True
                            if lat:
                                for j in range(8):
                                    S.op("pe", lambda e, j=j, first=first: e.matmul(po_[:, :], pp_[:, j, :], v_[:, rs + j, hl, :], start=first, stop=False),
                                         reads=[pp_, v_], writes=[po_])
                                    first = False
                            for j in range(4):
                                S.op("pe", lambda e, j=j, first=first: e.matmul(po_[:, :], ex_[:, j, :], v_[:, 32 + j, hl, :], start=first, stop=(j == 3)),
                                     reads=[ex_, v_], writes=[po_])
                                first = False
                            S.op("dve", lambda e: e.reciprocal(rc_[:, :], po_[:, 64:65]), reads=[po_], writes=[rc_])
                            S.op("dve", lambda e: e.tensor_scalar(a_[:, r, hs], po_[:, 0:64], rc_[:, :], None, ALU.mult), reads=[po_, rc_], writes=[a_])
                    for g0 in range(0, nq, 8):
                        ng = min(8, nq - g0)
                        o_ = oT[(g0 // 8) % 2]
                        for j in range(ng):
                            S.op("pe", lambda e, j=j: e.transpose(pT[:, j * 64:(j + 1) * 64], a_[:, g0 + j, :], self.ident_b[0:64, 0:64]),
                                 reads=[a_, self.ident_b], writes=[pT])
                        S.op("act", lambda e: e.copy(o_[:, 0:ng * 64], pT[:, 0:ng * 64]), reads=[pT], writes=[o_])
                        S.dma("pool", sc["mixT"][b, hp * 128:(hp + 1) * 128, g0 * 64:(g0 + ng) * 64], o_[:, 0:ng * 64], reads=[o_], writes=[sc["mixT"]])

    def build(self):
        self.declare()
        stop = getattr(self, "stop_after", None)
        seq = [("init", self.phase_init), ("mod", self.phase_mod)]
        for l in range(self.depth):
            last = (l == self.depth - 1)
            if l % 2 == 0:
                seq += [(f"eproj{l}", lambda l=l: self.phase_even_proj(l)),
                        (f"filt{l}", lambda l=l: (self.phase_filters_pe(l) if self.hy_pe else self.phase_filters(l))),
                        (f"mlstm{l}", lambda l=l: self.phase_mlstm(l)),
                        (f"hyena{l}", lambda l=l: (self.phase_hyena_pe(l) if self.hy_pe else self.phase_hyena(l))),
                        (f"oproj{l}", lambda l=l, last=last: self.phase_outproj(l, self.inp["e_w_out"][l // 2], last))]
            else:
                seq += [(f"qproj{l}", lambda l=l: self.phase_odd_proj(l)),
                        (f"natten{l}", lambda l=l, last=last: self.phase_natten(l, last)),
                        (f"oproj{l}", lambda l=l, last=last: self.phase_outproj(l, self.inp["o_w_out"][l // 2], last))]
            seq.append((f"mlp{l}", lambda l=l, last=last: self.phase_mlp(l, last)))
        for name, fn in seq:
            if self.only is not None and name not in self.only:
                continue
            fn()
            if stop == name:
                break
        self.S.barrier()
        return self.nc


_CONSTS = None


def make_in_maps(inputs, needed=None):
    global _CONSTS
    if _CONSTS is None:
        _CONSTS = host_consts()
    f = lambda a: np.ascontiguousarray(np.asarray(a, dtype=np.float32))
    shared = {k: f(inputs[k]) for k in ("w_mod", "b_mod", "w_mlp_in", "w_mlp_out", "e_w_in", "e_gate_b", "e_hnorm", "e_conv_w", "e_conv_b",
                                        "e_f_w1", "e_f_b1", "e_f_w2", "e_f_b2", "e_f_w3", "e_f_freq", "e_hy_d", "e_w_out", "o_w_qkv", "o_qn", "o_kn", "o_w_out")}
    shared["rpbx"] = rpb_expand(f(inputs["o_rpb"]))
    shared.update(_CONSTS)
    x = f(inputs["x"]); c = f(inputs["c"]); ctx = f(inputs["ctx"]); cc = f(inputs["c_ctx"])
    maps = []
    for core in range(NCORES):
        m = dict(shared)
        b0 = core * NB
        m["x"] = np.ascontiguousarray(x[b0:b0 + NB])
        m["ctx"] = np.ascontiguousarray(ctx[b0:b0 + NB])
        c3 = np.stack([c[b0], c[b0 + 1], cc], 0)
        m["cT"] = np.ascontiguousarray(c3.reshape(3, 8, 128).transpose(2, 1, 0))
        if needed is not None:
            m = {k: v for k, v in m.items() if k in needed}
        maps.append(m)
    return maps


def kernel(**inputs):
    bld = Builder(depth=DEPTH)
    nc = bld.build()
    maps = make_in_maps(inputs, set(bld.inp.keys()))
    res = run_bass_kernel_spmd(nc, maps, core_ids=list(range(NCORES)))
    out = np.concatenate([np.asarray(r["out"], dtype=np.float32) for r in res.results], axis=0)
    return out
```

```python
import os
import math
import numpy as np
import ml_dtypes
from contextlib import ExitStack
import concourse.bass as bass
import concourse.mybir as mybir
from concourse.bass_utils import run_bass_kernel_spmd

F32 = mybir.dt.float32
BF16 = mybir.dt.bfloat16
ALU = mybir.AluOpType
AF = mybir.ActivationFunctionType
AX = mybir.AxisListType

NCORES = 8
NB = 2
D = 1024
SEQ = 2048
CTX = 256
TT = SEQ + CTX
NTILE = TT // 128
DEPTH = 4
DFF = 4096
EPS = 1e-6
E_IN = 3600
GRID_W = 64
PI = math.pi


class T:
    __slots__ = ("ap", "lw", "rd", "name")

    def __init__(self, ap, name=""):
        self.ap = ap
        self.lw = None
        self.rd = []
        self.name = name

    def __getitem__(self, idx):
        return self.ap[idx]


class Sched:
    NDMA = 10

    def __init__(self, nc):
        self.nc = nc
        self.es = ExitStack()
        self.engs = {"pe": nc.tensor, "act": nc.scalar, "dve": nc.vector, "pool": nc.gpsimd, "sp": nc.sync}
        self.sem = {}
        self.cnt = {}
        for k in self.engs:
            self.sem[k] = self.es.enter_context(nc.semaphore("s_" + k))
            self.cnt[k] = 0
        self.dq = {}
        for q in ("sp", "act", "pool"):
            sems = [self.es.enter_context(nc.semaphore(f"d_{q}{i}")) for i in range(self.NDMA)]
            self.dq[q] = {"sems": sems, "n": 0}
        self.known = {k: {} for k in self.engs}
        self.n_wait = 0
        self.n_inst = 0
        self.pe_pend = None
        self.pe_pend_w = None

    def _flush_pe(self):
        if self.pe_pend is not None:
            self.cnt["pe"] += 1
            self.pe_pend.then_inc(self.sem["pe"], 1)
            self.pe_pend = None
            self.pe_pend_w = None

    def _wait(self, e, tk):
        if tk is None:
            return
        key, sem, val, src = tk
        if src == e and e == "pe":
            return
        if src == "pe" and val > self.cnt["pe"]:
            self._flush_pe()
        kn = self.known[e]
        if kn.get(key, 0) >= val:
            return
        kn[key] = val
        self.engs[e].wait_ge(sem, val)
        self.n_wait += 1

    def _deps(self, e, reads, writes):
        for b in reads:
            self._wait(e, b.lw)
        for b in writes:
            self._wait(e, b.lw)
            for t in b.rd:
                self._wait(e, t)

    def _commit(self, tk, reads, writes):
        for b in reads:
            b.rd.append(tk)
            if len(b.rd) > 48:
                best = {}
                for t in b.rd:
                    if t[0] not in best or best[t[0]][2] < t[2]:
                        best[t[0]] = t
                b.rd = list(best.values())
        for b in writes:
            b.lw = tk
            b.rd = []

    def op(self, e, fn, reads=(), writes=()):
        self._deps(e, reads, writes)
        if e == "pe":
            w0 = (id(writes[0]) if writes else None, tuple(id(r) for r in reads))
            if self.pe_pend is not None and self.pe_pend_w != w0:
                self._flush_pe()
            ins = fn(self.engs[e])
            self.pe_pend = ins
            self.pe_pend_w = w0
            tk = (e, self.sem[e], self.cnt[e] + 1, e)
        else:
            ins = fn(self.engs[e])
            self.cnt[e] += 1
            ins.then_inc(self.sem[e], 1)
            tk = (e, self.sem[e], self.cnt[e], e)
        self._commit(tk, reads, writes)
        self.n_inst += 1
        return tk

    def dma(self, q, out, in_, reads=(), writes=(), **kw):
        d = self.dq[q]
        j = d["n"]
        i = j % self.NDMA
        rnd = j // self.NDMA
        sem = d["sems"][i]
        key = f"d_{q}{i}"
        if rnd > 0:
            self._wait(q, (key, sem, 16 * rnd, None))
        self._deps(q, reads, writes)
        ins = self.engs[q].dma_start(out=out, in_=in_, **kw)
        ins.then_inc(sem, 16)
        d["n"] = j + 1
        tk = (key, sem, 16 * (rnd + 1), None)
        self._commit(tk, reads, writes)
        self.n_inst += 1
        return tk

    def all_tickets(self):
        self._flush_pe()
        tks = []
        for k in self.engs:
            if self.cnt[k] > 0:
                tks.append((k, self.sem[k], self.cnt[k], k))
        for q, d in self.dq.items():
            j = d["n"]
            for i in range(min(j, self.NDMA)):
                last = ((j - 1 - i) // self.NDMA) * self.NDMA + i
                tks.append((f"d_{q}{i}", d["sems"][i], 16 * (last // self.NDMA + 1), None))
        return tks

    def barrier(self, engines=None):
        tks = self.all_tickets()
        for e in (engines or self.engs):
            for tk in tks:
                if tk[3] == e:
                    continue
                self._wait(e, tk)


class Phase:
    def __init__(self, S, name):
        self.S = S
        self.nc = S.nc
        self.name = name
        self.es = ExitStack()
        self.k = 0

    def __enter__(self):
        return self

    def sb(self, shape, dt=F32, name=None):
        self.k += 1
        h = self.es.enter_context(self.nc.sbuf_tensor(f"{self.name}_{name or 't'}{self.k}", list(shape), dt))
        return T(h.ap() if hasattr(h, "ap") and callable(h.ap) else h, name or "")

    def ps(self, shape, dt=F32, name=None):
        self.k += 1
        h = self.es.enter_context(self.nc.psum_tensor(f"{self.name}_{name or 'p'}{self.k}", list(shape), dt))
        return T(h.ap() if hasattr(h, "ap") and callable(h.ap) else h, name or "")

    def __exit__(self, *a):
        self.S.barrier()
        self.es.close()
        return False


def _bf(a):
    return np.asarray(a, np.float32).astype(ml_dtypes.bfloat16)


def host_consts():
    c = {}
    c["ident_b"] = _bf(np.eye(128))
    c["ident_f"] = np.eye(128, dtype=np.float32)
    c["jrev_b"] = _bf(np.eye(128)[::-1])
    s = np.arange(128)
    mf = (s[:, None] <= s[None, :]).astype(np.float32)
    mb = (s[:, None] >= s[None, :]).astype(np.float32)
    c["tri_f"] = np.stack([mf, mb, np.ones((128, 128), np.float32)], 0)
    c["tri_b"] = _bf(np.stack([mf, mb], 0))
    t = np.arange(SEQ)
    rows, cols = t // GRID_W, t % GRID_W
    nf = 32
    inv = (10000.0 ** (-np.arange(nf, dtype=np.float32) / nf)).astype(np.float32)
    ang_r = rows.astype(np.float32)[:, None] * inv[None, :]
    ang_c = cols.astype(np.float32)[:, None] * inv[None, :]
    cosT = np.concatenate([np.cos(ang_r), np.cos(ang_c)], 1).astype(np.float32)
    sinT = np.concatenate([np.sin(ang_r), np.sin(ang_c)], 1).astype(np.float32)
    ks = np.float32(128 ** -0.5)
    c["rope"] = np.stack([cosT, sinT, cosT * ks, sinT * ks], 0).astype(np.float32)
    for nm, L in (("L", SEQ), ("C", CTX)):
        tt = np.linspace(0.0, 1.0, L, dtype=np.float32)[:, None]
        bands = 16
        wpos = (2.0 * np.pi * np.arange(L, dtype=np.float32) / L).astype(np.float32)
        fr = np.linspace(1e-4, bands - 1, bands, dtype=np.float32)
        ang = wpos[:, None] * fr[None, :]
        emb = np.concatenate([tt, np.cos(ang), -np.sin(ang)], -1).astype(np.float32)
        c["embT_" + nm] = np.ascontiguousarray(emb.T)
        deltas = np.abs(np.linspace(math.log(1e-2) / 1.5, math.log(1e-2) / 0.3, 512, dtype=np.float32))
        c["dec_" + nm] = np.ascontiguousarray(np.exp(-tt * deltas[None, :]).T.astype(np.float32))
        c["embTr_" + nm] = np.ascontiguousarray(c["embT_" + nm][:, ::-1])
        c["decr_" + nm] = np.ascontiguousarray(c["dec_" + nm][:, ::-1])
    cidx = np.arange(GRID_W)
    cstart = np.clip(cidx - 8, 0, GRID_W - 16)
    cmask = (cidx[None, :] >= cstart[:, None]) & (cidx[None, :] < cstart[:, None] + 16)
    c["cmaskT"] = np.ascontiguousarray(cmask.T.astype(np.float32))
    return c


def rpb_expand(o_rpb):
    cidx = np.arange(GRID_W)
    dc = np.clip(cidx[:, None] - cidx[None, :] + 15, 0, 30)
    r = o_rpb[:, :, :, dc]
    return np.ascontiguousarray(r.transpose(0, 1, 3, 2, 4))


class Builder:
    def __init__(self, depth=DEPTH, debug=False, hy_pool_frac=0.0, hy_pe=True, hy_kb=64):
        self.depth = depth
        self.debug = debug
        self.hy_pool_frac = hy_pool_frac
        self.hy_pe = hy_pe
        self.hy_kb = hy_kb
        self.nc = bass.Bass("TRN2", target_bir_lowering=False)
        self.S = Sched(self.nc)
        self.inp = {}
        self.scr = {}
        self.feed = set()
        self.only = None

    IN_SHAPES = {
        "x": ([NB, SEQ, D], F32), "ctx": ([NB, CTX, D], F32), "cT": ([128, 8, 3], F32),
        "w_mod": ([DEPTH, D, 6 * D], F32), "b_mod": ([DEPTH, 6 * D], F32),
        "w_mlp_in": ([DEPTH, D, DFF], F32), "w_mlp_out": ([DEPTH, DFF, D], F32),
        "e_w_in": ([2, D, E_IN], F32), "e_gate_b": ([2, 16], F32), "e_hnorm": ([2, 512], F32),
        "e_conv_w": ([2, 3, 1536], F32), "e_conv_b": ([2, 1536], F32),
        "e_f_w1": ([2, 33, 64], F32), "e_f_b1": ([2, 64], F32), "e_f_w2": ([2, 64, 64], F32), "e_f_b2": ([2, 64], F32),
        "e_f_w3": ([2, 64, 2048], F32), "e_f_freq": ([2, 64], F32), "e_hy_d": ([2, 2, 512], F32),
        "e_w_out": ([2, D, D], F32), "o_w_qkv": ([2, D, 3 * D], F32), "o_qn": ([2, 64], F32), "o_kn": ([2, 64], F32),
        "rpbx": ([2, 16, 64, 15, 64], F32), "o_w_out": ([2, D, D], F32),
        "jrev_b": ([128, 128], BF16), "ident_b": ([128, 128], BF16), "ident_f": ([128, 128], F32), "tri_f": ([3, 128, 128], F32),
        "tri_b": ([2, 128, 128], BF16), "rope": ([4, SEQ, 64], F32),
        "embT_L": ([33, SEQ], F32), "embT_C": ([33, CTX], F32), "dec_L": ([512, SEQ], F32), "dec_C": ([512, CTX], F32),
        "cmaskT": ([64, 64], F32),
        "embTr_L": ([33, SEQ], F32), "embTr_C": ([33, CTX], F32), "decr_L": ([512, SEQ], F32), "decr_C": ([512, CTX], F32),
    }
    SCR_SHAPES = {
        "xs": ([NB, TT, D], F32), "modrow": ([DEPTH, 3, 6 * D], F32),
        "qT": ([NB, 4, 128, TT], BF16), "kT": ([NB, 4, 128, TT], BF16),
        "ktm": ([NB, TT, 512], BF16), "vtm": ([NB, TT, 512], BF16), "osig": ([NB, TT, 512], BF16),
        "gates": ([NB, TT, 16], F32), "hy": ([NB, 1536, TT], F32), "mixT": ([NB, D, TT], BF16),
        "filtL": ([2, 2, 512, SEQ], F32), "filtC": ([2, 2, 512, CTX], F32),
        "gpL": ([2, 512, 2 * SEQ], BF16), "gpC": ([2, 512, 2 * CTX], BF16),
        "oqT": ([NB, 8, 128, TT], BF16), "okT": ([NB, 8, 128, TT], BF16), "ov": ([NB, TT, 16, 64], BF16),
    }

    def declare(self):
        bld = self

        class LazyIn(dict):
            def __missing__(s, name):
                shape, dt = bld.IN_SHAPES[name]
                s[name] = bld.nc.dram_tensor(name, list(shape), dt, kind="ExternalInput").ap()
                return s[name]

        class LazyScr(dict):
            def __missing__(s, name):
                shape, dt = bld.SCR_SHAPES[name]
                if name in bld.feed:
                    kind = "ExternalInput"
                else:
                    kind = "ExternalOutput" if bld.debug else "Internal"
                s[name] = T(bld.nc.dram_tensor(name, list(shape), dt, kind=kind).ap(), name)
                return s[name]

        self.inp = LazyIn()
        self.scr = LazyScr()
        self.out = self.nc.dram_tensor("out", [NB, SEQ, D], F32, kind="ExternalOutput").ap()

    def load_consts(self, P):
        S, I = self.S, self.inp
        self.ident_b = P.sb([128, 128], BF16, "identb")
        S.dma("sp", self.ident_b[:, :], I["ident_b"][:, :], writes=[self.ident_b])

    def phase_init(self):
        S, I = self.S, self.inp
        xs = self.scr["xs"]
        for b in range(NB):
            S.dma("sp", xs[b, 0:SEQ, :], I["x"][b, :, :], writes=[xs])
            S.dma("pool", xs[b, SEQ:TT, :], I["ctx"][b, :, :], writes=[xs])
        S.barrier()

    def phase_mod(self):
        S, I = self.S, self.inp
        modrow = self.scr["modrow"]
        with Phase(S, "mod") as P:
            sc = P.sb([128, 8, 3], F32, "sc")
            S.dma("sp", sc[:, :, :], I["cT"][:, :, :], writes=[sc])
            S.op("act", lambda e: e.activation(sc[:, :, :], sc[:, :, :], AF.Silu), reads=[sc], writes=[sc])
            wst = [P.sb([128, 8, 512], F32, "wst") for _ in range(2)]
            pss = [P.ps([3, 512], F32, "ps") for _ in range(2)]
            bias = P.sb([3, 6 * D], F32, "bias")
            msb = P.sb([3, 6 * D], F32, "msb")
            k = 0
            for l in range(self.depth):
                S.dma("pool", bias[:, :], I["b_mod"][l:l + 1, :].partition_broadcast(3), writes=[bias])
                wv = I["w_mod"][l].rearrange("(j p) n -> p j n", p=128)
                for cg in range(12):
                    w = wst[k % 2]; ps = pss[k % 2]; k += 1
                    S.dma("sp" if cg % 2 == 0 else "act", w[:, :, :], wv[:, :, cg * 512:(cg + 1) * 512], writes=[w])
                    for j in range(8):
                        S.op("pe", lambda e, j=j, w=w, ps=ps: e.matmul(ps[:, :], sc[:, j, :], w[:, j, :], start=(j == 0), stop=(j == 7)),
                             reads=[sc, w], writes=[ps])
                    S.op("dve", lambda e, ps=ps, cg=cg: e.tensor_tensor(msb[:, cg * 512:(cg + 1) * 512], ps[:, :], bias[:, cg * 512:(cg + 1) * 512], ALU.add),
                         reads=[ps, bias], writes=[msb])
                S.dma("sp", modrow[l, :, :], msb[:, :], reads=[msb], writes=[modrow])

    def load_mod_fm(self, P, l, which):
        S = self.S
        modrow = self.scr["modrow"]
        off_sh = (0 if which == 0 else 3) * D
        off_sc = off_sh + D
        scp = P.sb([128, 3, 8], F32, "scp")
        shp = P.sb([128, 3, 8], F32, "shp")
        for r in range(3):
            S.dma("pool", scp[:, r, :], modrow[l, r, off_sc:off_sc + D].rearrange("(j p) -> p j", p=128),
                  reads=[modrow], writes=[scp], allow_slow_non_contiguous=True)
            S.dma("pool", shp[:, r, :], modrow[l, r, off_sh:off_sh + D].rearrange("(j p) -> p j", p=128),
                  reads=[modrow], writes=[shp], allow_slow_non_contiguous=True)
        S.op("dve", lambda e: e.tensor_scalar(scp[:, :, :], scp[:, :, :], 1.0, None, ALU.add), reads=[scp], writes=[scp])
        return scp, shp

    def load_gate_bc(self, P, l, which):
        S = self.S
        modrow = self.scr["modrow"]
        off = (2 if which == 0 else 5) * D
        g = P.sb([128, 3, D], F32, "gbc")
        for r in range(3):
            S.dma("pool", g[:, r, :], modrow[l, r:r + 1, off:off + D].partition_broadcast(128), reads=[modrow], writes=[g])
        return g

    def load_weight_bf16(self, P, w_ap, K, N, name, stage, cw=512):
        S = self.S
        kc = K // 128
        wb = P.sb([128, kc, N], BF16, name)
        wv = w_ap.rearrange("(j p) n -> p j n", p=128)
        i = 0
        for c0 in range(0, N, cw):
            c1 = min(N, c0 + cw)
            for j0 in range(0, kc, 8):
                st = stage[i % len(stage)]
                q = ("sp", "act")[i % 2]
                S.dma(q, st[:, 0:8, 0:c1 - c0], wv[:, j0:j0 + 8, c0:c1], writes=[st])
                eng = ("pool", "dve")[i % 2] if (i % 4 != 3) else "act"
                if eng == "act":
                    S.op("act", lambda e, st=st, j0=j0, c0=c0, c1=c1: e.copy(wb[:, j0:j0 + 8, c0:c1], st[:, 0:8, 0:c1 - c0]), reads=[st], writes=[wb])
                else:
                    S.op(eng, lambda e, st=st, j0=j0, c0=c0, c1=c1: e.tensor_copy(wb[:, j0:j0 + 8, c0:c1], st[:, 0:8, 0:c1 - c0]), reads=[st], writes=[wb])
                i += 1
        return wb

    def rsqrt(self, t, sl, mul, add):
        S = self.S
        S.op("dve", lambda e: e.tensor_scalar(t.ap[sl], t.ap[sl], float(mul), float(add), ALU.mult, ALU.add), reads=[t], writes=[t])
        S.op("act", lambda e: e.activation(t.ap[sl], t.ap[sl], AF.Sqrt), reads=[t], writes=[t])
        S.op("dve", lambda e: e.reciprocal(t.ap[sl], t.ap[sl]), reads=[t], writes=[t])

    def norm_to_hT(self, xt, scp, shp, r, hT, col, tmp):
        S = self.S
        junk, ss, rstd, xn, pT = tmp["junk"], tmp["ss"], tmp["rstd"], tmp["xn"], tmp["pT"]
        S.op("dve", lambda e: e.memset(ss[:, :], 0.0), writes=[ss])
        S.op("act", lambda e: e.activation(junk[:, :], xt[:, :], AF.Square, accum_out=ss[:, :]), reads=[xt, ss], writes=[junk, ss])
        S.op("dve", lambda e: e.tensor_copy(rstd[:, :], ss[:, :]), reads=[ss], writes=[rstd])
        self.rsqrt(rstd, (slice(None), slice(None)), 1.0 / D, EPS)
        S.op("act", lambda e: e.activation(xn[:, :], xt[:, :], AF.Copy, scale=rstd[:, :]), reads=[xt, rstd], writes=[xn])
        for j in range(8):
            S.op("pe", lambda e, j=j: e.transpose(pT[:, j * 128:(j + 1) * 128], xn[:, j * 128:(j + 1) * 128], self.ident_b[:, :]),
                 reads=[xn, self.ident_b], writes=[pT])
        for j in range(8):
            if j % 2 == 0:
                S.op("dve", lambda e, j=j: e.tensor_scalar(hT[:, j, col * 128:(col + 1) * 128], pT[:, j * 128:(j + 1) * 128],
                                                            scp[:, r, j:j + 1], shp[:, r, j:j + 1], ALU.mult, ALU.add),
                     reads=[pT, scp, shp], writes=[hT])
            else:
                S.op("act", lambda e, j=j: e.activation(hT[:, j, col * 128:(col + 1) * 128], pT[:, j * 128:(j + 1) * 128], AF.Identity,
                                                         bias=shp[:, r, j:j + 1], scale=scp[:, r, j:j + 1]),
                     reads=[pT, scp, shp], writes=[hT])

    def norm_tmp(self, P):
        return {"junk": P.sb([128, D], BF16, "junk"), "ss": P.sb([128, 1], F32, "ss"), "rstd": P.sb([128, 1], F32, "rstd"),
                "xn": P.sb([128, D], BF16, "xn"), "pT": P.ps([128, D], BF16, "pT")}

    def groups(self, with_ctx=True):
        g = []
        for b in range(NB):
            for k in range(4):
                g.append((b, k * 512, 4, b))
            if with_ctx:
                g.append((b, SEQ, 2, 2))
        return g

    def phase_even_proj(self, l):
        S, I, sc = self.S, self.inp, self.scr
        i = l // 2
        with Phase(S, f"ep{l}") as P:
            self.load_consts(P)
            scp, shp = self.load_mod_fm(P, l, 0)
            stage = [P.sb([128, 8, 512], F32, "stg") for _ in range(2)]
            W = self.load_weight_bf16(P, I["e_w_in"][i], D, E_IN, "Win", stage)
            gb = P.sb([128, 16], F32, "gb")
            S.dma("sp", gb[:, :], I["e_gate_b"][i:i + 1, :].partition_broadcast(128), writes=[gb])
            tmp = self.norm_tmp(P)
            xts = [P.sb([128, D], F32, "xt") for _ in range(2)]
            hTs = [P.sb([128, 8, 512], BF16, "hT") for _ in range(2)]
            ropes = [P.sb([128, 4, 64], F32, "rope") for _ in range(2)]
            psq = [P.ps([128, 512], F32, "psq") for _ in range(4)]
            psg = P.ps([128, 16], F32, "psg")
            pT2 = P.ps([128, 1024], BF16, "pT2")
            qf = P.sb([128, 512], F32, "qf")
            t1 = P.sb([128, 256], F32, "t1"); t2 = P.sb([128, 256], F32, "t2")
            qk_b = [P.sb([128, 1024], BF16, "qkb") for _ in range(2)]
            vo_b = [P.sb([128, 1024], BF16, "vob") for _ in range(2)]
            gsb = [P.sb([128, 16], F32, "gsb") for _ in range(2)]
            qkT = [P.sb([128, 8, 128], BF16, "qkT") for _ in range(2)]
            hyo = [P.sb([128, 512], F32, "hyo") for _ in range(2)]
            n = 0
            nh = 0
            for gi, (b, tok0, nt, r) in enumerate(self.groups()):
                hT = hTs[gi % 2]
                for ti in range(nt):
                    t0 = tok0 + ti * 128
                    xt = xts[n % 2]; rp = ropes[n % 2]; qkb = qk_b[n % 2]; vob = vo_b[n % 2]; gs = gsb[n % 2]; qT_s = qkT[n % 2]
                    n += 1
                    S.dma("sp", xt[:, :], sc["xs"][b, t0:t0 + 128, :], reads=[sc["xs"]], writes=[xt])
                    latent = tok0 < SEQ
                    if latent:
                        S.dma("pool", rp[:, :, :], I["rope"][:, t0:t0 + 128, :].rearrange("k p c -> p k c"), writes=[rp])
                    self.norm_to_hT(xt, scp, shp, r, hT, ti, tmp)
                    for cgp in range(4):
                        ps = psq[cgp]
                        for j in range(8):
                            S.op("pe", lambda e, j=j, ps=ps, cgp=cgp: e.matmul(ps[:, :], hT[:, j, ti * 128:(ti + 1) * 128], W[:, j, cgp * 512:(cgp + 1) * 512],
                                                                               start=(j == 0), stop=(j == 7)), reads=[hT, W], writes=[ps])
                    for j in range(8):
                        S.op("pe", lambda e, j=j: e.matmul(psg[:, :], hT[:, j, ti * 128:(ti + 1) * 128], W[:, j, 2048:2064], start=(j == 0), stop=(j == 7)),
                             reads=[hT, W], writes=[psg])
                    for which in range(2):
                        ps = psq[which]
                        dst = qkb
                        dc0 = which * 512
                        if latent:
                            S.op("act", lambda e, ps=ps: e.copy(qf[:, :], ps[:, :]), reads=[ps], writes=[qf])
                            qv = qf.ap.rearrange("p (h a b c) -> p h a b c", h=4, a=2, b=2)
                            dv = dst.ap[:, dc0:dc0 + 512].rearrange("p (h a b c) -> p h a b c", h=4, a=2, b=2)
                            cos = rp.ap[:, 2 * which + 0, :].rearrange("p (a c) -> p a c", a=2).unsqueeze(1).broadcast_to([128, 4, 2, 32])
                            sin = rp.ap[:, 2 * which + 1, :].rearrange("p (a c) -> p a c", a=2).unsqueeze(1).broadcast_to([128, 4, 2, 32])
                            t1v = t1.ap.rearrange("p (h a c) -> p h a c", h=4, a=2)
                            t2v = t2.ap.rearrange("p (h a c) -> p h a c", h=4, a=2)
                            x1 = qv[:, :, :, 0, :]; x2 = qv[:, :, :, 1, :]
                            S.op("dve", lambda e: e.tensor_tensor(t1v, x1, cos, ALU.mult), reads=[qf, rp], writes=[t1])
                            S.op("pool", lambda e: e.tensor_tensor(t2v, x2, sin, ALU.mult), reads=[qf, rp], writes=[t2])
                            S.op("dve", lambda e: e.tensor_tensor(dv[:, :, :, 0, :], t1v, t2v, ALU.subtract), reads=[t1, t2], writes=[dst])
                            S.op("dve", lambda e: e.tensor_tensor(t1v, x1, sin, ALU.mult), reads=[qf, rp], writes=[t1])
                            S.op("pool", lambda e: e.tensor_tensor(t2v, x2, cos, ALU.mult), reads=[qf, rp], writes=[t2])
                            S.op("dve", lambda e: e.tensor_tensor(dv[:, :, :, 1, :], t1v, t2v, ALU.add), reads=[t1, t2], writes=[dst])
                        else:
                            scale = 1.0 if which == 0 else float(np.float32(128 ** -0.5))
                            S.op("act", lambda e, ps=ps: e.mul(dst[:, dc0:dc0 + 512], ps[:, :], scale), reads=[ps], writes=[dst])
                    for j in range(8):
                        S.op("pe", lambda e, j=j: e.transpose(pT2[:, j * 128:(j + 1) * 128], qkb[:, j * 128:(j + 1) * 128], self.ident_b[:, :]),
                             reads=[qkb, self.ident_b], writes=[pT2])
                    S.op("dve", lambda e: e.tensor_copy(qT_s[:, 0:4, :], pT2[:, 0:512].rearrange("p (j t) -> p j t", j=4)), reads=[pT2], writes=[qT_s])
                    S.op("act", lambda e: e.copy(qT_s[:, 4:8, :], pT2[:, 512:1024].rearrange("p (j t) -> p j t", j=4)), reads=[pT2], writes=[qT_s])
                    S.dma("sp", sc["qT"][b, :, :, t0:t0 + 128].rearrange("h d t -> d h t"), qT_s[:, 0:4, :], reads=[qT_s], writes=[sc["qT"]])
                    S.dma("sp", sc["kT"][b, :, :, t0:t0 + 128].rearrange("h d t -> d h t"), qT_s[:, 4:8, :], reads=[qT_s], writes=[sc["kT"]])
                    S.dma("pool", sc["ktm"][b, t0:t0 + 128, :], qkb[:, 512:1024], reads=[qkb], writes=[sc["ktm"]])
                    S.op("dve", lambda e: e.tensor_copy(vob[:, 0:512], psq[2][:, :]), reads=[psq[2]], writes=[vob])
                    S.op("act", lambda e: e.activation(vob[:, 512:1024], psq[3][:, :], AF.Sigmoid), reads=[psq[3]], writes=[vob])
                    S.op("dve", lambda e: e.tensor_tensor(gs[:, :], psg[:, :], gb[:, :], ALU.add), reads=[psg, gb], writes=[gs])
                    S.dma("pool", sc["vtm"][b, t0:t0 + 128, :], vob[:, 0:512], reads=[vob], writes=[sc["vtm"]])
                    S.dma("pool", sc["osig"][b, t0:t0 + 128, :], vob[:, 512:1024], reads=[vob], writes=[sc["osig"]])
                    S.dma("pool", sc["gates"][b, t0:t0 + 128, :], gs[:, :], reads=[gs], writes=[sc["gates"]])
                ntok = nt * 128
                for cc in range(12):
                    ps = psq[cc % 4]
                    ho = hyo[nh % 2]; nh += 1
                    for j in range(8):
                        S.op("pe", lambda e, j=j, ps=ps, cc=cc: e.matmul(ps[:, 0:ntok], W[:, j, 2064 + cc * 128:2064 + (cc + 1) * 128], hT[:, j, 0:ntok],
                                                                         start=(j == 0), stop=(j == 7)), reads=[hT, W], writes=[ps])
                    if cc % 2 == 0:
                        S.op("act", lambda e, ps=ps, ho=ho: e.copy(ho[:, 0:ntok], ps[:, 0:ntok]), reads=[ps], writes=[ho])
                    else:
                        S.op("dve", lambda e, ps=ps, ho=ho: e.tensor_copy(ho[:, 0:ntok], ps[:, 0:ntok]), reads=[ps], writes=[ho])
                    S.dma("act", sc["hy"][b, cc * 128:(cc + 1) * 128, tok0:tok0 + ntok], ho[:, 0:ntok], reads=[ho], writes=[sc["hy"]])

    def phase_mlstm(self, l):
        S, I, sc = self.S, self.inp, self.scr
        i = l // 2
        with Phase(S, f"ml{l}") as P:
            self.load_consts(P)
            trif = P.sb([128, 3, 128], F32, "trif")
            trib = P.sb([128, 2, 128], BF16, "trib")
            S.dma("sp", trif[:, :, :], I["tri_f"].rearrange("k p c -> p k c"), writes=[trif])
            S.dma("sp", trib[:, :, :], I["tri_b"].rearrange("k p c -> p k c"), writes=[trib])
            hn = P.sb([128, 512], F32, "hn")
            S.dma("pool", hn[:, :], I["e_hnorm"][i:i + 1, :].partition_broadcast(128), writes=[hn])
            qT = P.sb([128, 4, TT], BF16, "qT"); kT = P.sb([128, 4, TT], BF16, "kT")
            ktm = P.sb([128, NTILE, 512], BF16, "ktm")
            v1 = P.sb([128, NTILE, 4, 129], BF16, "v1")
            G = P.sb([128, NTILE, 16], F32, "G")
            hm = P.sb([128, NTILE, 512], F32, "hm")
            lf = P.sb([128, NTILE, 8], F32, "lf")
            cum = P.sb([128, NTILE, 8], F32, "cum")
            es = P.sb([128, NTILE, 8], F32, "es"); eb = P.sb([128, NTILE, 8], F32, "eb"); dec = P.sb([128, NTILE, 8], F32, "dec")
            psG = P.ps([128, 16], F32, "psG")
            NSET = 4
            psS = [P.ps([128, 3, 132], F32, "psS") for _ in range(NSET)]
            psA = [T(p.ap[:, 0, 0:128], "psA") for p in psS]
            psB = [T(p.ap[:, 1, 0:129], "psB") for p in psS]
            psC = [T(p.ap[:, 2, 0:129], "psC") for p in psS]
            pTo = P.ps([128, 512], BF16, "pTo")
            Sf = [P.sb([128, 129], F32, "Sf") for _ in range(8)]
            Sb = [P.sb([128, 129], BF16, "Sb") for _ in range(8)]
            PT = [P.sb([128, 128], BF16, "PT") for _ in range(4)]
            vE = [P.sb([128, 129], BF16, "vE") for _ in range(4)]
            tmpS = [P.sb([128, 129], F32, "tmpS") for _ in range(4)]
            ne = [P.sb([128, 129], F32, "ne") for _ in range(4)]
            dn = [P.sb([128, 1], F32, "dn") for _ in range(4)]
            sq = P.sb([128, 512], F32, "sq"); ss4 = P.sb([128, 4], F32, "ss4"); rs4 = P.sb([128, 4], F32, "rs4")
            osg = [P.sb([128, 512], BF16, "osg") for _ in range(2)]
            ab = [P.sb([128, 512], BF16, "ab") for _ in range(2)]
            aT = [P.sb([128, 4, 128], BF16, "aT") for _ in range(2)]
            for b in range(NB):
                S.dma("sp", qT[:, :, :], sc["qT"][b].rearrange("h d t -> d h t"), reads=[sc["qT"]], writes=[qT])
                S.dma("act", kT[:, :, :], sc["kT"][b].rearrange("h d t -> d h t"), reads=[sc["kT"]], writes=[kT])
                S.dma("sp", ktm[:, :, :], sc["ktm"][b].rearrange("(n p) c -> p n c", p=128), reads=[sc["ktm"]], writes=[ktm])
                S.op("pool", lambda e: e.memset(v1[:, :, :, :], 1.0), writes=[v1])
                for h in range(4):
                    S.dma("pool", v1[:, :, h, 0:128], sc["vtm"][b, :, h * 128:(h + 1) * 128].rearrange("(n p) c -> p n c", p=128), reads=[sc["vtm"]], writes=[v1])
                S.dma("sp", G[:, :, :], sc["gates"][b].rearrange("(n p) c -> p n c", p=128), reads=[sc["gates"]], writes=[G])
                S.op("pool", lambda e: e.memset(hm[:, :, :], 0.0), writes=[hm])
                Gv = G.ap.rearrange("p n (k h) -> p n k h", k=4)
                lfv = lf.ap.rearrange("p n (k h) -> p n k h", k=2)
                S.op("act", lambda e: e.activation(lfv, Gv[:, :, 1::2, :], AF.Exp, scale=-1.0), reads=[G], writes=[lf])
                S.op("act", lambda e: e.activation(lf[:, :, :], lf[:, :, :], AF.Ln, bias=1.0), reads=[lf], writes=[lf])
                S.op("dve", lambda e: e.tensor_scalar(lf[:, :, :], lf[:, :, :], -1.0, None, ALU.mult), reads=[lf], writes=[lf])
                for n in range(NTILE):
                    S.op("pe", lambda e, n=n: e.matmul(psG[:, 0:4], trif[:, 0, :], lf[:, n, 0:4], start=True, stop=True), reads=[trif, lf], writes=[psG])
                    S.op("pe", lambda e, n=n: e.matmul(psG[:, 4:8], trif[:, 1, :], lf[:, n, 4:8], start=True, stop=True), reads=[trif, lf], writes=[psG])
                    S.op("pe", lambda e, n=n: e.matmul(psG[:, 8:16], trif[:, 2, :], lf[:, n, 0:8], start=True, stop=True), reads=[trif, lf], writes=[psG])
                    S.op("dve", lambda e, n=n: e.tensor_copy(cum[:, n, :], psG[:, 0:8]), reads=[psG], writes=[cum])
                    S.op("act", lambda e, n=n: e.activation(dec[:, n, :], psG[:, 8:16], AF.Exp), reads=[psG], writes=[dec])
                esv = es.ap.rearrange("p n (k h) -> p n k h", k=2)
                cumv = cum.ap.rearrange("p n (k h) -> p n k h", k=2)
                S.op("dve", lambda e: e.tensor_tensor(esv, Gv[:, :, 0::2, :], cumv, ALU.subtract), reads=[G, cum], writes=[es])
                S.op("act", lambda e: e.activation(es[:, :, :], es[:, :, :], AF.Exp), reads=[es], writes=[es])
                S.op("act", lambda e: e.activation(eb[:, :, :], cum[:, :, :], AF.Exp), reads=[cum], writes=[eb])
                for c_ in range(8):
                    S.op("pool", lambda e, c_=c_: e.memset(Sf[c_][:, :], 0.0), writes=[Sf[c_]])
                    S.op("pool", lambda e, c_=c_: e.memset(Sb[c_][:, :], 0.0), writes=[Sb[c_]])
                order = {0: [16, 17] + list(range(16)), 1: [17, 16] + list(range(15, -1, -1))}
                k = 0
                for step in range(NTILE):
                    for dr in range(2):
                        n = order[dr][step]
                        tsl = slice(n * 128, (n + 1) * 128)
                        for h in range(4):
                            ci = dr * 4 + h
                            gi = dr * 4 + h
                            a = k % NSET; k += 1
                            pA, pB, pC = psA[a], psB[a], psC[a]
                            pt, ve, tS, nE, dN = PT[a], vE[a], tmpS[a], ne[a], dn[a]
                            sF, sB = Sf[ci], Sb[ci]
                            S.op("pe", lambda e: e.matmul(pA[:, :], kT[:, h, tsl], qT[:, h, tsl], start=True, stop=True), reads=[kT, qT], writes=[pA])
                            S.op("dve", lambda e: e.scalar_tensor_tensor(pt[:, :], pA[:, :], es[:, n, gi:gi + 1], trib[:, dr, :], ALU.mult, ALU.mult),
                                 reads=[pA, es, trib], writes=[pt])
                            S.op("pool", lambda e: e.tensor_scalar(ve[:, :], v1[:, n, h, :], es[:, n, gi:gi + 1], None, ALU.mult), reads=[v1, es], writes=[ve])
                            S.op("pe", lambda e: e.matmul(pB[:, :], pt[:, :], v1[:, n, h, :], start=True, stop=False), reads=[pt, v1], writes=[pB])
                            S.op("pe", lambda e: e.matmul(pB[:, :], qT[:, h, tsl], sB[:, :], start=False, stop=True), reads=[qT, sB], writes=[pB])
                            S.op("pe", lambda e: e.matmul(pC[:, :], ktm[:, n, h * 128:(h + 1) * 128], ve[:, :], start=True, stop=True), reads=[ktm, ve], writes=[pC])
                            S.op("act", lambda e: e.activation(tS[:, :], pC[:, :], AF.Copy, scale=dec[:, n, gi:gi + 1]), reads=[pC, dec], writes=[tS])
                            S.op("dve", lambda e: e.scalar_tensor_tensor(sF[:, :], sF[:, :], dec[:, n, gi:gi + 1], tS[:, :], ALU.mult, ALU.add),
                                 reads=[sF, dec, tS], writes=[sF])
                            S.op("act", lambda e: e.copy(sB[:, :], sF[:, :]), reads=[sF], writes=[sB])
                            S.op("dve", lambda e: e.tensor_scalar(nE[:, :], pB[:, :], eb[:, n, gi:gi + 1], None, ALU.mult), reads=[pB, eb], writes=[nE])
                            S.op("act", lambda e: e.activation(dN[:, :], nE[:, 128:129], AF.Abs), reads=[nE], writes=[dN])
                            S.op("dve", lambda e: e.tensor_scalar(dN[:, :], dN[:, :], 1.0, None, ALU.max), reads=[dN], writes=[dN])
                            S.op("dve", lambda e: e.reciprocal(dN[:, :], dN[:, :]), reads=[dN], writes=[dN])
                            S.op("dve", lambda e: e.scalar_tensor_tensor(hm[:, n, h * 128:(h + 1) * 128], nE[:, 0:128], dN[:, :], hm[:, n, h * 128:(h + 1) * 128],
                                                                         ALU.mult, ALU.add), reads=[nE, dN, hm], writes=[hm])
                for n in range(NTILE):
                    og = osg[n % 2]; abt = ab[n % 2]; at = aT[n % 2]
                    S.dma("sp", og[:, :], sc["osig"][b, n * 128:(n + 1) * 128, :], reads=[sc["osig"]], writes=[og])
                    S.op("pool", lambda e, n=n: e.tensor_tensor(sq[:, :], hm[:, n, :], hm[:, n, :], ALU.mult), reads=[hm], writes=[sq])
                    S.op("dve", lambda e: e.tensor_reduce(ss4[:, :], sq.ap.rearrange("p (h d) -> p h d", h=4), AX.X, ALU.add), reads=[sq], writes=[ss4])
                    S.op("dve", lambda e: e.tensor_copy(rs4[:, :], ss4[:, :]), reads=[ss4], writes=[rs4])
                    self.rsqrt(rs4, (slice(None), slice(None)), 1.0 / 128, EPS)
                    S.op("dve", lambda e, n=n: e.tensor_tensor(sq.ap.rearrange("p (h d) -> p h d", h=4), hm[:, n, :].rearrange("p (h d) -> p h d", h=4),
                                                                rs4.ap.unsqueeze(2).broadcast_to([128, 4, 128]), ALU.mult), reads=[hm, rs4], writes=[sq])
                    S.op("pool", lambda e: e.tensor_tensor(sq[:, :], sq[:, :], hn[:, :], ALU.mult), reads=[sq, hn], writes=[sq])
                    S.op("dve", lambda e, og=og, abt=abt: e.tensor_tensor(abt[:, :], sq[:, :], og[:, :], ALU.mult), reads=[sq, og], writes=[abt])
                    for j in range(4):
                        S.op("pe", lambda e, j=j, abt=abt: e.transpose(pTo[:, j * 128:(j + 1) * 128], abt[:, j * 128:(j + 1) * 128], self.ident_b[:, :]),
                             reads=[abt, self.ident_b], writes=[pTo])
                    S.op("act", lambda e, at=at: e.copy(at[:, :, :], pTo.ap.rearrange("p (j t) -> p j t", j=4)), reads=[pTo], writes=[at])
                    S.dma("pool", sc["mixT"][b, 0:512, n * 128:(n + 1) * 128].rearrange("(j p) t -> p j t", p=128), at[:, :, :], reads=[at], writes=[sc["mixT"]])

    def phase_filters(self, l):
        S, I, sc = self.S, self.inp, self.scr
        i = l // 2
        with Phase(S, f"hf{l}") as P:
            w1 = P.sb([33, 64], F32, "w1"); w2 = P.sb([64, 64], F32, "w2"); w3 = P.sb([64, 2048], F32, "w3")
            S.dma("sp", w1[:, :], I["e_f_w1"][i], writes=[w1])
            S.dma("sp", w2[:, :], I["e_f_w2"][i], writes=[w2])
            S.dma("sp", w3[:, :], I["e_f_w3"][i], writes=[w3])
            fb = P.sb([64, 3], F32, "fb")
            S.dma("pool", fb[:, 0:1], I["e_f_freq"][i].rearrange("(p o) -> p o", o=1), writes=[fb])
            S.dma("pool", fb[:, 1:2], I["e_f_b1"][i].rearrange("(p o) -> p o", o=1), writes=[fb])
            S.dma("pool", fb[:, 2:3], I["e_f_b2"][i].rearrange("(p o) -> p o", o=1), writes=[fb])
            fbb = P.sb([64, 2], F32, "fbb")
            S.op("dve", lambda e: e.tensor_scalar(fbb[:, :], fb[:, 1:3], fb[:, 0:1], None, ALU.mult), reads=[fb], writes=[fbb])
            embT = P.sb([33, SEQ], F32, "embT")
            h1 = P.sb([64, SEQ], F32, "h1"); h2 = P.sb([64, SEQ], F32, "h2")
            pre = P.sb([64, 512], F32, "pre")
            kk = P.sb([64, 512], F32, "kk")
            ps = [P.ps([128, 512], F32, "ps") for _ in range(2)]
            hd = P.sb([128, SEQ], F32, "hd"); dcy = P.sb([128, SEQ], F32, "dcy")
            junk = P.sb([128, SEQ], F32, "junk")
            ssq = P.sb([128, 1], F32, "ssq")
            for nm, L, filt in (("L", SEQ, sc["filtL"]), ("C", CTX, sc["filtC"])):
                cw = min(512, L)
                S.dma("sp", embT[:, 0:L], I["embT_" + nm][:, :], writes=[embT])
                for (src, wgt, dst, K, bi) in ((embT, w1, h1, 33, 0), (h1, w2, h2, 64, 1)):
                    for c0 in range(0, L, cw):
                        p = ps[(c0 // cw) % 2]
                        S.op("pe", lambda e, p=p, src=src, wgt=wgt, c0=c0, K=K: e.matmul(p[0:64, 0:cw], wgt[0:K, :], src[0:K, c0:c0 + cw], start=True, stop=True),
                             reads=[src, wgt], writes=[p])
                        S.op("dve", lambda e, p=p, bi=bi: e.tensor_scalar(pre[:, 0:cw], p[0:64, 0:cw], fb[:, 0:1], fbb[:, bi:bi + 1], ALU.mult, ALU.add),
                             reads=[p, fb, fbb], writes=[pre])
                        MAGIC = 12582912.0
                        S.op("dve", lambda e: e.tensor_scalar(kk[:, 0:cw], pre[:, 0:cw], float(1.0 / (2 * PI)), MAGIC, ALU.mult, ALU.add), reads=[pre], writes=[kk])
                        S.op("dve", lambda e: e.tensor_scalar(kk[:, 0:cw], kk[:, 0:cw], -MAGIC, None, ALU.add), reads=[kk], writes=[kk])
                        S.op("dve", lambda e: e.scalar_tensor_tensor(pre[:, 0:cw], kk[:, 0:cw], float(-2 * PI), pre[:, 0:cw], ALU.mult, ALU.add), reads=[kk, pre], writes=[pre])
                        S.op("dve", lambda e: e.tensor_scalar(kk[:, 0:cw], pre[:, 0:cw], float(PI), None, ALU.is_gt), reads=[pre], writes=[kk])
                        S.op("dve", lambda e: e.scalar_tensor_tensor(pre[:, 0:cw], kk[:, 0:cw], float(-2 * PI), pre[:, 0:cw], ALU.mult, ALU.add), reads=[kk, pre], writes=[pre])
                        S.op("dve", lambda e: e.tensor_scalar(kk[:, 0:cw], pre[:, 0:cw], float(-PI), None, ALU.is_lt), reads=[pre], writes=[kk])
                        S.op("dve", lambda e: e.scalar_tensor_tensor(pre[:, 0:cw], kk[:, 0:cw], float(2 * PI), pre[:, 0:cw], ALU.mult, ALU.add), reads=[kk, pre], writes=[pre])
                        S.op("dve", lambda e: e.tensor_scalar(pre[:, 0:cw], pre[:, 0:cw], 3.141592, -3.141592, ALU.min, ALU.max), reads=[pre], writes=[pre])
                        S.op("act", lambda e, dst=dst, c0=c0: e.activation(dst[:, c0:c0 + cw], pre[:, 0:cw], AF.Sin), reads=[pre], writes=[dst])
                for ch in range(16):
                    o_, d_, cq = ch // 8, (ch // 4) % 2, ch % 4
                    S.dma("sp", dcy[:, 0:L], I["dec_" + nm][cq * 128:(cq + 1) * 128, :], writes=[dcy])
                    for c0 in range(0, L, cw):
                        p = ps[(c0 // cw) % 2]
                        S.op("pe", lambda e, p=p, ch=ch, c0=c0: e.matmul(p[:, 0:cw], w3[:, ch * 128:(ch + 1) * 128], h2[:, c0:c0 + cw], start=True, stop=True),
                             reads=[w3, h2], writes=[p])
                        S.op("dve", lambda e, p=p, c0=c0: e.tensor_tensor(hd[:, c0:c0 + cw], p[:, 0:cw], dcy[:, c0:c0 + cw], ALU.mult), reads=[p, dcy], writes=[hd])
                    S.op("dve", lambda e: e.memset(ssq[:, :], 0.0), writes=[ssq])
                    S.op("act", lambda e: e.activation(junk[:, 0:L], hd[:, 0:L], AF.Square, accum_out=ssq[:, :]), reads=[hd, ssq], writes=[junk, ssq])
                    self.rsqrt(ssq, (slice(None), slice(None)), 1.0, EPS)
                    S.op("act", lambda e: e.activation(junk[:, 0:L], hd[:, 0:L], AF.Copy, scale=ssq[:, :]), reads=[hd, ssq], writes=[junk])
                    S.dma("pool", filt[o_, d_, cq * 128:(cq + 1) * 128, :], junk[:, 0:L], reads=[junk], writes=[filt])

    def phase_hyena(self, l):
        S, I, sc = self.S, self.inp, self.scr
        i = l // 2
        with Phase(S, f"hy{l}") as P:
            cw = P.sb([128, 3, 4, 3], F32, "cw")
            cb = P.sb([128, 3, 4], F32, "cb")
            hd = P.sb([128, 2, 4], F32, "hd")
            for part in range(3):
                for tap in range(3):
                    S.dma("pool", cw[:, part, :, tap], I["e_conv_w"][i, tap, part * 512:(part + 1) * 512].rearrange("(q p) -> p q", p=128),
                          writes=[cw], allow_slow_non_contiguous=True)
                S.dma("pool", cb[:, part, :], I["e_conv_b"][i, part * 512:(part + 1) * 512].rearrange("(q p) -> p q", p=128), writes=[cb], allow_slow_non_contiguous=True)
            for o_ in range(2):
                S.dma("pool", hd[:, o_, :], I["e_hy_d"][i, o_, :].rearrange("(q p) -> p q", p=128), writes=[hd], allow_slow_non_contiguous=True)
            for (L, tok0, filt) in ((SEQ, 0, sc["filtL"]), (CTX, SEQ, sc["filtC"])):
                raw = P.sb([128, NB, L + 2], F32, "raw")
                ys = [P.sb([128, NB, L], F32, "ys") for _ in range(3)]
                accd = P.sb([128, NB, L], F32, "accd"); accp = P.sb([128, NB, L], F32, "accp")
                gf = P.sb([128, L], F32, "gf"); gbk = P.sb([128, L], F32, "gbk")
                ob = P.sb([128, NB, L], BF16, "ob")
                S.op("pool", lambda e: e.memset(raw[:, :, :], 0.0), writes=[raw])
                for cq in range(4):
                    for part in range(3):
                        r0 = part * 512 + cq * 128
                        S.dma("sp", raw[:, :, 1:L + 1], sc["hy"][:, r0:r0 + 128, tok0:tok0 + L].rearrange("b p t -> p b t"), reads=[sc["hy"]], writes=[raw])
                        y = ys[part]
                        S.op("dve", lambda e, y=y, part=part: e.tensor_scalar(y[:, :, :], raw[:, :, 0:L], cw[:, part, cq, 0:1], cb[:, part, cq:cq + 1], ALU.mult, ALU.add),
                             reads=[raw, cw, cb], writes=[y])
                        S.op("dve", lambda e, y=y, part=part: e.scalar_tensor_tensor(y[:, :, :], raw[:, :, 1:L + 1], cw[:, part, cq, 1:2], y[:, :, :], ALU.mult, ALU.add),
                             reads=[raw, cw, y], writes=[y])
                        S.op("dve", lambda e, y=y, part=part: e.scalar_tensor_tensor(y[:, :, :], raw[:, :, 2:L + 2], cw[:, part, cq, 2:3], y[:, :, :], ALU.mult, ALU.add),
                             reads=[raw, cw, y], writes=[y])
                    z = ys[0]
                    for o_ in range(2):
                        S.dma("sp", gf[:, :], filt[o_, 0, cq * 128:(cq + 1) * 128, :], reads=[filt], writes=[gf])
                        S.dma("act", gbk[:, :], filt[o_, 1, cq * 128:(cq + 1) * 128, :], reads=[filt], writes=[gbk])
                        S.op("dve", lambda e: e.tensor_scalar(accd[:, :, :], z[:, :, :], hd[:, o_, cq:cq + 1], None, ALU.mult), reads=[z, hd], writes=[accd])
                        S.op("pool", lambda e: e.memset(accp[:, :, :], 0.0), writes=[accp])
                        taps = [(0, s) for s in range(L)] + [(1, s) for s in range(L)]
                        total = sum(L - s for _, s in taps)
                        pool_budget = total * self.hy_pool_frac
                        acc_cost = 0.0
                        for (dr, s) in taps:
                            g = gf if dr == 0 else gbk
                            use_pool = False
                            if acc_cost < pool_budget and ((s % 3) == 1 or self.hy_pool_frac >= 0.5):
                                use_pool = True
                                acc_cost += (L - s)
                            eng, acc = ("pool", accp) if use_pool else ("dve", accd)
                            if dr == 0:
                                oa = acc[:, :, s:L]; za = z[:, :, 0:L - s]
                            else:
                                oa = acc[:, :, 0:L - s]; za = z[:, :, s:L]
                            S.op(eng, lambda e, oa=oa, za=za, g=g, s=s: e.scalar_tensor_tensor(oa, za, g[:, s:s + 1], oa, ALU.mult, ALU.add),
                                 reads=[z, g, acc], writes=[acc])
                        S.op("dve", lambda e: e.tensor_tensor(accd[:, :, :], accd[:, :, :], accp[:, :, :], ALU.add), reads=[accd, accp], writes=[accd])
                        gate = ys[1 + o_]
                        if o_ == 0:
                            S.op("dve", lambda e: e.tensor_tensor(z[:, :, :], accd[:, :, :], gate[:, :, :], ALU.mult), reads=[accd, gate], writes=[z])
                        else:
                            S.op("dve", lambda e: e.tensor_tensor(ob[:, :, :], accd[:, :, :], gate[:, :, :], ALU.mult), reads=[accd, gate], writes=[ob])
                    S.dma("sp", sc["mixT"][:, 512 + cq * 128:512 + (cq + 1) * 128, tok0:tok0 + L].rearrange("b p t -> p b t"), ob[:, :, :], reads=[ob], writes=[sc["mixT"]])

    def phase_filters_pe(self, l):
        S, I, sc = self.S, self.inp, self.scr
        i = l // 2
        with Phase(S, f"hg{l}") as P:
            w1 = P.sb([33, 64], F32, "w1"); w2 = P.sb([64, 64], F32, "w2"); w3 = P.sb([64, 2048], F32, "w3")
            S.dma("sp", w1[:, :], I["e_f_w1"][i], writes=[w1])
            S.dma("sp", w2[:, :], I["e_f_w2"][i], writes=[w2])
            S.dma("sp", w3[:, :], I["e_f_w3"][i], writes=[w3])
            fb = P.sb([64, 3], F32, "fb")
            S.dma("pool", fb[:, 0:1], I["e_f_freq"][i].rearrange("(p o) -> p o", o=1), writes=[fb])
            S.dma("pool", fb[:, 1:2], I["e_f_b1"][i].rearrange("(p o) -> p o", o=1), writes=[fb])
            S.dma("pool", fb[:, 2:3], I["e_f_b2"][i].rearrange("(p o) -> p o", o=1), writes=[fb])
            fbb = P.sb([64, 2], F32, "fbb")
            S.op("dve", lambda e: e.tensor_scalar(fbb[:, :], fb[:, 1:3], fb[:, 0:1], None, ALU.mult), reads=[fb], writes=[fbb])
            embT = P.sb([33, SEQ], F32, "embT")
            h1 = P.sb([64, SEQ], F32, "h1")
            h2s = [P.sb([64, SEQ], F32, "h2") for _ in range(2)]
            pre = P.sb([64, 512], F32, "pre"); kk = P.sb([64, 512], F32, "kk")
            ps = [P.ps([128, 512], F32, "ps") for _ in range(2)]
            hd = P.sb([128, SEQ], F32, "hd"); dcy = P.sb([128, SEQ], F32, "dcy")
            junk = P.sb([128, SEQ], F32, "junk")
            ssq = P.sb([128, 1], F32, "ssq")
            gt = P.sb([128, 2 * SEQ], F32, "gt")
            gbf = P.sb([128, 2 * SEQ], BF16, "gbf")
            MAGIC = 12582912.0
            for nm, L, gp in (("L", SEQ, sc["gpL"]), ("C", CTX, sc["gpC"])):
                cw = min(512, L)
                for rv in range(2):
                    h2 = h2s[rv]
                    S.dma("sp", embT[:, 0:L], I[("embTr_" if rv else "embT_") + nm][:, :], writes=[embT])
                    for (src_, wgt, dst, K, bi) in ((embT, w1, h1, 33, 0), (h1, w2, h2, 64, 1)):
                        for c0 in range(0, L, cw):
                            p = ps[(c0 // cw) % 2]
                            S.op("pe", lambda e: e.matmul(p[0:64, 0:cw], wgt[0:K, :], src_[0:K, c0:c0 + cw], start=True, stop=True), reads=[src_, wgt], writes=[p])
                            S.op("dve", lambda e: e.tensor_scalar(pre[:, 0:cw], p[0:64, 0:cw], fb[:, 0:1], fbb[:, bi:bi + 1], ALU.mult, ALU.add), reads=[p, fb, fbb], writes=[pre])
                            S.op("dve", lambda e: e.tensor_scalar(kk[:, 0:cw], pre[:, 0:cw], float(1.0 / (2 * PI)), MAGIC, ALU.mult, ALU.add), reads=[pre], writes=[kk])
                            S.op("dve", lambda e: e.tensor_scalar(kk[:, 0:cw], kk[:, 0:cw], -MAGIC, None, ALU.add), reads=[kk], writes=[kk])
                            S.op("dve", lambda e: e.scalar_tensor_tensor(pre[:, 0:cw], kk[:, 0:cw], float(-2 * PI), pre[:, 0:cw], ALU.mult, ALU.add), reads=[kk, pre], writes=[pre])
                            S.op("dve", lambda e: e.tensor_scalar(kk[:, 0:cw], pre[:, 0:cw], float(PI), None, ALU.is_gt), reads=[pre], writes=[kk])
                            S.op("dve", lambda e: e.scalar_tensor_tensor(pre[:, 0:cw], kk[:, 0:cw], float(-2 * PI), pre[:, 0:cw], ALU.mult, ALU.add), reads=[kk, pre], writes=[pre])
                            S.op("dve", lambda e: e.tensor_scalar(kk[:, 0:cw], pre[:, 0:cw], float(-PI), None, ALU.is_lt), reads=[pre], writes=[kk])
                            S.op("dve", lambda e: e.scalar_tensor_tensor(pre[:, 0:cw], kk[:, 0:cw], float(2 * PI), pre[:, 0:cw], ALU.mult, ALU.add), reads=[kk, pre], writes=[pre])
                            S.op("dve", lambda e: e.tensor_scalar(pre[:, 0:cw], pre[:, 0:cw], 3.141592, -3.141592, ALU.min, ALU.max), reads=[pre], writes=[pre])
                            S.op("act", lambda e: e.activation(dst[:, c0:c0 + cw], pre[:, 0:cw], AF.Sin), reads=[pre], writes=[dst])
                for o_ in range(2):
                    for cq in range(4):
                        S.op("pool", lambda e: e.memset(gt[:, 0:1], 0.0), writes=[gt])
                        for d_ in (1, 0):
                            ch = o_ * 8 + d_ * 4 + cq
                            h2 = h2s[d_]
                            S.dma("sp", dcy[:, 0:L], I[("decr_" if d_ else "dec_") + nm][cq * 128:(cq + 1) * 128, :], writes=[dcy])
                            for c0 in range(0, L, cw):
                                p = ps[(c0 // cw) % 2]
                                S.op("pe", lambda e: e.matmul(p[:, 0:cw], w3[:, ch * 128:(ch + 1) * 128], h2[:, c0:c0 + cw], start=True, stop=True), reads=[w3, h2], writes=[p])
                                S.op("dve", lambda e: e.tensor_tensor(hd[:, c0:c0 + cw], p[:, 0:cw], dcy[:, c0:c0 + cw], ALU.mult), reads=[p, dcy], writes=[hd])
                            S.op("dve", lambda e: e.memset(ssq[:, :], 0.0), writes=[ssq])
                            S.op("act", lambda e: e.activation(junk[:, 0:L], hd[:, 0:L], AF.Square, accum_out=ssq[:, :]), reads=[hd, ssq], writes=[junk, ssq])
                            self.rsqrt(ssq, (slice(None), slice(None)), 1.0, EPS)
                            if d_ == 1:
                                S.op("act", lambda e: e.activation(gt[:, 1:L + 1], hd[:, 0:L], AF.Copy, scale=ssq[:, :]), reads=[hd, ssq], writes=[gt])
                            else:
                                S.op("act", lambda e: e.activation(junk[:, 0:L], hd[:, 0:L], AF.Copy, scale=ssq[:, :]), reads=[hd, ssq], writes=[junk])
                                S.op("dve", lambda e: e.tensor_copy(gt[:, L + 1:2 * L], junk[:, 1:L]), reads=[junk], writes=[gt])
                                S.op("dve", lambda e: e.tensor_tensor(gt[:, L:L + 1], gt[:, L:L + 1], junk[:, 0:1], ALU.add), reads=[junk, gt], writes=[gt])
                        S.op("act", lambda e: e.copy(gbf[:, 0:2 * L], gt[:, 0:2 * L]), reads=[gt], writes=[gbf])
                        S.dma("pool", gp[o_, cq * 128:(cq + 1) * 128, :], gbf[:, 0:2 * L], reads=[gbf], writes=[gp])

    def phase_hyena_pe(self, l):
        S, I, sc = self.S, self.inp, self.scr
        nc = self.nc
        i = l // 2
        with Phase(S, f"hp{l}") as P:
            self.load_consts(P)
            ident_f = P.sb([128, 128], F32, "identf")
            S.dma("sp", ident_f[:, :], I["ident_f"][:, :], writes=[ident_f])
            jrev = P.sb([128, 128], BF16, "jrev")
            S.dma("sp", jrev[:, :], I["jrev_b"][:, :], writes=[jrev])
            cw = P.sb([128, 3, 4, 3], F32, "cw")
            cb = P.sb([128, 3, 4], F32, "cb")
            for part in range(3):
                for tap in range(3):
                    S.dma("pool", cw[:, part, :, tap], I["e_conv_w"][i, tap, part * 512:(part + 1) * 512].rearrange("(q p) -> p q", p=128),
                          writes=[cw], allow_slow_non_contiguous=True)
                S.dma("pool", cb[:, part, :], I["e_conv_b"][i, part * 512:(part + 1) * 512].rearrange("(q p) -> p q", p=128), writes=[cb], allow_slow_non_contiguous=True)
            dbc = P.sb([128, 2, 512], F32, "dbc")
            for o_ in range(2):
                S.dma("pool", dbc[:, o_, :], I["e_hy_d"][i, o_:o_ + 1, :].partition_broadcast(128), writes=[dbc])
            psT = [P.ps([128, 512], F32, "psT") for _ in range(2)]
            psB = P.ps([128, 1024], BF16, "psB")
            for (L, tok0, gp, nb) in ((SEQ, 0, sc["gpL"], SEQ // 128), (CTX, SEQ, sc["gpC"], CTX // 128)):
              with Phase(S, f"hp{l}_{L}") as P:
                  GW = 2 * L - 128
                  raw = P.sb([128, NB, L + 2], F32, "raw")
                  ybuf = P.sb([128, NB, L], F32, "ys")
                  tm = [P.sb([128, NB * nb, 128], F32, "tm") for _ in range(3)]
                  zb = P.sb([128, NB * nb * 128], BF16, "zb")
                  KB = self.hy_kb
                  NQ = 128 // KB
                  zq = P.sb([KB, NQ, NB, nb, 128], BF16, "zq")
                  Yt = P.sb([128, NB * nb, 128], F32, "Yt")
                  outb = P.sb([128, NB * nb, 128], BF16, "outb")
                  WW = 2 * L - KB
                  ring = [P.sb([KB, WW], BF16, "G") for _ in range(3)]
                  psY = [P.ps([128, 16, NB, nb], F32, "psY") for _ in range(2)]
                  ob = P.sb([128, NB, L], BF16, "ob")
                  S.op("pool", lambda e: e.memset(raw[:, :, :], 0.0), writes=[raw])
                  gtensor = gp.ap.tensor
                  nG = 0
                  for cq in range(4):
                      for part in range(3):
                          r0 = part * 512 + cq * 128
                          S.dma("sp", raw[:, :, 1:L + 1], sc["hy"][:, r0:r0 + 128, tok0:tok0 + L].rearrange("b p t -> p b t"), reads=[sc["hy"]], writes=[raw])
                          y = ybuf
                          S.op("dve", lambda e: e.tensor_scalar(y[:, :, :], raw[:, :, 0:L], cw[:, part, cq, 0:1], cb[:, part, cq:cq + 1], ALU.mult, ALU.add),
                               reads=[raw, cw, cb], writes=[y])
                          S.op("dve", lambda e: e.scalar_tensor_tensor(y[:, :, :], raw[:, :, 1:L + 1], cw[:, part, cq, 1:2], y[:, :, :], ALU.mult, ALU.add),
                               reads=[raw, cw, y], writes=[y])
                          S.op("dve", lambda e: e.scalar_tensor_tensor(y[:, :, :], raw[:, :, 2:L + 2], cw[:, part, cq, 2:3], y[:, :, :], ALU.mult, ALU.add),
                               reads=[raw, cw, y], writes=[y])
                          k = 0
                          blocks = [(b, j) for b in range(NB) for j in range(nb)]
                          for g0 in range(0, len(blocks), 4):
                              grp = blocks[g0:g0 + 4]
                              pt = psT[(g0 // 4) % 2]
                              for q, (b, j) in enumerate(grp):
                                  S.op("pe", lambda e: e.transpose(pt[:, q * 128:(q + 1) * 128], y[:, b, j * 128:(j + 1) * 128], ident_f[:, :]),
                                       reads=[y, ident_f], writes=[pt])
                              n_ = len(grp)
                              S.op("act" if (g0 // 4) % 2 else "dve",
                                   (lambda e: e.copy(tm[part][:, g0:g0 + n_, :], pt[:, 0:n_ * 128].rearrange("p (q c) -> p q c", q=n_))) if (g0 // 4) % 2 else
                                   (lambda e: e.tensor_copy(tm[part][:, g0:g0 + n_, :], pt[:, 0:n_ * 128].rearrange("p (q c) -> p q c", q=n_))),
                                   reads=[pt], writes=[tm[part]])
                      zt = tm[0]
                      for o_ in range(2):
                          S.op("act", lambda e: e.copy(zb.ap.rearrange("p (n c) -> p n c", c=128), zt[:, :, :]), reads=[zt], writes=[zb])
                          NEL = NB * nb * 128
                          zqv = zq.ap.rearrange("p q b j c -> p q (b j c)")
                          kq = 0
                          for q0 in range(0, NEL, 512):
                              q1 = min(NEL, q0 + 512)
                              for hi in range(NQ):
                                  pt = psT[kq % 2]; kq += 1
                                  S.op("pe", lambda e: e.matmul(pt[0:KB, 0:q1 - q0], jrev[:, KB * hi:KB * (hi + 1)], zb[:, q0:q1], start=True, stop=True),
                                       reads=[jrev, zb], writes=[pt])
                                  if kq % 2:
                                      S.op("dve", lambda e: e.tensor_copy(zqv[:, hi, q0:q1], pt[0:KB, 0:q1 - q0]), reads=[pt], writes=[zq])
                                  else:
                                      S.op("act", lambda e: e.copy(zqv[:, hi, q0:q1], pt[0:KB, 0:q1 - q0]), reads=[pt], writes=[zq])
                          for c0 in range(0, 128, 16):
                              py = psY[(c0 // 16) % 2]
                              for cs in range(16):
                                  c = c0 + cs
                                  G = ring[nG % 3]
                                  q_ = ("sp", "pool")[nG % 2]
                                  nG += 1
                                  src_ap = bass.AP(tensor=gtensor, offset=(o_ * 512 + cq * 128 + c) * (2 * L) + 1, ap=[[1, KB], [1, WW]])
                                  S.dma(q_, G[:, :], src_ap, reads=[gp], writes=[G])
                                  dlist = [0]
                                  for dd in range(1, nb):
                                      dlist += [dd, -dd]
                                  nmm = len(dlist) * NQ
                                  km = 0
                                  for di, d_ in enumerate(dlist):
                                      j0, j1 = max(0, -d_), min(nb, nb - d_)
                                      i0, i1 = j0 + d_, j1 + d_
                                      for hi in range(NQ):
                                          w0 = (d_ + nb - 1) * 128 + KB * hi
                                          S.op("pe", lambda e: e.matmul(py[:, cs, :, i0:i1], G[:, w0:w0 + 128], zq[:, hi, :, j0:j1, c],
                                                                        start=(km == 0), stop=(km == nmm - 1)), reads=[G, zq], writes=[py])
                                          km += 1
                              S.op("dve", lambda e: e.tensor_copy(Yt[:, :, c0:c0 + 16].rearrange("p (b j) c -> p b j c", b=NB), py.ap.rearrange("p c b j -> p b j c")),
                                   reads=[py], writes=[Yt])
                          dsl = dbc[:, o_, cq * 128:(cq + 1) * 128].unsqueeze(1).broadcast_to([128, NB * nb, 128])
                          t1v = ybuf.ap.rearrange("p b (j c) -> p (b j) c", c=128)
                          S.op("pool", lambda e: e.tensor_tensor(t1v, zt[:, :, :], dsl, ALU.mult), reads=[zt, dbc], writes=[ybuf])
                          S.op("dve", lambda e: e.tensor_tensor(Yt[:, :, :], Yt[:, :, :], t1v, ALU.add), reads=[Yt, ybuf], writes=[Yt])
                          gate = tm[1 + o_]
                          if o_ == 0:
                              S.op("dve", lambda e: e.tensor_tensor(zt[:, :, :], Yt[:, :, :], gate[:, :, :], ALU.mult), reads=[Yt, gate], writes=[zt])
                          else:
                              S.op("dve", lambda e: e.tensor_tensor(outb[:, :, :], Yt[:, :, :], gate[:, :, :], ALU.mult), reads=[Yt, gate], writes=[outb])
                      for b in range(NB):
                          for j0 in range(0, nb, 8):
                              n_ = min(8, nb - j0)
                              for q in range(n_):
                                  S.op("pe", lambda e: e.transpose(psB[:, q * 128:(q + 1) * 128], outb[:, b * nb + j0 + q, :], self.ident_b[:, :]),
                                       reads=[outb, self.ident_b], writes=[psB])
                              S.op("act", lambda e: e.copy(ob[:, b, j0 * 128:(j0 + n_) * 128], psB[:, 0:n_ * 128]), reads=[psB], writes=[ob])
                      S.dma("sp", sc["mixT"][:, 512 + cq * 128:512 + (cq + 1) * 128, tok0:tok0 + L].rearrange("b p t -> p b t"), ob[:, :, :], reads=[ob], writes=[sc["mixT"]])

    def phase_outproj(self, l, w_ap, last):
        S, I, sc = self.S, self.inp, self.scr
        with Phase(S, f"op{l}") as P:
            stage = [P.sb([128, 8, 512], F32, "stg") for _ in range(2)]
            W = self.load_weight_bf16(P, w_ap, D, D, "Wo", stage)
            g1 = self.load_gate_bc(P, l, 0)
            mT = [P.sb([128, 8, 512], BF16, "mT") for _ in range(2)]
            xts = [P.sb([128, D], F32, "xt") for _ in range(2)]
            ps = [P.ps([128, 512], F32, "ps") for _ in range(4)]
            tmp = [P.sb([128, 512], F32, "tmp") for _ in range(2)]
            n = 0
            for gi, (b, tok0, nt, r) in enumerate(self.groups(with_ctx=not last)):
                m = mT[gi % 2]
                ntok = nt * 128
                S.dma("sp", m[:, :, 0:ntok], sc["mixT"][b, :, tok0:tok0 + ntok].rearrange("(j p) t -> p j t", p=128), reads=[sc["mixT"]], writes=[m])
                for ti in range(nt):
                    t0 = tok0 + ti * 128
                    xt = xts[n % 2]
                    S.dma("act", xt[:, :], sc["xs"][b, t0:t0 + 128, :], reads=[sc["xs"]], writes=[xt])
                    for half in range(2):
                        p = ps[(2 * n + half) % 4]; tm = tmp[half]
                        for j in range(8):
                            S.op("pe", lambda e, j=j, p=p, half=half: e.matmul(p[:, :], m[:, j, ti * 128:(ti + 1) * 128], W[:, j, half * 512:(half + 1) * 512],
                                                                               start=(j == 0), stop=(j == 7)), reads=[m, W], writes=[p])
                        S.op("dve", lambda e, p=p, tm=tm, half=half: e.tensor_tensor(tm[:, :], p[:, :], g1[:, r, half * 512:(half + 1) * 512], ALU.mult), reads=[p, g1], writes=[tm])
                        S.op("pool", lambda e, tm=tm, half=half, xt=xt: e.tensor_tensor(xt[:, half * 512:(half + 1) * 512], xt[:, half * 512:(half + 1) * 512], tm[:, :], ALU.add),
                             reads=[xt, tm], writes=[xt])
                    S.dma("pool", sc["xs"][b, t0:t0 + 128, :], xt[:, :], reads=[xt], writes=[sc["xs"]])
                    n += 1

    def phase_mlp(self, l, last):
        S, I, sc = self.S, self.inp, self.scr
        GT = 2
        with Phase(S, f"mlp{l}") as P:
            self.load_consts(P)
            scp, shp = self.load_mod_fm(P, l, 1)
            g2 = self.load_gate_bc(P, l, 1)
            stage = [P.sb([128, 8, 256], F32, "stg") for _ in range(2)]
            W1 = self.load_weight_bf16(P, I["w_mlp_in"][l], D, DFF, "W1", stage, cw=256)
            W2 = self.load_weight_bf16(P, I["w_mlp_out"][l], DFF, D, "W2", stage, cw=256)
            tmp = self.norm_tmp(P)
            xts = [P.sb([128, D], F32, "xt") for _ in range(3)]
            hT = P.sb([128, 8, GT * 128], BF16, "hT")
            aT = P.sb([128, 32, GT * 128], BF16, "aT")
            ps = [P.ps([128, 512], F32, "ps") for _ in range(4)]
            tm = [P.sb([128, 512], F32, "tm") for _ in range(2)]
            rl = [P.sb([128, GT * 128], F32, "rl") for _ in range(2)]
            grp = []
            for b in range(NB):
                for k in range(0, SEQ // 128, GT):
                    grp.append((b, k * 128, GT, b))
                if not last:
                    grp.append((b, SEQ, 2, 2))
            n = 0
            for gi, (b, tok0, nt, r) in enumerate(grp):
                ntok = nt * 128
                gx = []
                for ti in range(nt):
                    xt = xts[n % 3]; n += 1
                    gx.append(xt)
                    t0 = tok0 + ti * 128
                    S.dma("sp", xt[:, :], sc["xs"][b, t0:t0 + 128, :], reads=[sc["xs"]], writes=[xt])
                    self.norm_to_hT(xt, scp, shp, r, hT, ti, tmp)
                for fc in range(32):
                    p = ps[fc % 2]; rr = rl[fc % 2]
                    for j in range(8):
                        S.op("pe", lambda e, j=j, p=p, fc=fc: e.matmul(p[:, 0:ntok], W1[:, j, fc * 128:(fc + 1) * 128], hT[:, j, 0:ntok], start=(j == 0), stop=(j == 7)),
                             reads=[W1, hT], writes=[p])
                    S.op("act", lambda e, p=p, rr=rr: e.activation(rr[:, 0:ntok], p[:, 0:ntok], AF.Relu), reads=[p], writes=[rr])
                    eng = "dve" if fc % 2 == 0 else "pool"
                    S.op(eng, lambda e, rr=rr, fc=fc: e.tensor_tensor(aT[:, fc, 0:ntok], rr[:, 0:ntok], rr[:, 0:ntok], ALU.mult), reads=[rr], writes=[aT])
                for ti in range(nt):
                    xt = gx[ti]
                    t0 = tok0 + ti * 128
                    for half in range(2):
                        p = ps[2 + half]; t_ = tm[half]
                        for fc in range(32):
                            S.op("pe", lambda e, fc=fc, p=p, half=half: e.matmul(p[:, :], aT[:, fc, ti * 128:(ti + 1) * 128], W2[:, fc, half * 512:(half + 1) * 512],
                                                                                 start=(fc == 0), stop=(fc == 31)), reads=[aT, W2], writes=[p])
                        S.op("dve", lambda e, p=p, t_=t_, half=half: e.tensor_tensor(t_[:, :], p[:, :], g2[:, r, half * 512:(half + 1) * 512], ALU.mult), reads=[p, g2], writes=[t_])
                        S.op("pool", lambda e, t_=t_, half=half, xt=xt: e.tensor_tensor(xt[:, half * 512:(half + 1) * 512], xt[:, half * 512:(half + 1) * 512], t_[:, :], ALU.add),
                             reads=[xt, t_], writes=[xt])
                    if last:
                        S.dma("pool", self.out[b, t0:t0 + 128, :], xt[:, :], reads=[xt])
                    else:
                        S.dma("pool", sc["xs"][b, t0:t0 + 128, :], xt[:, :], reads=[xt], writes=[sc["xs"]])

    def phase_odd_proj(self, l):
        S, I, sc = self.S, self.inp, self.scr
        i = l // 2
        with Phase(S, f"oq{l}") as P:
            self.load_consts(P)
            scp, shp = self.load_mod_fm(P, l, 0)
            stage = [P.sb([128, 8, 256], F32, "stg") for _ in range(2)]
            W = self.load_weight_bf16(P, I["o_w_qkv"][i], D, 3 * D, "Wqkv", stage, cw=256)
            gn = P.sb([128, 2, 64], F32, "gn")
            S.dma("sp", gn[:, 0, :], I["o_qn"][i:i + 1, :].partition_broadcast(128), writes=[gn])
            S.dma("sp", gn[:, 1, :], I["o_kn"][i:i + 1, :].partition_broadcast(128), writes=[gn])
            S.op("dve", lambda e: e.tensor_scalar(gn[:, 0, :], gn[:, 0, :], 0.125, None, ALU.mult), reads=[gn], writes=[gn])
            tmp = self.norm_tmp(P)
            xts = [P.sb([128, D], F32, "xt") for _ in range(2)]
            hT = P.sb([128, 8, 128], BF16, "hT")
            psq = [P.ps([128, 512], F32, "psq") for _ in range(6)]
            pT2 = P.ps([128, 1024], BF16, "pT2")
            sq = P.sb([128, D], F32, "sq")
            ss16 = P.sb([128, 16], F32, "ss16")
            qn = [P.sb([128, D], BF16, "qn") for _ in range(2)]
            vb = [P.sb([128, D], BF16, "vb") for _ in range(2)]
            qT_s = [P.sb([128, 8, 128], BF16, "qTs") for _ in range(2)]
            n = 0
            for b in range(NB):
                for ti in range(NTILE):
                    t0 = ti * 128
                    r = b if t0 < SEQ else 2
                    xt = xts[n % 2]; vbt = vb[n % 2]; n += 1
                    S.dma("sp", xt[:, :], sc["xs"][b, t0:t0 + 128, :], reads=[sc["xs"]], writes=[xt])
                    self.norm_to_hT(xt, scp, shp, r, hT, 0, tmp)
                    for cg in range(6):
                        ps = psq[cg]
                        for j in range(8):
                            S.op("pe", lambda e, j=j, ps=ps, cg=cg: e.matmul(ps[:, :], hT[:, j, :], W[:, j, cg * 512:(cg + 1) * 512], start=(j == 0), stop=(j == 7)),
                                 reads=[hT, W], writes=[ps])
                    for which in range(2):
                        qnt = qn[which]
                        for half in range(2):
                            ps = psq[which * 2 + half]
                            S.op("act", lambda e, ps=ps, half=half: e.activation(sq[:, half * 512:(half + 1) * 512], ps[:, :], AF.Square), reads=[ps], writes=[sq])
                        S.op("dve", lambda e: e.tensor_reduce(ss16[:, :], sq.ap.rearrange("p (h d) -> p h d", h=16), AX.X, ALU.add), reads=[sq], writes=[ss16])
                        self.rsqrt(ss16, (slice(None), slice(None)), 1.0 / 64, EPS)
                        for half in range(2):
                            ps = psq[which * 2 + half]
                            S.op("dve", lambda e, ps=ps, half=half: e.tensor_tensor(sq[:, half * 512:(half + 1) * 512].rearrange("p (h d) -> p h d", h=8),
                                                                                    ps[:, :].rearrange("p (h d) -> p h d", h=8),
                                                                                    ss16[:, half * 8:(half + 1) * 8].unsqueeze(2).broadcast_to([128, 8, 64]), ALU.mult),
                                 reads=[ps, ss16], writes=[sq])
                        S.op("pool", lambda e, qnt=qnt, which=which: e.tensor_tensor(qnt.ap.rearrange("p (h d) -> p h d", h=16), sq.ap.rearrange("p (h d) -> p h d", h=16),
                                                                                    gn[:, which, :].unsqueeze(1).broadcast_to([128, 16, 64]), ALU.mult),
                             reads=[sq, gn], writes=[qnt])
                        for j in range(8):
                            S.op("pe", lambda e, j=j, qnt=qnt: e.transpose(pT2[:, j * 128:(j + 1) * 128], qnt[:, j * 128:(j + 1) * 128], self.ident_b[:, :]),
                                 reads=[qnt, self.ident_b], writes=[pT2])
                        qs = qT_s[which]
                        S.op("act", lambda e, qs=qs: e.copy(qs[:, :, :], pT2.ap.rearrange("p (j t) -> p j t", j=8)), reads=[pT2], writes=[qs])
                        dst = sc["oqT"] if which == 0 else sc["okT"]
                        S.dma("sp", dst[b, :, :, t0:t0 + 128].rearrange("h d t -> d h t"), qs[:, :, :], reads=[qs], writes=[dst])
                    S.op("dve", lambda e, vbt=vbt: e.tensor_copy(vbt[:, 0:512], psq[4][:, :]), reads=[psq[4]], writes=[vbt])
                    S.op("act", lambda e, vbt=vbt: e.copy(vbt[:, 512:1024], psq[5][:, :]), reads=[psq[5]], writes=[vbt])
                    S.dma("pool", sc["ov"][b, t0:t0 + 128, :, :].rearrange("p h d -> p (h d)"), vbt[:, :], reads=[vbt], writes=[sc["ov"]])

    def phase_natten(self, l, last=False):
        S, I, sc = self.S, self.inp, self.scr
        i = l // 2
        NR = SEQ // GRID_W
        with Phase(S, f"na{l}") as P:
            self.load_consts(P)
            cm = P.sb([64, 64], F32, "cm")
            S.dma("sp", cm[:, :], I["cmaskT"][:, :], writes=[cm])
            rp = P.sb([64, 2, 15, 64], F32, "rp")
            ebm = P.sb([64, 2, 15, 64], BF16, "ebm")
            qT = [P.sb([128, TT], BF16, "qT") for _ in range(2)]
            kT = [P.sb([128, TT], BF16, "kT") for _ in range(2)]
            v64 = [P.sb([64, 36, 2, 65], BF16, "v64") for _ in range(2)]
            psL = [P.ps([64, 8, 64], F32, "psL") for _ in range(2)]
            psX = [P.ps([64, 4, 64], F32, "psX") for _ in range(2)]
            psO = [P.ps([64, 65], F32, "psO") for _ in range(2)]
            pT = P.ps([128, 512], BF16, "pT")
            eL = [P.sb([64, 8, 64], BF16, "eL") for _ in range(2)]
            pL = [P.sb([64, 8, 64], BF16, "pL") for _ in range(2)]
            eX = [P.sb([64, 4, 64], BF16, "eX") for _ in range(2)]
            ao = [P.sb([64, 36, 128], BF16, "ao") for _ in range(2)]
            oT = [P.sb([128, 512], BF16, "oT") for _ in range(2)]
            rcs = [P.sb([64, 1], F32, "rc") for _ in range(2)]
            k = 0
            it = 0
            for hp in range(8):
                S.dma("sp", rp[:, :, :, :], I["rpbx"][i, 2 * hp:2 * hp + 2].rearrange("h k r q -> k h r q"), writes=[rp])
                S.op("act", lambda e: e.activation(rp[:, :, :, :], rp[:, :, :, :], AF.Exp), reads=[rp], writes=[rp])
                S.op("dve", lambda e: e.tensor_tensor(ebm.ap.rearrange("k h r q -> k (h r) q"), rp.ap.rearrange("k h r q -> k (h r) q"),
                                                     cm.ap.unsqueeze(1).broadcast_to([64, 30, 64]), ALU.mult), reads=[rp, cm], writes=[ebm])
                for b in range(NB):
                    q_, k_, v_, a_ = qT[it % 2], kT[it % 2], v64[it % 2], ao[it % 2]
                    it += 1
                    S.dma("sp", q_[:, :], sc["oqT"][b, hp, :, :], reads=[sc["oqT"]], writes=[q_])
                    S.dma("act", k_[:, :], sc["okT"][b, hp, :, :], reads=[sc["okT"]], writes=[k_])
                    S.op("pool", lambda e, v_=v_: e.memset(v_[:, :, :, :], 1.0), writes=[v_])
                    for hl_ in range(2):
                        S.dma("pool", v_[:, :, hl_, 0:64], sc["ov"][b, :, 2 * hp + hl_, :].rearrange("(n p) d -> p n d", p=64), reads=[sc["ov"]], writes=[v_])
                    nq = 32 if last else 36
                    for r in range(nq):
                        lat = r < NR
                        for hl in range(2):
                            a = k % 2; k += 1
                            hs = slice(hl * 64, (hl + 1) * 64)
                            pl_, px_, po_ = psL[a], psX[a], psO[a]
                            el_, pp_, ex_ = eL[a], pL[a], eX[a]
                            rc_ = rcs[a]
                            qs = slice(r * 64, (r + 1) * 64)
                            if lat:
                                rs = min(max(r - 4, 0), NR - 8)
                                for j in range(8):
                                    S.op("pe", lambda e, j=j: e.matmul(pl_[:, j, :], k_[hs, (rs + j) * 64:(rs + j + 1) * 64], q_[hs, qs], start=True, stop=True),
                                         reads=[k_, q_], writes=[pl_])
                            for j in range(4):
                                S.op("pe", lambda e, j=j: e.matmul(px_[:, j, :], k_[hs, SEQ + j * 64:SEQ + (j + 1) * 64], q_[hs, qs], start=True, stop=True),
                                     reads=[k_, q_], writes=[px_])
                            if lat:
                                S.op("act", lambda e: e.activation(el_[:, :, :], pl_[:, :, :], AF.Exp), reads=[pl_], writes=[el_])
                                d0 = rs - r + 7
                                S.op("dve", lambda e: e.tensor_tensor(pp_[:, :, :], el_[:, :, :], ebm[:, hl, d0:d0 + 8, :], ALU.mult), reads=[el_, ebm], writes=[pp_])
                            S.op("act", lambda e: e.activation(ex_[:, :, :], px_[:, :, :], AF.Exp), reads=[px_], writes=[ex_])
                            first = True
                            if lat:
                                for j in range(8):
                                    S.op("pe", lambda e, j=j, first=first: e.matmul(po_[:, :], pp_[:, j, :], v_[:, rs + j, hl, :], start=first, stop=False),
                                         reads=[pp_, v_], writes=[po_])
                                    first = False
                            for j in range(4):
                                S.op("pe", lambda e, j=j, first=first: e.matmul(po_[:, :], ex_[:, j, :], v_[:, 32 + j, hl, :], start=first, stop=(j == 3)),
                                     reads=[ex_, v_], writes=[po_])
                                first = False
                            S.op("dve", lambda e: e.reciprocal(rc_[:, :], po_[:, 64:65]), reads=[po_], writes=[rc_])
                            S.op("dve", lambda e: e.tensor_scalar(a_[:, r, hs], po_[:, 0:64], rc_[:, :], None, ALU.mult), reads=[po_, rc_], writes=[a_])
                    for g0 in range(0, nq, 8):
                        ng = min(8, nq - g0)
                        o_ = oT[(g0 // 8) % 2]
                        for j in range(ng):
                            S.op("pe", lambda e, j=j: e.transpose(pT[:, j * 64:(j + 1) * 64], a_[:, g0 + j, :], self.ident_b[0:64, 0:64]),
                                 reads=[a_, self.ident_b], writes=[pT])
                        S.op("act", lambda e: e.copy(o_[:, 0:ng * 64], pT[:, 0:ng * 64]), reads=[pT], writes=[o_])
                        S.dma("pool", sc["mixT"][b, hp * 128:(hp + 1) * 128, g0 * 64:(g0 + ng) * 64], o_[:, 0:ng * 64], reads=[o_], writes=[sc["mixT"]])

    def build(self):
        self.declare()
        stop = getattr(self, "stop_after", None)
        seq = [("init", self.phase_init), ("mod", self.phase_mod)]
        for l in range(self.depth):
            last = (l == self.depth - 1)
            if l % 2 == 0:
                seq += [(f"eproj{l}", lambda l=l: self.phase_even_proj(l)),
                        (f"filt{l}", lambda l=l: (self.phase_filters_pe(l) if self.hy_pe else self.phase_filters(l))),
                        (f"mlstm{l}", lambda l=l: self.phase_mlstm(l)),
                        (f"hyena{l}", lambda l=l: (self.phase_hyena_pe(l) if self.hy_pe else self.phase_hyena(l))),
                        (f"oproj{l}", lambda l=l, last=last: self.phase_outproj(l, self.inp["e_w_out"][l // 2], last))]
            else:
                seq += [(f"qproj{l}", lambda l=l: self.phase_odd_proj(l)),
                        (f"natten{l}", lambda l=l, last=last: self.phase_natten(l, last)),
                        (f"oproj{l}", lambda l=l, last=last: self.phase_outproj(l, self.inp["o_w_out"][l // 2], last))]
            seq.append((f"mlp{l}", lambda l=l, last=last: self.phase_mlp(l, last)))
        for name, fn in seq:
            if self.only is not None and name not in self.only:
                continue
            fn()
            if stop == name:
                break
        self.S.barrier()
        return self.nc


_CONSTS = None


def make_in_maps(inputs, needed=None):
    global _CONSTS
    if _CONSTS is None:
        _CONSTS = host_consts()
    f = lambda a: np.ascontiguousarray(np.asarray(a, dtype=np.float32))
    shared = {k: f(inputs[k]) for k in ("w_mod", "b_mod", "w_mlp_in", "w_mlp_out", "e_w_in", "e_gate_b", "e_hnorm", "e_conv_w", "e_conv_b",
                                        "e_f_w1", "e_f_b1", "e_f_w2", "e_f_b2", "e_f_w3", "e_f_freq", "e_hy_d", "e_w_out", "o_w_qkv", "o_qn", "o_kn", "o_w_out")}
    shared["rpbx"] = rpb_expand(f(inputs["o_rpb"]))
    shared.update(_CONSTS)
    x = f(inputs["x"]); c = f(inputs["c"]); ctx = f(inputs["ctx"]); cc = f(inputs["c_ctx"])
    maps = []
    for core in range(NCORES):
        m = dict(shared)
        b0 = core * NB
        m["x"] = np.ascontiguousarray(x[b0:b0 + NB])
        m["ctx"] = np.ascontiguousarray(ctx[b0:b0 + NB])
        c3 = np.stack([c[b0], c[b0 + 1], cc], 0)
        m["cT"] = np.ascontiguousarray(c3.reshape(3, 8, 128).transpose(2, 1, 0))
        if needed is not None:
            m = {k: v for k, v in m.items() if k in needed}
        maps.append(m)
    return maps


def kernel(**inputs):
    bld = Builder(depth=DEPTH)
    nc = bld.build()
    maps = make_in_maps(inputs, set(bld.inp.keys()))
    res = run_bass_kernel_spmd(nc, maps, core_ids=list(range(NCORES)))
    out = np.concatenate([np.asarray(r["out"], dtype=np.float32) for r in res.results], axis=0)
    return out
```

```python
import os
import math
import numpy as np
import ml_dtypes
from contextlib import ExitStack
import concourse.bass as bass
import concourse.mybir as mybir
from concourse.bass_utils import run_bass_kernel_spmd

F32 = mybir.dt.float32
BF16 = mybir.dt.bfloat16
ALU = mybir.AluOpType
AF = mybir.ActivationFunctionType
AX = mybir.AxisListType

NCORES = 8
NB = 2
D = 1024
SEQ = 2048
CTX = 256
TT = SEQ + CTX
NTILE = TT // 128
DEPTH = 4
DFF = 4096
EPS = 1e-6
E_IN = 3600
GRID_W = 64
PI = math.pi


class T:
    __slots__ = ("ap", "lw", "rd", "name")

    def __init__(self, ap, name=""):
        self.ap = ap
        self.lw = None
        self.rd = []
        self.name = name

    def __getitem__(self, idx):
        return self.ap[idx]


class Sched:
    NDMA = 10

    def __init__(self, nc):
        self.nc = nc
        self.es = ExitStack()
        self.engs = {"pe": nc.tensor, "act": nc.scalar, "dve": nc.vector, "pool": nc.gpsimd, "sp": nc.sync}
        self.sem = {}
        self.cnt = {}
        for k in self.engs:
            self.sem[k] = self.es.enter_context(nc.semaphore("s_" + k))
            self.cnt[k] = 0
        self.dq = {}
        for q in ("sp", "act", "pool"):
            sems = [self.es.enter_context(nc.semaphore(f"d_{q}{i}")) for i in range(self.NDMA)]
            self.dq[q] = {"sems": sems, "n": 0}
        self.known = {k: {} for k in self.engs}
        self.n_wait = 0
        self.n_inst = 0
        self.pe_pend = None
        self.pe_pend_w = None

    def _flush_pe(self):
        if self.pe_pend is not None:
            self.cnt["pe"] += 1
            self.pe_pend.then_inc(self.sem["pe"], 1)
            self.pe_pend = None
            self.pe_pend_w = None

    def _wait(self, e, tk):
        if tk is None:
            return
        key, sem, val, src = tk
        if src == e and e == "pe":
            return
        if src == "pe" and val > self.cnt["pe"]:
            self._flush_pe()
        kn = self.known[e]
        if kn.get(key, 0) >= val:
            return
        kn[key] = val
        self.engs[e].wait_ge(sem, val)
        self.n_wait += 1

    def _deps(self, e, reads, writes):
        for b in reads:
            self._wait(e, b.lw)
        for b in writes:
            self._wait(e, b.lw)
            for t in b.rd:
                self._wait(e, t)

    def _commit(self, tk, reads, writes):
        for b in reads:
            b.rd.append(tk)
            if len(b.rd) > 48:
                best = {}
                for t in b.rd:
                    if t[0] not in best or best[t[0]][2] < t[2]:
                        best[t[0]] = t
                b.rd = list(best.values())
        for b in writes:
            b.lw = tk
            b.rd = []

    def op(self, e, fn, reads=(), writes=()):
        self._deps(e, reads, writes)
        if e == "pe":
            w0 = (id(writes[0]) if writes else None, tuple(id(r) for r in reads))
            if self.pe_pend is not None and self.pe_pend_w != w0:
                self._flush_pe()
            ins = fn(self.engs[e])
            self.pe_pend = ins
            self.pe_pend_w = w0
            tk = (e, self.sem[e], self.cnt[e] + 1, e)
        else:
            ins = fn(self.engs[e])
            self.cnt[e] += 1
            ins.then_inc(self.sem[e], 1)
            tk = (e, self.sem[e], self.cnt[e], e)
        self._commit(tk, reads, writes)
        self.n_inst += 1
        return tk

    def dma(self, q, out, in_, reads=(), writes=(), **kw):
        d = self.dq[q]
        j = d["n"]
        i = j % self.NDMA
        rnd = j // self.NDMA
        sem = d["sems"][i]
        key = f"d_{q}{i}"
        if rnd > 0:
            self._wait(q, (key, sem, 16 * rnd, None))
        self._deps(q, reads, writes)
        ins = self.engs[q].dma_start(out=out, in_=in_, **kw)
        ins.then_inc(sem, 16)
        d["n"] = j + 1
        tk = (key, sem, 16 * (rnd + 1), None)
        self._commit(tk, reads, writes)
        self.n_inst += 1
        return tk

    def all_tickets(self):
        self._flush_pe()
        tks = []
        for k in self.engs:
            if self.cnt[k] > 0:
                tks.append((k, self.sem[k], self.cnt[k], k))
        for q, d in self.dq.items():
            j = d["n"]
            for i in range(min(j, self.NDMA)):
                last = ((j - 1 - i) // self.NDMA) * self.NDMA + i
                tks.append((f"d_{q}{i}", d["sems"][i], 16 * (last // self.NDMA + 1), None))
        return tks

    def barrier(self, engines=None):
        tks = self.all_tickets()
        for e in (engines or self.engs):
            for tk in tks:
                if tk[3] == e:
                    continue
                self._wait(e, tk)


class Phase:
    def __init__(self, S, name):
        self.S = S
        self.nc = S.nc
        self.name = name
        self.es = ExitStack()
        self.k = 0

    def __enter__(self):
        return self

    def sb(self, shape, dt=F32, name=None):
        self.k += 1
        h = self.es.enter_context(self.nc.sbuf_tensor(f"{self.name}_{name or 't'}{self.k}", list(shape), dt))
        return T(h.ap() if hasattr(h, "ap") and callable(h.ap) else h, name or "")

    def ps(self, shape, dt=F32, name=None):
        self.k += 1
        h = self.es.enter_context(self.nc.psum_tensor(f"{self.name}_{name or 'p'}{self.k}", list(shape), dt))
        return T(h.ap() if hasattr(h, "ap") and callable(h.ap) else h, name or "")

    def __exit__(self, *a):
        self.S.barrier()
        self.es.close()
        return False


def _bf(a):
    return np.asarray(a, np.float32).astype(ml_dtypes.bfloat16)


def host_consts():
    c = {}
    c["ident_b"] = _bf(np.eye(128))
    c["ident_f"] = np.eye(128, dtype=np.float32)
    c["jrev_b"] = _bf(np.eye(128)[::-1])
    s = np.arange(128)
    mf = (s[:, None] <= s[None, :]).astype(np.float32)
    mb = (s[:, None] >= s[None, :]).astype(np.float32)
    c["tri_f"] = np.stack([mf, mb, np.ones((128, 128), np.float32)], 0)
    c["tri_b"] = _bf(np.stack([mf, mb], 0))
    t = np.arange(SEQ)
    rows, cols = t // GRID_W, t % GRID_W
    nf = 32
    inv = (10000.0 ** (-np.arange(nf, dtype=np.float32) / nf)).astype(np.float32)
    ang_r = rows.astype(np.float32)[:, None] * inv[None, :]
    ang_c = cols.astype(np.float32)[:, None] * inv[None, :]
    cosT = np.concatenate([np.cos(ang_r), np.cos(ang_c)], 1).astype(np.float32)
    sinT = np.concatenate([np.sin(ang_r), np.sin(ang_c)], 1).astype(np.float32)
    ks = np.float32(128 ** -0.5)
    c["rope"] = np.stack([cosT, sinT, cosT * ks, sinT * ks], 0).astype(np.float32)
    for nm, L in (("L", SEQ), ("C", CTX)):
        tt = np.linspace(0.0, 1.0, L, dtype=np.float32)[:, None]
        bands = 16
        wpos = (2.0 * np.pi * np.arange(L, dtype=np.float32) / L).astype(np.float32)
        fr = np.linspace(1e-4, bands - 1, bands, dtype=np.float32)
        ang = wpos[:, None] * fr[None, :]
        emb = np.concatenate([tt, np.cos(ang), -np.sin(ang)], -1).astype(np.float32)
        c["embT_" + nm] = np.ascontiguousarray(emb.T)
        deltas = np.abs(np.linspace(math.log(1e-2) / 1.5, math.log(1e-2) / 0.3, 512, dtype=np.float32))
        c["dec_" + nm] = np.ascontiguousarray(np.exp(-tt * deltas[None, :]).T.astype(np.float32))
        c["embTr_" + nm] = np.ascontiguousarray(c["embT_" + nm][:, ::-1])
        c["decr_" + nm] = np.ascontiguousarray(c["dec_" + nm][:, ::-1])
    cidx = np.arange(GRID_W)
    cstart = np.clip(cidx - 8, 0, GRID_W - 16)
    cmask = (cidx[None, :] >= cstart[:, None]) & (cidx[None, :] < cstart[:, None] + 16)
    c["cmaskT"] = np.ascontiguousarray(cmask.T.astype(np.float32))
    return c


def rpb_expand(o_rpb):
    cidx = np.arange(GRID_W)
    dc = np.clip(cidx[:, None] - cidx[None, :] + 15, 0, 30)
    r = o_rpb[:, :, :, dc]
    return np.ascontiguousarray(r.transpose(0, 1, 3, 2, 4))


class Builder:
    def __init__(self, depth=DEPTH, debug=False, hy_pool_frac=0.0, hy_pe=True, hy_kb=64):
        self.depth = depth
        self.debug = debug
        self.hy_pool_frac = hy_pool_frac
        self.hy_pe = hy_pe
        self.hy_kb = hy_kb
        self.nc = bass.Bass("TRN2", target_bir_lowering=False)
        self.S = Sched(self.nc)
        self.inp = {}
        self.scr = {}
        self.feed = set()
        self.only = None

    IN_SHAPES = {
        "x": ([NB, SEQ, D], F32), "ctx": ([NB, CTX, D], F32), "cT": ([128, 8, 3], F32),
        "w_mod": ([DEPTH, D, 6 * D], F32), "b_mod": ([DEPTH, 6 * D], F32),
        "w_mlp_in": ([DEPTH, D, DFF], F32), "w_mlp_out": ([DEPTH, DFF, D], F32),
        "e_w_in": ([2, D, E_IN], F32), "e_gate_b": ([2, 16], F32), "e_hnorm": ([2, 512], F32),
        "e_conv_w": ([2, 3, 1536], F32), "e_conv_b": ([2, 1536], F32),
        "e_f_w1": ([2, 33, 64], F32), "e_f_b1": ([2, 64], F32), "e_f_w2": ([2, 64, 64], F32), "e_f_b2": ([2, 64], F32),
        "e_f_w3": ([2, 64, 2048], F32), "e_f_freq": ([2, 64], F32), "e_hy_d": ([2, 2, 512], F32),
        "e_w_out": ([2, D, D], F32), "o_w_qkv": ([2, D, 3 * D], F32), "o_qn": ([2, 64], F32), "o_kn": ([2, 64], F32),
        "rpbx": ([2, 16, 64, 15, 64], F32), "o_w_out": ([2, D, D], F32),
        "jrev_b": ([128, 128], BF16), "ident_b": ([128, 128], BF16), "ident_f": ([128, 128], F32), "tri_f": ([3, 128, 128], F32),
        "tri_b": ([2, 128, 128], BF16), "rope": ([4, SEQ, 64], F32),
        "embT_L": ([33, SEQ], F32), "embT_C": ([33, CTX], F32), "dec_L": ([512, SEQ], F32), "dec_C": ([512, CTX], F32),
        "cmaskT": ([64, 64], F32),
        "embTr_L": ([33, SEQ], F32), "embTr_C": ([33, CTX], F32), "decr_L": ([512, SEQ], F32), "decr_C": ([512, CTX], F32),
    }
    SCR_SHAPES = {
        "xs": ([NB, TT, D], F32), "modrow": ([DEPTH, 3, 6 * D], F32),
        "qT": ([NB, 4, 128, TT], BF16), "kT": ([NB, 4, 128, TT], BF16),
        "ktm": ([NB, TT, 512], BF16), "vtm": ([NB, TT, 512], BF16), "osig": ([NB, TT, 512], BF16),
        "gates": ([NB, TT, 16], F32), "hy": ([NB, 1536, TT], F32), "mixT": ([NB, D, TT], BF16),
        "filtL": ([2, 2, 512, SEQ], F32), "filtC": ([2, 2, 512, CTX], F32),
        "gpL": ([2, 512, 2 * SEQ], BF16), "gpC": ([2, 512, 2 * CTX], BF16),
        "oqT": ([NB, 8, 128, TT], BF16), "okT": ([NB, 8, 128, TT], BF16), "ov": ([NB, TT, 16, 64], BF16),
    }

    def declare(self):
        bld = self

        class LazyIn(dict):
            def __missing__(s, name):
                shape, dt = bld.IN_SHAPES[name]
                s[name] = bld.nc.dram_tensor(name, list(shape), dt, kind="ExternalInput").ap()
                return s[name]

        class LazyScr(dict):
            def __missing__(s, name):
                shape, dt = bld.SCR_SHAPES[name]
                if name in bld.feed:
                    kind = "ExternalInput"
                else:
                    kind = "ExternalOutput" if bld.debug else "Internal"
                s[name] = T(bld.nc.dram_tensor(name, list(shape), dt, kind=kind).ap(), name)
                return s[name]

        self.inp = LazyIn()
        self.scr = LazyScr()
        self.out = self.nc.dram_tensor("out", [NB, SEQ, D], F32, kind="ExternalOutput").ap()

    def load_consts(self, P):
        S, I = self.S, self.inp
        self.ident_b = P.sb([128, 128], BF16, "identb")
        S.dma("sp", self.ident_b[:, :], I["ident_b"][:, :], writes=[self.ident_b])

    def phase_init(self):
        S, I = self.S, self.inp
        xs = self.scr["xs"]
        for b in range(NB):
            S.dma("sp", xs[b, 0:SEQ, :], I["x"][b, :, :], writes=[xs])
            S.dma("pool", xs[b, SEQ:TT, :], I["ctx"][b, :, :], writes=[xs])
        S.barrier()

    def phase_mod(self):
        S, I = self.S, self.inp
        modrow = self.scr["modrow"]
        with Phase(S, "mod") as P:
            sc = P.sb([128, 8, 3], F32, "sc")
            S.dma("sp", sc[:, :, :], I["cT"][:, :, :], writes=[sc])
            S.op("act", lambda e: e.activation(sc[:, :, :], sc[:, :, :], AF.Silu), reads=[sc], writes=[sc])
            wst = [P.sb([128, 8, 512], F32, "wst") for _ in range(2)]
            pss = [P.ps([3, 512], F32, "ps") for _ in range(2)]
            bias = P.sb([3, 6 * D], F32, "bias")
            msb = P.sb([3, 6 * D], F32, "msb")
            k = 0
            for l in range(self.depth):
                S.dma("pool", bias[:, :], I["b_mod"][l:l + 1, :].partition_broadcast(3), writes=[bias])
                wv = I["w_mod"][l].rearrange("(j p) n -> p j n", p=128)
                for cg in range(12):
                    w = wst[k % 2]; ps = pss[k % 2]; k += 1
                    S.dma("sp" if cg % 2 == 0 else "act", w[:, :, :], wv[:, :, cg * 512:(cg + 1) * 512], writes=[w])
                    for j in range(8):
                        S.op("pe", lambda e, j=j, w=w, ps=ps: e.matmul(ps[:, :], sc[:, j, :], w[:, j, :], start=(j == 0), stop=(j == 7)),
                             reads=[sc, w], writes=[ps])
                    S.op("dve", lambda e, ps=ps, cg=cg: e.tensor_tensor(msb[:, cg * 512:(cg + 1) * 512], ps[:, :], bias[:, cg * 512:(cg + 1) * 512], ALU.add),
                         reads=[ps, bias], writes=[msb])
                S.dma("sp", modrow[l, :, :], msb[:, :], reads=[msb], writes=[modrow])

    def load_mod_fm(self, P, l, which):
        S = self.S
        modrow = self.scr["modrow"]
        off_sh = (0 if which == 0 else 3) * D
        off_sc = off_sh + D
        scp = P.sb([128, 3, 8], F32, "scp")
        shp = P.sb([128, 3, 8], F32, "shp")
        for r in range(3):
            S.dma("pool", scp[:, r, :], modrow[l, r, off_sc:off_sc + D].rearrange("(j p) -> p j", p=128),
                  reads=[modrow], writes=[scp], allow_slow_non_contiguous=True)
            S.dma("pool", shp[:, r, :], modrow[l, r, off_sh:off_sh + D].rearrange("(j p) -> p j", p=128),
                  reads=[modrow], writes=[shp], allow_slow_non_contiguous=True)
        S.op("dve", lambda e: e.tensor_scalar(scp[:, :, :], scp[:, :, :], 1.0, None, ALU.add), reads=[scp], writes=[scp])
        return scp, shp

    def load_gate_bc(self, P, l, which):
        S = self.S
        modrow = self.scr["modrow"]
        off = (2 if which == 0 else 5) * D
        g = P.sb([128, 3, D], F32, "gbc")
        for r in range(3):
            S.dma("pool", g[:, r, :], modrow[l, r:r + 1, off:off + D].partition_broadcast(128), reads=[modrow], writes=[g])
        return g

    def load_weight_bf16(self, P, w_ap, K, N, name, stage, cw=512):
        S = self.S
        kc = K // 128
        wb = P.sb([128, kc, N], BF16, name)
        wv = w_ap.rearrange("(j p) n -> p j n", p=128)
        i = 0
        for c0 in range(0, N, cw):
            c1 = min(N, c0 + cw)
            for j0 in range(0, kc, 8):
                st = stage[i % len(stage)]
                q = ("sp", "act")[i % 2]
                S.dma(q, st[:, 0:8, 0:c1 - c0], wv[:, j0:j0 + 8, c0:c1], writes=[st])
                eng = ("pool", "dve")[i % 2] if (i % 4 != 3) else "act"
                if eng == "act":
                    S.op("act", lambda e, st=st, j0=j0, c0=c0, c1=c1: e.copy(wb[:, j0:j0 + 8, c0:c1], st[:, 0:8, 0:c1 - c0]), reads=[st], writes=[wb])
                else:
                    S.op(eng, lambda e, st=st, j0=j0, c0=c0, c1=c1: e.tensor_copy(wb[:, j0:j0 + 8, c0:c1], st[:, 0:8, 0:c1 - c0]), reads=[st], writes=[wb])
                i += 1
        return wb

    def rsqrt(self, t, sl, mul, add):
        S = self.S
        S.op("dve", lambda e: e.tensor_scalar(t.ap[sl], t.ap[sl], float(mul), float(add), ALU.mult, ALU.add), reads=[t], writes=[t])
        S.op("act", lambda e: e.activation(t.ap[sl], t.ap[sl], AF.Sqrt), reads=[t], writes=[t])
        S.op("dve", lambda e: e.reciprocal(t.ap[sl], t.ap[sl]), reads=[t], writes=[t])

    def norm_to_hT(self, xt, scp, shp, r, hT, col, tmp):
        S = self.S
        junk, ss, rstd, xn, pT = tmp["junk"], tmp["ss"], tmp["rstd"], tmp["xn"], tmp["pT"]
        S.op("dve", lambda e: e.memset(ss[:, :], 0.0), writes=[ss])
        S.op("act", lambda e: e.activation(junk[:, :], xt[:, :], AF.Square, accum_out=ss[:, :]), reads=[xt, ss], writes=[junk, ss])
        S.op("dve", lambda e: e.tensor_copy(rstd[:, :], ss[:, :]), reads=[ss], writes=[rstd])
        self.rsqrt(rstd, (slice(None), slice(None)), 1.0 / D, EPS)
        S.op("act", lambda e: e.activation(xn[:, :], xt[:, :], AF.Copy, scale=rstd[:, :]), reads=[xt, rstd], writes=[xn])
        for j in range(8):
            S.op("pe", lambda e, j=j: e.transpose(pT[:, j * 128:(j + 1) * 128], xn[:, j * 128:(j + 1) * 128], self.ident_b[:, :]),
                 reads=[xn, self.ident_b], writes=[pT])
        for j in range(8):
            if j % 2 == 0:
                S.op("dve", lambda e, j=j: e.tensor_scalar(hT[:, j, col * 128:(col + 1) * 128], pT[:, j * 128:(j + 1) * 128],
                                                            scp[:, r, j:j + 1], shp[:, r, j:j + 1], ALU.mult, ALU.add),
                     reads=[pT, scp, shp], writes=[hT])
            else:
                S.op("act", lambda e, j=j: e.activation(hT[:, j, col * 128:(col + 1) * 128], pT[:, j * 128:(j + 1) * 128], AF.Identity,
                                                         bias=shp[:, r, j:j + 1], scale=scp[:, r, j:j + 1]),
                     reads=[pT, scp, shp], writes=[hT])

    def norm_tmp(self, P):
        return {"junk": P.sb([128, D], BF16, "junk"), "ss": P.sb([128, 1], F32, "ss"), "rstd": P.sb([128, 1], F32, "rstd"),
                "xn": P.sb([128, D], BF16, "xn"), "pT": P.ps([128, D], BF16, "pT")}

    def groups(self, with_ctx=True):
        g = []
        for b in range(NB):
            for k in range(4):
                g.append((b, k * 512, 4, b))
            if with_ctx:
                g.append((b, SEQ, 2, 2))
        return g

    def phase_even_proj(self, l):
        S, I, sc = self.S, self.inp, self.scr
        i = l // 2
        with Phase(S, f"ep{l}") as P:
            self.load_consts(P)
            scp, shp = self.load_mod_fm(P, l, 0)
            stage = [P.sb([128, 8, 512], F32, "stg") for _ in range(2)]
            W = self.load_weight_bf16(P, I["e_w_in"][i], D, E_IN, "Win", stage)
            gb = P.sb([128, 16], F32, "gb")
            S.dma("sp", gb[:, :], I["e_gate_b"][i:i + 1, :].partition_broadcast(128), writes=[gb])
            tmp = self.norm_tmp(P)
            xts = [P.sb([128, D], F32, "xt") for _ in range(2)]
            hTs = [P.sb([128, 8, 512], BF16, "hT") for _ in range(2)]
            ropes = [P.sb([128, 4, 64], F32, "rope") for _ in range(2)]
            psq = [P.ps([128, 512], F32, "psq") for _ in range(4)]
            psg = P.ps([128, 16], F32, "psg")
            pT2 = P.ps([128, 1024], BF16, "pT2")
            qf = P.sb([128, 512], F32, "qf")
            t1 = P.sb([128, 256], F32, "t1"); t2 = P.sb([128, 256], F32, "t2")
            qk_b = [P.sb([128, 1024], BF16, "qkb") for _ in range(2)]
            vo_b = [P.sb([128, 1024], BF16, "vob") for _ in range(2)]
            gsb = [P.sb([128, 16], F32, "gsb") for _ in range(2)]
            qkT = [P.sb([128, 8, 128], BF16, "qkT") for _ in range(2)]
            hyo = [P.sb([128, 512], F32, "hyo") for _ in range(2)]
            n = 0
            nh = 0
            for gi, (b, tok0, nt, r) in enumerate(self.groups()):
                hT = hTs[gi % 2]
                for ti in range(nt):
                    t0 = tok0 + ti * 128
                    xt = xts[n % 2]; rp = ropes[n % 2]; qkb = qk_b[n % 2]; vob = vo_b[n % 2]; gs = gsb[n % 2]; qT_s = qkT[n % 2]
                    n += 1
                    S.dma("sp", xt[:, :], sc["xs"][b, t0:t0 + 128, :], reads=[sc["xs"]], writes=[xt])
                    latent = tok0 < SEQ
                    if latent:
                        S.dma("pool", rp[:, :, :], I["rope"][:, t0:t0 + 128, :].rearrange("k p c -> p k c"), writes=[rp])
                    self.norm_to_hT(xt, scp, shp, r, hT, ti, tmp)
                    for cgp in range(4):
                        ps = psq[cgp]
                        for j in range(8):
                            S.op("pe", lambda e, j=j, ps=ps, cgp=cgp: e.matmul(ps[:, :], hT[:, j, ti * 128:(ti + 1) * 128], W[:, j, cgp * 512:(cgp + 1) * 512],
                                                                               start=(j == 0), stop=(j == 7)), reads=[hT, W], writes=[ps])
                    for j in range(8):
                        S.op("pe", lambda e, j=j: e.matmul(psg[:, :], hT[:, j, ti * 128:(ti + 1) * 128], W[:, j, 2048:2064], start=(j == 0), stop=(j == 7)),
                             reads=[hT, W], writes=[psg])
                    for which in range(2):
                        ps = psq[which]
                        dst = qkb
                        dc0 = which * 512
                        if latent:
                            S.op("act", lambda e, ps=ps: e.copy(qf[:, :], ps[:, :]), reads=[ps], writes=[qf])
                            qv = qf.ap.rearrange("p (h a b c) -> p h a b c", h=4, a=2, b=2)
                            dv = dst.ap[:, dc0:dc0 + 512].rearrange("p (h a b c) -> p h a b c", h=4, a=2, b=2)
                            cos = rp.ap[:, 2 * which + 0, :].rearrange("p (a c) -> p a c", a=2).unsqueeze(1).broadcast_to([128, 4, 2, 32])
                            sin = rp.ap[:, 2 * which + 1, :].rearrange("p (a c) -> p a c", a=2).unsqueeze(1).broadcast_to([128, 4, 2, 32])
                            t1v = t1.ap.rearrange("p (h a c) -> p h a c", h=4, a=2)
                            t2v = t2.ap.rearrange("p (h a c) -> p h a c", h=4, a=2)
                            x1 = qv[:, :, :, 0, :]; x2 = qv[:, :, :, 1, :]
                            S.op("dve", lambda e: e.tensor_tensor(t1v, x1, cos, ALU.mult), reads=[qf, rp], writes=[t1])
                            S.op("pool", lambda e: e.tensor_tensor(t2v, x2, sin, ALU.mult), reads=[qf, rp], writes=[t2])
                            S.op("dve", lambda e: e.tensor_tensor(dv[:, :, :, 0, :], t1v, t2v, ALU.subtract), reads=[t1, t2], writes=[dst])
                            S.op("dve", lambda e: e.tensor_tensor(t1v, x1, sin, ALU.mult), reads=[qf, rp], writes=[t1])
                            S.op("pool", lambda e: e.tensor_tensor(t2v, x2, cos, ALU.mult), reads=[qf, rp], writes=[t2])
                            S.op("dve", lambda e: e.tensor_tensor(dv[:, :, :, 1, :], t1v, t2v, ALU.add), reads=[t1, t2], writes=[dst])
                        else:
                            scale = 1.0 if which == 0 else float(np.float32(128 ** -0.5))
                            S.op("act", lambda e, ps=ps: e.mul(dst[:, dc0:dc0 + 512], ps[:, :], scale), reads=[ps], writes=[dst])
                    for j in range(8):
                        S.op("pe", lambda e, j=j: e.transpose(pT2[:, j * 128:(j + 1) * 128], qkb[:, j * 128:(j + 1) * 128], self.ident_b[:, :]),
                             reads=[qkb, self.ident_b], writes=[pT2])
                    S.op("dve", lambda e: e.tensor_copy(qT_s[:, 0:4, :], pT2[:, 0:512].rearrange("p (j t) -> p j t", j=4)), reads=[pT2], writes=[qT_s])
                    S.op("act", lambda e: e.copy(qT_s[:, 4:8, :], pT2[:, 512:1024].rearrange("p (j t) -> p j t", j=4)), reads=[pT2], writes=[qT_s])
                    S.dma("sp", sc["qT"][b, :, :, t0:t0 + 128].rearrange("h d t -> d h t"), qT_s[:, 0:4, :], reads=[qT_s], writes=[sc["qT"]])
                    S.dma("sp", sc["kT"][b, :, :, t0:t0 + 128].rearrange("h d t -> d h t"), qT_s[:, 4:8, :], reads=[qT_s], writes=[sc["kT"]])
                    S.dma("pool", sc["ktm"][b, t0:t0 + 128, :], qkb[:, 512:1024], reads=[qkb], writes=[sc["ktm"]])
                    S.op("dve", lambda e: e.tensor_copy(vob[:, 0:512], psq[2][:, :]), reads=[psq[2]], writes=[vob])
                    S.op("act", lambda e: e.activation(vob[:, 512:1024], psq[3][:, :], AF.Sigmoid), reads=[psq[3]], writes=[vob])
                    S.op("dve", lambda e: e.tensor_tensor(gs[:, :], psg[:, :], gb[:, :], ALU.add), reads=[psg, gb], writes=[gs])
                    S.dma("pool", sc["vtm"][b, t0:t0 + 128, :], vob[:, 0:512], reads=[vob], writes=[sc["vtm"]])
                    S.dma("pool", sc["osig"][b, t0:t0 + 128, :], vob[:, 512:1024], reads=[vob], writes=[sc["osig"]])
                    S.dma("pool", sc["gates"][b, t0:t0 + 128, :], gs[:, :], reads=[gs], writes=[sc["gates"]])
                ntok = nt * 128
                for cc in range(12):
                    ps = psq[cc % 4]
                    ho = hyo[nh % 2]; nh += 1
                    for j in range(8):
                        S.op("pe", lambda e, j=j, ps=ps, cc=cc: e.matmul(ps[:, 0:ntok], W[:, j, 2064 + cc * 128:2064 + (cc + 1) * 128], hT[:, j, 0:ntok],
                                                                         start=(j == 0), stop=(j == 7)), reads=[hT, W], writes=[ps])
                    if cc % 2 == 0:
                        S.op("act", lambda e, ps=ps, ho=ho: e.copy(ho[:, 0:ntok], ps[:, 0:ntok]), reads=[ps], writes=[ho])
                    else:
                        S.op("dve", lambda e, ps=ps, ho=ho: e.tensor_copy(ho[:, 0:ntok], ps[:, 0:ntok]), reads=[ps], writes=[ho])
                    S.dma("act", sc["hy"][b, cc * 128:(cc + 1) * 128, tok0:tok0 + ntok], ho[:, 0:ntok], reads=[ho], writes=[sc["hy"]])

    def phase_mlstm(self, l):
        S, I, sc = self.S, self.inp, self.scr
        i = l // 2
        with Phase(S, f"ml{l}") as P:
            self.load_consts(P)
            trif = P.sb([128, 3, 128], F32, "trif")
            trib = P.sb([128, 2, 128], BF16, "trib")
            S.dma("sp", trif[:, :, :], I["tri_f"].rearrange("k p c -> p k c"), writes=[trif])
            S.dma("sp", trib[:, :, :], I["tri_b"].rearrange("k p c -> p k c"), writes=[trib])
            hn = P.sb([128, 512], F32, "hn")
            S.dma("pool", hn[:, :], I["e_hnorm"][i:i + 1, :].partition_broadcast(128), writes=[hn])
            qT = P.sb([128, 4, TT], BF16, "qT"); kT = P.sb([128, 4, TT], BF16, "kT")
            ktm = P.sb([128, NTILE, 512], BF16, "ktm")
            v1 = P.sb([128, NTILE, 4, 129], BF16, "v1")
            G = P.sb([128, NTILE, 16], F32, "G")
            hm = P.sb([128, NTILE, 512], F32, "hm")
            lf = P.sb([128, NTILE, 8], F32, "lf")
            cum = P.sb([128, NTILE, 8], F32, "cum")
            es = P.sb([128, NTILE, 8], F32, "es"); eb = P.sb([128, NTILE, 8], F32, "eb"); dec = P.sb([128, NTILE, 8], F32, "dec")
            psG = P.ps([128, 16], F32, "psG")
            NSET = 4
            psS = [P.ps([128, 3, 132], F32, "psS") for _ in range(NSET)]
            psA = [T(p.ap[:, 0, 0:128], "psA") for p in psS]
            psB = [T(p.ap[:, 1, 0:129], "psB") for p in psS]
            psC = [T(p.ap[:, 2, 0:129], "psC") for p in psS]
            pTo = P.ps([128, 512], BF16, "pTo")
            Sf = [P.sb([128, 129], F32, "Sf") for _ in range(8)]
            Sb = [P.sb([128, 129], BF16, "Sb") for _ in range(8)]
            PT = [P.sb([128, 128], BF16, "PT") for _ in range(4)]
            vE = [P.sb([128, 129], BF16, "vE") for _ in range(4)]
            tmpS = [P.sb([128, 129], F32, "tmpS") for _ in range(4)]
            ne = [P.sb([128, 129], F32, "ne") for _ in range(4)]
            dn = [P.sb([128, 1], F32, "dn") for _ in range(4)]
            sq = P.sb([128, 512], F32, "sq"); ss4 = P.sb([128, 4], F32, "ss4"); rs4 = P.sb([128, 4], F32, "rs4")
            osg = [P.sb([128, 512], BF16, "osg") for _ in range(2)]
            ab = [P.sb([128, 512], BF16, "ab") for _ in range(2)]
            aT = [P.sb([128, 4, 128], BF16, "aT") for _ in range(2)]
            for b in range(NB):
                S.dma("sp", qT[:, :, :], sc["qT"][b].rearrange("h d t -> d h t"), reads=[sc["qT"]], writes=[qT])
                S.dma("act", kT[:, :, :], sc["kT"][b].rearrange("h d t -> d h t"), reads=[sc["kT"]], writes=[kT])
                S.dma("sp", ktm[:, :, :], sc["ktm"][b].rearrange("(n p) c -> p n c", p=128), reads=[sc["ktm"]], writes=[ktm])
                S.op("pool", lambda e: e.memset(v1[:, :, :, :], 1.0), writes=[v1])
                for h in range(4):
                    S.dma("pool", v1[:, :, h, 0:128], sc["vtm"][b, :, h * 128:(h + 1) * 128].rearrange("(n p) c -> p n c", p=128), reads=[sc["vtm"]], writes=[v1])
                S.dma("sp", G[:, :, :], sc["gates"][b].rearrange("(n p) c -> p n c", p=128), reads=[sc["gates"]], writes=[G])
                S.op("pool", lambda e: e.memset(hm[:, :, :], 0.0), writes=[hm])
                Gv = G.ap.rearrange("p n (k h) -> p n k h", k=4)
                lfv = lf.ap.rearrange("p n (k h) -> p n k h", k=2)
                S.op("act", lambda e: e.activation(lfv, Gv[:, :, 1::2, :], AF.Exp, scale=-1.0), reads=[G], writes=[lf])
                S.op("act", lambda e: e.activation(lf[:, :, :], lf[:, :, :], AF.Ln, bias=1.0), reads=[lf], writes=[lf])
                S.op("dve", lambda e: e.tensor_scalar(lf[:, :, :], lf[:, :, :], -1.0, None, ALU.mult), reads=[lf], writes=[lf])
                for n in range(NTILE):
                    S.op("pe", lambda e, n=n: e.matmul(psG[:, 0:4], trif[:, 0, :], lf[:, n, 0:4], start=True, stop=True), reads=[trif, lf], writes=[psG])
                    S.op("pe", lambda e, n=n: e.matmul(psG[:, 4:8], trif[:, 1, :], lf[:, n, 4:8], start=True, stop=True), reads=[trif, lf], writes=[psG])
                    S.op("pe", lambda e, n=n: e.matmul(psG[:, 8:16], trif[:, 2, :], lf[:, n, 0:8], start=True, stop=True), reads=[trif, lf], writes=[psG])
                    S.op("dve", lambda e, n=n: e.tensor_copy(cum[:, n, :], psG[:, 0:8]), reads=[psG], writes=[cum])
                    S.op("act", lambda e, n=n: e.activation(dec[:, n, :], psG[:, 8:16], AF.Exp), reads=[psG], writes=[dec])
                esv = es.ap.rearrange("p n (k h) -> p n k h", k=2)
                cumv = cum.ap.rearrange("p n (k h) -> p n k h", k=2)
                S.op("dve", lambda e: e.tensor_tensor(esv, Gv[:, :, 0::2, :], cumv, ALU.subtract), reads=[G, cum], writes=[es])
                S.op("act", lambda e: e.activation(es[:, :, :], es[:, :, :], AF.Exp), reads=[es], writes=[es])
                S.op("act", lambda e: e.activation(eb[:, :, :], cum[:, :, :], AF.Exp), reads=[cum], writes=[eb])
                for c_ in range(8):
                    S.op("pool", lambda e, c_=c_: e.memset(Sf[c_][:, :], 0.0), writes=[Sf[c_]])
                    S.op("pool", lambda e, c_=c_: e.memset(Sb[c_][:, :], 0.0), writes=[Sb[c_]])
                order = {0: [16, 17] + list(range(16)), 1: [17, 16] + list(range(15, -1, -1))}
                k = 0
                for step in range(NTILE):
                    for dr in range(2):
                        n = order[dr][step]
                        tsl = slice(n * 128, (n + 1) * 128)
                        for h in range(4):
                            ci = dr * 4 + h
                            gi = dr * 4 + h
                            a = k % NSET; k += 1
                            pA, pB, pC = psA[a], psB[a], psC[a]
                            pt, ve, tS, nE, dN = PT[a], vE[a], tmpS[a], ne[a], dn[a]
                            sF, sB = Sf[ci], Sb[ci]
                            S.op("pe", lambda e: e.matmul(pA[:, :], kT[:, h, tsl], qT[:, h, tsl], start=True, stop=True), reads=[kT, qT], writes=[pA])
                            S.op("dve", lambda e: e.scalar_tensor_tensor(pt[:, :], pA[:, :], es[:, n, gi:gi + 1], trib[:, dr, :], ALU.mult, ALU.mult),
                                 reads=[pA, es, trib], writes=[pt])
                            S.op("pool", lambda e: e.tensor_scalar(ve[:, :], v1[:, n, h, :], es[:, n, gi:gi + 1], None, ALU.mult), reads=[v1, es], writes=[ve])
                            S.op("pe", lambda e: e.matmul(pB[:, :], pt[:, :], v1[:, n, h, :], start=True, stop=False), reads=[pt, v1], writes=[pB])
                            S.op("pe", lambda e: e.matmul(pB[:, :], qT[:, h, tsl], sB[:, :], start=False, stop=True), reads=[qT, sB], writes=[pB])
                            S.op("pe", lambda e: e.matmul(pC[:, :], ktm[:, n, h * 128:(h + 1) * 128], ve[:, :], start=True, stop=True), reads=[ktm, ve], writes=[pC])
                            S.op("act", lambda e: e.activation(tS[:, :], pC[:, :], AF.Copy, scale=dec[:, n, gi:gi + 1]), reads=[pC, dec], writes=[tS])
                            S.op("dve", lambda e: e.scalar_tensor_tensor(sF[:, :], sF[:, :], dec[:, n, gi:gi + 1], tS[:, :], ALU.mult, ALU.add),
                                 reads=[sF, dec, tS], writes=[sF])
                            S.op("act", lambda e: e.copy(sB[:, :], sF[:, :]), reads=[sF], writes=[sB])
                            S.op("dve", lambda e: e.tensor_scalar(nE[:, :], pB[:, :], eb[:, n, gi:gi + 1], None, ALU.mult), reads=[pB, eb], writes=[nE])
                            S.op("act", lambda e: e.activation(dN[:, :], nE[:, 128:129], AF.Abs), reads=[nE], writes=[dN])
                            S.op("dve", lambda e: e.tensor_scalar(dN[:, :], dN[:, :], 1.0, None, ALU.max), reads=[dN], writes=[dN])
                            S.op("dve", lambda e: e.reciprocal(dN[:, :], dN[:, :]), reads=[dN], writes=[dN])
                            S.op("dve", lambda e: e.scalar_tensor_tensor(hm[:, n, h * 128:(h + 1) * 128], nE[:, 0:128], dN[:, :], hm[:, n, h * 128:(h + 1) * 128],
                                                                         ALU.mult, ALU.add), reads=[nE, dN, hm], writes=[hm])
                for n in range(NTILE):
                    og = osg[n % 2]; abt = ab[n % 2]; at = aT[n % 2]
                    S.dma("sp", og[:, :], sc["osig"][b, n * 128:(n + 1) * 128, :], reads=[sc["osig"]], writes=[og])
                    S.op("pool", lambda e, n=n: e.tensor_tensor(sq[:, :], hm[:, n, :], hm[:, n, :], ALU.mult), reads=[hm], writes=[sq])
                    S.op("dve", lambda e: e.tensor_reduce(ss4[:, :], sq.ap.rearrange("p (h d) -> p h d", h=4), AX.X, ALU.add), reads=[sq], writes=[ss4])
                    S.op("dve", lambda e: e.tensor_copy(rs4[:, :], ss4[:, :]), reads=[ss4], writes=[rs4])
                    self.rsqrt(rs4, (slice(None), slice(None)), 1.0 / 128, EPS)
                    S.op("dve", lambda e, n=n: e.tensor_tensor(sq.ap.rearrange("p (h d) -> p h d", h=4), hm[:, n, :].rearrange("p (h d) -> p h d", h=4),
                                                                rs4.ap.unsqueeze(2).broadcast_to([128, 4, 128]), ALU.mult), reads=[hm, rs4], writes=[sq])
                    S.op("pool", lambda e: e.tensor_tensor(sq[:, :], sq[:, :], hn[:, :], ALU.mult), reads=[sq, hn], writes=[sq])
                    S.op("dve", lambda e, og=og, abt=abt: e.tensor_tensor(abt[:, :], sq[:, :], og[:, :], ALU.mult), reads=[sq, og], writes=[abt])
                    for j in range(4):
                        S.op("pe", lambda e, j=j, abt=abt: e.transpose(pTo[:, j * 128:(j + 1) * 128], abt[:, j * 128:(j + 1) * 128], self.ident_b[:, :]),
                             reads=[abt, self.ident_b], writes=[pTo])
                    S.op("act", lambda e, at=at: e.copy(at[:, :, :], pTo.ap.rearrange("p (j t) -> p j t", j=4)), reads=[pTo], writes=[at])
                    S.dma("pool", sc["mixT"][b, 0:512, n * 128:(n + 1) * 128].rearrange("(j p) t -> p j t", p=128), at[:, :, :], reads=[at], writes=[sc["mixT"]])

    def phase_filters(self, l):
        S, I, sc = self.S, self.inp, self.scr
        i = l // 2
        with Phase(S, f"hf{l}") as P:
            w1 = P.sb([33, 64], F32, "w1"); w2 = P.sb([64, 64], F32, "w2"); w3 = P.sb([64, 2048], F32, "w3")
            S.dma("sp", w1[:, :], I["e_f_w1"][i], writes=[w1])
            S.dma("sp", w2[:, :], I["e_f_w2"][i], writes=[w2])
            S.dma("sp", w3[:, :], I["e_f_w3"][i], writes=[w3])
            fb = P.sb([64, 3], F32, "fb")
            S.dma("pool", fb[:, 0:1], I["e_f_freq"][i].rearrange("(p o) -> p o", o=1), writes=[fb])
            S.dma("pool", fb[:, 1:2], I["e_f_b1"][i].rearrange("(p o) -> p o", o=1), writes=[fb])
            S.dma("pool", fb[:, 2:3], I["e_f_b2"][i].rearrange("(p o) -> p o", o=1), writes=[fb])
            fbb = P.sb([64, 2], F32, "fbb")
            S.op("dve", lambda e: e.tensor_scalar(fbb[:, :], fb[:, 1:3], fb[:, 0:1], None, ALU.mult), reads=[fb], writes=[fbb])
            embT = P.sb([33, SEQ], F32, "embT")
            h1 = P.sb([64, SEQ], F32, "h1"); h2 = P.sb([64, SEQ], F32, "h2")
            pre = P.sb([64, 512], F32, "pre")
            kk = P.sb([64, 512], F32, "kk")
            ps = [P.ps([128, 512], F32, "ps") for _ in range(2)]
            hd = P.sb([128, SEQ], F32, "hd"); dcy = P.sb([128, SEQ], F32, "dcy")
            junk = P.sb([128, SEQ], F32, "junk")
            ssq = P.sb([128, 1], F32, "ssq")
            for nm, L, filt in (("L", SEQ, sc["filtL"]), ("C", CTX, sc["filtC"])):
                cw = min(512, L)
                S.dma("sp", embT[:, 0:L], I["embT_" + nm][:, :], writes=[embT])
                for (src, wgt, dst, K, bi) in ((embT, w1, h1, 33, 0), (h1, w2, h2, 64, 1)):
                    for c0 in range(0, L, cw):
                        p = ps[(c0 // cw) % 2]
                        S.op("pe", lambda e, p=p, src=src, wgt=wgt, c0=c0, K=K: e.matmul(p[0:64, 0:cw], wgt[0:K, :], src[0:K, c0:c0 + cw], start=True, stop=True),
                             reads=[src, wgt], writes=[p])
                        S.op("dve", lambda e, p=p, bi=bi: e.tensor_scalar(pre[:, 0:cw], p[0:64, 0:cw], fb[:, 0:1], fbb[:, bi:bi + 1], ALU.mult, ALU.add),
                             reads=[p, fb, fbb], writes=[pre])
                        MAGIC = 12582912.0
                        S.op("dve", lambda e: e.tensor_scalar(kk[:, 0:cw], pre[:, 0:cw], float(1.0 / (2 * PI)), MAGIC, ALU.mult, ALU.add), reads=[pre], writes=[kk])
                        S.op("dve", lambda e: e.tensor_scalar(kk[:, 0:cw], kk[:, 0:cw], -MAGIC, None, ALU.add), reads=[kk], writes=[kk])
                        S.op("dve", lambda e: e.scalar_tensor_tensor(pre[:, 0:cw], kk[:, 0:cw], float(-2 * PI), pre[:, 0:cw], ALU.mult, ALU.add), reads=[kk, pre], writes=[pre])
                        S.op("dve", lambda e: e.tensor_scalar(kk[:, 0:cw], pre[:, 0:cw], float(PI), None, ALU.is_gt), reads=[pre], writes=[kk])
                        S.op("dve", lambda e: e.scalar_tensor_tensor(pre[:, 0:cw], kk[:, 0:cw], float(-2 * PI), pre[:, 0:cw], ALU.mult, ALU.add), reads=[kk, pre], writes=[pre])
                        S.op("dve", lambda e: e.tensor_scalar(kk[:, 0:cw], pre[:, 0:cw], float(-PI), None, ALU.is_lt), reads=[pre], writes=[kk])
                        S.op("dve", lambda e: e.scalar_tensor_tensor(pre[:, 0:cw], kk[:, 0:cw], float(2 * PI), pre[:, 0:cw], ALU.mult, ALU.add), reads=[kk, pre], writes=[pre])
                        S.op("dve", lambda e: e.tensor_scalar(pre[:, 0:cw], pre[:, 0:cw], 3.141592, -3.141592, ALU.min, ALU.max), reads=[pre], writes=[pre])
                        S.op("act", lambda e, dst=dst, c0=c0: e.activation(dst[:, c0:c0 + cw], pre[:, 0:cw], AF.Sin), reads=[pre], writes=[dst])
                for ch in range(16):
                    o_, d_, cq = ch // 8, (ch // 4) % 2, ch % 4
                    S.dma("sp", dcy[:, 0:L], I["dec_" + nm][cq * 128:(cq + 1) * 128, :], writes=[dcy])
                    for c0 in range(0, L, cw):
                        p = ps[(c0 // cw) % 2]
                        S.op("pe", lambda e, p=p, ch=ch, c0=c0: e.matmul(p[:, 0:cw], w3[:, ch * 128:(ch + 1) * 128], h2[:, c0:c0 + cw], start=True, stop=True),
                             reads=[w3, h2], writes=[p])
                        S.op("dve", lambda e, p=p, c0=c0: e.tensor_tensor(hd[:, c0:c0 + cw], p[:, 0:cw], dcy[:, c0:c0 + cw], ALU.mult), reads=[p, dcy], writes=[hd])
                    S.op("dve", lambda e: e.memset(ssq[:, :], 0.0), writes=[ssq])
                    S.op("act", lambda e: e.activation(junk[:, 0:L], hd[:, 0:L], AF.Square, accum_out=ssq[:, :]), reads=[hd, ssq], writes=[junk, ssq])
                    self.rsqrt(ssq, (slice(None), slice(None)), 1.0, EPS)
                    S.op("act", lambda e: e.activation(junk[:, 0:L], hd[:, 0:L], AF.Copy, scale=ssq[:, :]), reads=[hd, ssq], writes=[junk])
                    S.dma("pool", filt[o_, d_, cq * 128:(cq + 1) * 128, :], junk[:, 0:L], reads=[junk], writes=[filt])

    def phase_hyena(self, l):
        S, I, sc = self.S, self.inp, self.scr
        i = l // 2
        with Phase(S, f"hy{l}") as P:
            cw = P.sb([128, 3, 4, 3], F32, "cw")
            cb = P.sb([128, 3, 4], F32, "cb")
            hd = P.sb([128, 2, 4], F32, "hd")
            for part in range(3):
                for tap in range(3):
                    S.dma("pool", cw[:, part, :, tap], I["e_conv_w"][i, tap, part * 512:(part + 1) * 512].rearrange("(q p) -> p q", p=128),
                          writes=[cw], allow_slow_non_contiguous=True)
                S.dma("pool", cb[:, part, :], I["e_conv_b"][i, part * 512:(part + 1) * 512].rearrange("(q p) -> p q", p=128), writes=[cb], allow_slow_non_contiguous=True)
            for o_ in range(2):
                S.dma("pool", hd[:, o_, :], I["e_hy_d"][i, o_, :].rearrange("(q p) -> p q", p=128), writes=[hd], allow_slow_non_contiguous=True)
            for (L, tok0, filt) in ((SEQ, 0, sc["filtL"]), (CTX, SEQ, sc["filtC"])):
                raw = P.sb([128, NB, L + 2], F32, "raw")
                ys = [P.sb([128, NB, L], F32, "ys") for _ in range(3)]
                accd = P.sb([128, NB, L], F32, "accd"); accp = P.sb([128, NB, L], F32, "accp")
                gf = P.sb([128, L], F32, "gf"); gbk = P.sb([128, L], F32, "gbk")
                ob = P.sb([128, NB, L], BF16, "ob")
                S.op("pool", lambda e: e.memset(raw[:, :, :], 0.0), writes=[raw])
                for cq in range(4):
                    for part in range(3):
                        r0 = part * 512 + cq * 128
                        S.dma("sp", raw[:, :, 1:L + 1], sc["hy"][:, r0:r0 + 128, tok0:tok0 + L].rearrange("b p t -> p b t"), reads=[sc["hy"]], writes=[raw])
                        y = ys[part]
                        S.op("dve", lambda e, y=y, part=part: e.tensor_scalar(y[:, :, :], raw[:, :, 0:L], cw[:, part, cq, 0:1], cb[:, part, cq:cq + 1], ALU.mult, ALU.add),
                             reads=[raw, cw, cb], writes=[y])
                        S.op("dve", lambda e, y=y, part=part: e.scalar_tensor_tensor(y[:, :, :], raw[:, :, 1:L + 1], cw[:, part, cq, 1:2], y[:, :, :], ALU.mult, ALU.add),
                             reads=[raw, cw, y], writes=[y])
                        S.op("dve", lambda e, y=y, part=part: e.scalar_tensor_tensor(y[:, :, :], raw[:, :, 2:L + 2], cw[:, part, cq, 2:3], y[:, :, :], ALU.mult, ALU.add),
                             reads=[raw, cw, y], writes=[y])
                    z = ys[0]
                    for o_ in range(2):
                        S.dma("sp", gf[:, :], filt[o_, 0, cq * 128:(cq + 1) * 128, :], reads=[filt], writes=[gf])
                        S.dma("act", gbk[:, :], filt[o_, 1, cq * 128:(cq + 1) * 128, :], reads=[filt], writes=[gbk])
                        S.op("dve", lambda e: e.tensor_scalar(accd[:, :, :], z[:, :, :], hd[:, o_, cq:cq + 1], None, ALU.mult), reads=[z, hd], writes=[accd])
                        S.op("pool", lambda e: e.memset(accp[:, :, :], 0.0), writes=[accp])
                        taps = [(0, s) for s in range(L)] + [(1, s) for s in range(L)]
                        total = sum(L - s for _, s in taps)
                        pool_budget = total * self.hy_pool_frac
                        acc_cost = 0.0
                        for (dr, s) in taps:
                            g = gf if dr == 0 else gbk
                            use_pool = False
                            if acc_cost < pool_budget and ((s % 3) == 1 or self.hy_pool_frac >= 0.5):
                                use_pool = True
                                acc_cost += (L - s)
                            eng, acc = ("pool", accp) if use_pool else ("dve", accd)
                            if dr == 0:
                                oa = acc[:, :, s:L]; za = z[:, :, 0:L - s]
                            else:
                                oa = acc[:, :, 0:L - s]; za = z[:, :, s:L]
                            S.op(eng, lambda e, oa=oa, za=za, g=g, s=s: e.scalar_tensor_tensor(oa, za, g[:, s:s + 1], oa, ALU.mult, ALU.add),
                                 reads=[z, g, acc], writes=[acc])
                        S.op("dve", lambda e: e.tensor_tensor(accd[:, :, :], accd[:, :, :], accp[:, :, :], ALU.add), reads=[accd, accp], writes=[accd])
                        gate = ys[1 + o_]
                        if o_ == 0:
                            S.op("dve", lambda e: e.tensor_tensor(z[:, :, :], accd[:, :, :], gate[:, :, :], ALU.mult), reads=[accd, gate], writes=[z])
                        else:
                            S.op("dve", lambda e: e.tensor_tensor(ob[:, :, :], accd[:, :, :], gate[:, :, :], ALU.mult), reads=[accd, gate], writes=[ob])
                    S.dma("sp", sc["mixT"][:, 512 + cq * 128:512 + (cq + 1) * 128, tok0:tok0 + L].rearrange("b p t -> p b t"), ob[:, :, :], reads=[ob], writes=[sc["mixT"]])

    def phase_filters_pe(self, l):
        S, I, sc = self.S, self.inp, self.scr
        i = l // 2
        with Phase(S, f"hg{l}") as P:
            w1 = P.sb([33, 64], F32, "w1"); w2 = P.sb([64, 64], F32, "w2"); w3 = P.sb([64, 2048], F32, "w3")
            S.dma("sp", w1[:, :], I["e_f_w1"][i], writes=[w1])
            S.dma("sp", w2[:, :], I["e_f_w2"][i], writes=[w2])
            S.dma("sp", w3[:, :], I["e_f_w3"][i], writes=[w3])
            fb = P.sb([64, 3], F32, "fb")
            S.dma("pool", fb[:, 0:1], I["e_f_freq"][i].rearrange("(p o) -> p o", o=1), writes=[fb])
            S.dma("pool", fb[:, 1:2], I["e_f_b1"][i].rearrange("(p o) -> p o", o=1), writes=[fb])
            S.dma("pool", fb[:, 2:3], I["e_f_b2"][i].rearrange("(p o) -> p o", o=1), writes=[fb])
            fbb = P.sb([64, 2], F32, "fbb")
            S.op("dve", lambda e: e.tensor_scalar(fbb[:, :], fb[:, 1:3], fb[:, 0:1], None, ALU.mult), reads=[fb], writes=[fbb])
            embT = P.sb([33, SEQ], F32, "embT")
            h1 = P.sb([64, SEQ], F32, "h1")
            h2s = [P.sb([64, SEQ], F32, "h2") for _ in range(2)]
            pre = P.sb([64, 512], F32, "pre"); kk = P.sb([64, 512], F32, "kk")
            ps = [P.ps([128, 512], F32, "ps") for _ in range(2)]
            hd = P.sb([128, SEQ], F32, "hd"); dcy = P.sb([128, SEQ], F32, "dcy")
            junk = P.sb([128, SEQ], F32, "junk")
            ssq = P.sb([128, 1], F32, "ssq")
            gt = P.sb([128, 2 * SEQ], F32, "gt")
            gbf = P.sb([128, 2 * SEQ], BF16, "gbf")
            MAGIC = 12582912.0
            for nm, L, gp in (("L", SEQ, sc["gpL"]), ("C", CTX, sc["gpC"])):
                cw = min(512, L)
                for rv in range(2):
                    h2 = h2s[rv]
                    S.dma("sp", embT[:, 0:L], I[("embTr_" if rv else "embT_") + nm][:, :], writes=[embT])
                    for (src_, wgt, dst, K, bi) in ((embT, w1, h1, 33, 0), (h1, w2, h2, 64, 1)):
                        for c0 in range(0, L, cw):
                            p = ps[(c0 // cw) % 2]
                            S.op("pe", lambda e: e.matmul(p[0:64, 0:cw], wgt[0:K, :], src_[0:K, c0:c0 + cw], start=True, stop=True), reads=[src_, wgt], writes=[p])
                            S.op("dve", lambda e: e.tensor_scalar(pre[:, 0:cw], p[0:64, 0:cw], fb[:, 0:1], fbb[:, bi:bi + 1], ALU.mult, ALU.add), reads=[p, fb, fbb], writes=[pre])
                            S.op("dve", lambda e: e.tensor_scalar(kk[:, 0:cw], pre[:, 0:cw], float(1.0 / (2 * PI)), MAGIC, ALU.mult, ALU.add), reads=[pre], writes=[kk])
                            S.op("dve", lambda e: e.tensor_scalar(kk[:, 0:cw], kk[:, 0:cw], -MAGIC, None, ALU.add), reads=[kk], writes=[kk])
                            S.op("dve", lambda e: e.scalar_tensor_tensor(pre[:, 0:cw], kk[:, 0:cw], float(-2 * PI), pre[:, 0:cw], ALU.mult, ALU.add), reads=[kk, pre], writes=[pre])
                            S.op("dve", lambda e: e.tensor_scalar(kk[:, 0:cw], pre[:, 0:cw], float(PI), None, ALU.is_gt), reads=[pre], writes=[kk])
                            S.op("dve", lambda e: e.scalar_tensor_tensor(pre[:, 0:cw], kk[:, 0:cw], float(-2 * PI), pre[:, 0:cw], ALU.mult, ALU.add), reads=[kk, pre], writes=[pre])
                            S.op("dve", lambda e: e.tensor_scalar(kk[:, 0:cw], pre[:, 0:cw], float(-PI), None, ALU.is_lt), reads=[pre], writes=[kk])
                            S.op("dve", lambda e: e.scalar_tensor_tensor(pre[:, 0:cw], kk[:, 0:cw], float(2 * PI), pre[:, 0:cw], ALU.mult, ALU.add), reads=[kk, pre], writes=[pre])
                            S.op("dve", lambda e: e.tensor_scalar(pre[:, 0:cw], pre[:, 0:cw], 3.141592, -3.141592, ALU.min, ALU.max), reads=[pre], writes=[pre])
                            S.op("act", lambda e: e.activation(dst[:, c0:c0 + cw], pre[:, 0:cw], AF.Sin), reads=[pre], writes=[dst])
                for o_ in range(2):
                    for cq in range(4):
                        S.op("pool", lambda e: e.memset(gt[:, 0:1], 0.0), writes=[gt])
                        for d_ in (1, 0):
                            ch = o_ * 8 + d_ * 4 + cq
                            h2 = h2s[d_]
                            S.dma("sp", dcy[:, 0:L], I[("decr_" if d_ else "dec_") + nm][cq * 128:(cq + 1) * 128, :], writes=[dcy])
                            for c0 in range(0, L, cw):
                                p = ps[(c0 // cw) % 2]
                                S.op("pe", lambda e: e.matmul(p[:, 0:cw], w3[:, ch * 128:(ch + 1) * 128], h2[:, c0:c0 + cw], start=True, stop=True), reads=[w3, h2], writes=[p])
                                S.op("dve", lambda e: e.tensor_tensor(hd[:, c0:c0 + cw], p[:, 0:cw], dcy[:, c0:c0 + cw], ALU.mult), reads=[p, dcy], writes=[hd])
                            S.op("dve", lambda e: e.memset(ssq[:, :], 0.0), writes=[ssq])
                            S.op("act", lambda e: e.activation(junk[:, 0:L], hd[:, 0:L], AF.Square, accum_out=ssq[:, :]), reads=[hd, ssq], writes=[junk, ssq])
                            self.rsqrt(ssq, (slice(None), slice(None)), 1.0, EPS)
                            if d_ == 1:
                                S.op("act", lambda e: e.activation(gt[:, 1:L + 1], hd[:, 0:L], AF.Copy, scale=ssq[:, :]), reads=[hd, ssq], writes=[gt])
                            else:
                                S.op("act", lambda e: e.activation(junk[:, 0:L], hd[:, 0:L], AF.Copy, scale=ssq[:, :]), reads=[hd, ssq], writes=[junk])
                                S.op("dve", lambda e: e.tensor_copy(gt[:, L + 1:2 * L], junk[:, 1:L]), reads=[junk], writes=[gt])
                                S.op("dve", lambda e: e.tensor_tensor(gt[:, L:L + 1], gt[:, L:L + 1], junk[:, 0:1], ALU.add), reads=[junk, gt], writes=[gt])
                        S.op("act", lambda e: e.copy(gbf[:, 0:2 * L], gt[:, 0:2 * L]), reads=[gt], writes=[gbf])
                        S.dma("pool", gp[o_, cq * 128:(cq + 1) * 128, :], gbf[:, 0:2 * L], reads=[gbf], writes=[gp])

    def phase_hyena_pe(self, l):
        S, I, sc = self.S, self.inp, self.scr
        nc = self.nc
        i = l // 2
        with Phase(S, f"hp{l}") as P:
            self.load_consts(P)
            ident_f = P.sb([128, 128], F32, "identf")
            S.dma("sp", ident_f[:, :], I["ident_f"][:, :], writes=[ident_f])
            jrev = P.sb([128, 128], BF16, "jrev")
            S.dma("sp", jrev[:, :], I["jrev_b"][:, :], writes=[jrev])
            cw = P.sb([128, 3, 4, 3], F32, "cw")
            cb = P.sb([128, 3, 4], F32, "cb")
            for part in range(3):
                for tap in range(3):
                    S.dma("pool", cw[:, part, :, tap], I["e_conv_w"][i, tap, part * 512:(part + 1) * 512].rearrange("(q p) -> p q", p=128),
                          writes=[cw], allow_slow_non_contiguous=True)
                S.dma("pool", cb[:, part, :], I["e_conv_b"][i, part * 512:(part + 1) * 512].rearrange("(q p) -> p q", p=128), writes=[cb], allow_slow_non_contiguous=True)
            dbc = P.sb([128, 2, 512], F32, "dbc")
            for o_ in range(2):
                S.dma("pool", dbc[:, o_, :], I["e_hy_d"][i, o_:o_ + 1, :].partition_broadcast(128), writes=[dbc])
            psT = [P.ps([128, 512], F32, "psT") for _ in range(2)]
            psB = P.ps([128, 1024], BF16, "psB")
            for (L, tok0, gp, nb) in ((SEQ, 0, sc["gpL"], SEQ // 128), (CTX, SEQ, sc["gpC"], CTX // 128)):
              with Phase(S, f"hp{l}_{L}") as P:
                  GW = 2 * L - 128
                  raw = P.sb([128, NB, L + 2], F32, "raw")
                  ybuf = P.sb([128, NB, L], F32, "ys")
                  tm = [P.sb([128, NB * nb, 128], F32, "tm") for _ in range(3)]
                  zb = P.sb([128, NB * nb * 128], BF16, "zb")
                  KB = self.hy_kb
                  NQ = 128 // KB
                  zq = P.sb([KB, NQ, NB, nb, 128], BF16, "zq")
                  Yt = P.sb([128, NB * nb, 128], F32, "Yt")
                  outb = P.sb([128, NB * nb, 128], BF16, "outb")
                  WW = 2 * L - KB
                  NRING = 8
                  ring = [P.sb([KB, WW], BF16, "G") for _ in range(NRING)]
                  psY = [P.ps([128, 16, NB, nb], F32, "psY") for _ in range(2)]
                  ob = P.sb([128, NB, L], BF16, "ob")
                  S.op("pool", lambda e: e.memset(raw[:, :, :], 0.0), writes=[raw])
                  gtensor = gp.ap.tensor
                  nG = 0
                  for cq in range(4):
                      for part in range(3):
                          r0 = part * 512 + cq * 128
                          S.dma("sp", raw[:, :, 1:L + 1], sc["hy"][:, r0:r0 + 128, tok0:tok0 + L].rearrange("b p t -> p b t"), reads=[sc["hy"]], writes=[raw])
                          y = ybuf
                          S.op("dve", lambda e: e.tensor_scalar(y[:, :, :], raw[:, :, 0:L], cw[:, part, cq, 0:1], cb[:, part, cq:cq + 1], ALU.mult, ALU.add),
                               reads=[raw, cw, cb], writes=[y])
                          S.op("dve", lambda e: e.scalar_tensor_tensor(y[:, :, :], raw[:, :, 1:L + 1], cw[:, part, cq, 1:2], y[:, :, :], ALU.mult, ALU.add),
                               reads=[raw, cw, y], writes=[y])
                          S.op("dve", lambda e: e.scalar_tensor_tensor(y[:, :, :], raw[:, :, 2:L + 2], cw[:, part, cq, 2:3], y[:, :, :], ALU.mult, ALU.add),
                               reads=[raw, cw, y], writes=[y])
                          k = 0
                          blocks = [(b, j) for b in range(NB) for j in range(nb)]
                          for g0 in range(0, len(blocks), 4):
                              grp = blocks[g0:g0 + 4]
                              pt = psT[(g0 // 4) % 2]
                              for q, (b, j) in enumerate(grp):
                                  S.op("pe", lambda e: e.transpose(pt[:, q * 128:(q + 1) * 128], y[:, b, j * 128:(j + 1) * 128], ident_f[:, :]),
                                       reads=[y, ident_f], writes=[pt])
                              n_ = len(grp)
                              S.op("act" if (g0 // 4) % 2 else "dve",
                                   (lambda e: e.copy(tm[part][:, g0:g0 + n_, :], pt[:, 0:n_ * 128].rearrange("p (q c) -> p q c", q=n_))) if (g0 // 4) % 2 else
                                   (lambda e: e.tensor_copy(tm[part][:, g0:g0 + n_, :], pt[:, 0:n_ * 128].rearrange("p (q c) -> p q c", q=n_))),
                                   reads=[pt], writes=[tm[part]])
                      zt = tm[0]
                      for o_ in range(2):
                          S.op("act", lambda e: e.copy(zb.ap.rearrange("p (n c) -> p n c", c=128), zt[:, :, :]), reads=[zt], writes=[zb])
                          NEL = NB * nb * 128
                          zqv = zq.ap.rearrange("p q b j c -> p q (b j c)")
                          kq = 0
                          for q0 in range(0, NEL, 512):
                              q1 = min(NEL, q0 + 512)
                              for hi in range(NQ):
                                  pt = psT[kq % 2]; kq += 1
                                  S.op("pe", lambda e: e.matmul(pt[0:KB, 0:q1 - q0], jrev[:, KB * hi:KB * (hi + 1)], zb[:, q0:q1], start=True, stop=True),
                                       reads=[jrev, zb], writes=[pt])
                                  if kq % 2:
                                      S.op("dve", lambda e: e.tensor_copy(zqv[:, hi, q0:q1], pt[0:KB, 0:q1 - q0]), reads=[pt], writes=[zq])
                                  else:
                                      S.op("act", lambda e: e.copy(zqv[:, hi, q0:q1], pt[0:KB, 0:q1 - q0]), reads=[pt], writes=[zq])
                          for c0 in range(0, 128, 16):
                              py = psY[(c0 // 16) % 2]
                              for cs in range(16):
                                  c = c0 + cs
                                  G = ring[nG % NRING]
                                  q_ = ("sp", "act")[nG % 2]
                                  nG += 1
                                  src_ap = bass.AP(tensor=gtensor, offset=(o_ * 512 + cq * 128 + c) * (2 * L) + 1, ap=[[1, KB], [1, WW]])
                                  S.dma(q_, G[:, :], src_ap, reads=[gp], writes=[G])
                                  dlist = [0]
                                  for dd in range(1, nb):
                                      dlist += [dd, -dd]
                                  nmm = len(dlist) * NQ
                                  km = 0
                                  for di, d_ in enumerate(dlist):
                                      j0, j1 = max(0, -d_), min(nb, nb - d_)
                                      i0, i1 = j0 + d_, j1 + d_
                                      for hi in range(NQ):
                                          w0 = (d_ + nb - 1) * 128 + KB * hi
                                          S.op("pe", lambda e: e.matmul(py[:, cs, :, i0:i1], G[:, w0:w0 + 128], zq[:, hi, :, j0:j1, c],
                                                                        start=(km == 0), stop=(km == nmm - 1)), reads=[G, zq], writes=[py])
                                          km += 1
                              S.op("dve", lambda e: e.tensor_copy(Yt[:, :, c0:c0 + 16].rearrange("p (b j) c -> p b j c", b=NB), py.ap.rearrange("p c b j -> p b j c")),
                                   reads=[py], writes=[Yt])
                          dsl = dbc[:, o_, cq * 128:(cq + 1) * 128].unsqueeze(1).broadcast_to([128, NB * nb, 128])
                          t1v = ybuf.ap.rearrange("p b (j c) -> p (b j) c", c=128)
                          S.op("pool", lambda e: e.tensor_tensor(t1v, zt[:, :, :], dsl, ALU.mult), reads=[zt, dbc], writes=[ybuf])
                          S.op("dve", lambda e: e.tensor_tensor(Yt[:, :, :], Yt[:, :, :], t1v, ALU.add), reads=[Yt, ybuf], writes=[Yt])
                          gate = tm[1 + o_]
                          if o_ == 0:
                              S.op("dve", lambda e: e.tensor_tensor(zt[:, :, :], Yt[:, :, :], gate[:, :, :], ALU.mult), reads=[Yt, gate], writes=[zt])
                          else:
                              S.op("dve", lambda e: e.tensor_tensor(outb[:, :, :], Yt[:, :, :], gate[:, :, :], ALU.mult), reads=[Yt, gate], writes=[outb])
                      for b in range(NB):
                          for j0 in range(0, nb, 8):
                              n_ = min(8, nb - j0)
                              for q in range(n_):
                                  S.op("pe", lambda e: e.transpose(psB[:, q * 128:(q + 1) * 128], outb[:, b * nb + j0 + q, :], self.ident_b[:, :]),
                                       reads=[outb, self.ident_b], writes=[psB])
                              S.op("act", lambda e: e.copy(ob[:, b, j0 * 128:(j0 + n_) * 128], psB[:, 0:n_ * 128]), reads=[psB], writes=[ob])
                      S.dma("sp", sc["mixT"][:, 512 + cq * 128:512 + (cq + 1) * 128, tok0:tok0 + L].rearrange("b p t -> p b t"), ob[:, :, :], reads=[ob], writes=[sc["mixT"]])

    def phase_outproj(self, l, w_ap, last):
        S, I, sc = self.S, self.inp, self.scr
        with Phase(S, f"op{l}") as P:
            stage = [P.sb([128, 8, 512], F32, "stg") for _ in range(2)]
            W = self.load_weight_bf16(P, w_ap, D, D, "Wo", stage)
            g1 = self.load_gate_bc(P, l, 0)
            mT = [P.sb([128, 8, 512], BF16, "mT") for _ in range(2)]
            xts = [P.sb([128, D], F32, "xt") for _ in range(2)]
            ps = [P.ps([128, 512], F32, "ps") for _ in range(4)]
            tmp = [P.sb([128, 512], F32, "tmp") for _ in range(2)]
            n = 0
            for gi, (b, tok0, nt, r) in enumerate(self.groups(with_ctx=not last)):
                m = mT[gi % 2]
                ntok = nt * 128
                S.dma("sp", m[:, :, 0:ntok], sc["mixT"][b, :, tok0:tok0 + ntok].rearrange("(j p) t -> p j t", p=128), reads=[sc["mixT"]], writes=[m])
                for ti in range(nt):
                    t0 = tok0 + ti * 128
                    xt = xts[n % 2]
                    S.dma("act", xt[:, :], sc["xs"][b, t0:t0 + 128, :], reads=[sc["xs"]], writes=[xt])
                    for half in range(2):
                        p = ps[(2 * n + half) % 4]; tm = tmp[half]
                        for j in range(8):
                            S.op("pe", lambda e, j=j, p=p, half=half: e.matmul(p[:, :], m[:, j, ti * 128:(ti + 1) * 128], W[:, j, half * 512:(half + 1) * 512],
                                                                               start=(j == 0), stop=(j == 7)), reads=[m, W], writes=[p])
                        S.op("dve", lambda e, p=p, tm=tm, half=half: e.tensor_tensor(tm[:, :], p[:, :], g1[:, r, half * 512:(half + 1) * 512], ALU.mult), reads=[p, g1], writes=[tm])
                        S.op("pool", lambda e, tm=tm, half=half, xt=xt: e.tensor_tensor(xt[:, half * 512:(half + 1) * 512], xt[:, half * 512:(half + 1) * 512], tm[:, :], ALU.add),
                             reads=[xt, tm], writes=[xt])
                    S.dma("pool", sc["xs"][b, t0:t0 + 128, :], xt[:, :], reads=[xt], writes=[sc["xs"]])
                    n += 1

    def phase_mlp(self, l, last):
        S, I, sc = self.S, self.inp, self.scr
        GT = 2
        with Phase(S, f"mlp{l}") as P:
            self.load_consts(P)
            scp, shp = self.load_mod_fm(P, l, 1)
            g2 = self.load_gate_bc(P, l, 1)
            stage = [P.sb([128, 8, 256], F32, "stg") for _ in range(2)]
            W1 = self.load_weight_bf16(P, I["w_mlp_in"][l], D, DFF, "W1", stage, cw=256)
            W2 = self.load_weight_bf16(P, I["w_mlp_out"][l], DFF, D, "W2", stage, cw=256)
            tmp = self.norm_tmp(P)
            xts = [P.sb([128, D], F32, "xt") for _ in range(3)]
            hT = P.sb([128, 8, GT * 128], BF16, "hT")
            aT = P.sb([128, 32, GT * 128], BF16, "aT")
            ps = [P.ps([128, 512], F32, "ps") for _ in range(4)]
            tm = [P.sb([128, 512], F32, "tm") for _ in range(2)]
            rl = [P.sb([128, GT * 128], F32, "rl") for _ in range(2)]
            grp = []
            for b in range(NB):
                for k in range(0, SEQ // 128, GT):
                    grp.append((b, k * 128, GT, b))
                if not last:
                    grp.append((b, SEQ, 2, 2))
            n = 0
            for gi, (b, tok0, nt, r) in enumerate(grp):
                ntok = nt * 128
                gx = []
                for ti in range(nt):
                    xt = xts[n % 3]; n += 1
                    gx.append(xt)
                    t0 = tok0 + ti * 128
                    S.dma("sp", xt[:, :], sc["xs"][b, t0:t0 + 128, :], reads=[sc["xs"]], writes=[xt])
                    self.norm_to_hT(xt, scp, shp, r, hT, ti, tmp)
                for fc in range(32):
                    p = ps[fc % 2]; rr = rl[fc % 2]
                    for j in range(8):
                        S.op("pe", lambda e, j=j, p=p, fc=fc: e.matmul(p[:, 0:ntok], W1[:, j, fc * 128:(fc + 1) * 128], hT[:, j, 0:ntok], start=(j == 0), stop=(j == 7)),
                             reads=[W1, hT], writes=[p])
                    S.op("act", lambda e, p=p, rr=rr: e.activation(rr[:, 0:ntok], p[:, 0:ntok], AF.Relu), reads=[p], writes=[rr])
                    eng = "dve" if fc % 2 == 0 else "pool"
                    S.op(eng, lambda e, rr=rr, fc=fc: e.tensor_tensor(aT[:, fc, 0:ntok], rr[:, 0:ntok], rr[:, 0:ntok], ALU.mult), reads=[rr], writes=[aT])
                for ti in range(nt):
                    xt = gx[ti]
                    t0 = tok0 + ti * 128
                    for half in range(2):
                        p = ps[2 + half]; t_ = tm[half]
                        for fc in range(32):
                            S.op("pe", lambda e, fc=fc, p=p, half=half: e.matmul(p[:, :], aT[:, fc, ti * 128:(ti + 1) * 128], W2[:, fc, half * 512:(half + 1) * 512],
                                                                                 start=(fc == 0), stop=(fc == 31)), reads=[aT, W2], writes=[p])
                        S.op("dve", lambda e, p=p, t_=t_, half=half: e.tensor_tensor(t_[:, :], p[:, :], g2[:, r, half * 512:(half + 1) * 512], ALU.mult), reads=[p, g2], writes=[t_])
                        S.op("pool", lambda e, t_=t_, half=half, xt=xt: e.tensor_tensor(xt[:, half * 512:(half + 1) * 512], xt[:, half * 512:(half + 1) * 512], t_[:, :], ALU.add),
                             reads=[xt, t_], writes=[xt])
                    if last:
                        S.dma("pool", self.out[b, t0:t0 + 128, :], xt[:, :], reads=[xt])
                    else:
                        S.dma("pool", sc["xs"][b, t0:t0 + 128, :], xt[:, :], reads=[xt], writes=[sc["xs"]])

    def phase_odd_proj(self, l):
        S, I, sc = self.S, self.inp, self.scr
        i = l // 2
        with Phase(S, f"oq{l}") as P:
            self.load_consts(P)
            scp, shp = self.load_mod_fm(P, l, 0)
            stage = [P.sb([128, 8, 256], F32, "stg") for _ in range(2)]
            W = self.load_weight_bf16(P, I["o_w_qkv"][i], D, 3 * D, "Wqkv", stage, cw=256)
            gn = P.sb([128, 2, 64], F32, "gn")
            S.dma("sp", gn[:, 0, :], I["o_qn"][i:i + 1, :].partition_broadcast(128), writes=[gn])
            S.dma("sp", gn[:, 1, :], I["o_kn"][i:i + 1, :].partition_broadcast(128), writes=[gn])
            S.op("dve", lambda e: e.tensor_scalar(gn[:, 0, :], gn[:, 0, :], 0.125, None, ALU.mult), reads=[gn], writes=[gn])
            tmp = self.norm_tmp(P)
            xts = [P.sb([128, D], F32, "xt") for _ in range(2)]
            hT = P.sb([128, 8, 128], BF16, "hT")
            psq = [P.ps([128, 512], F32, "psq") for _ in range(6)]
            pT2 = P.ps([128, 1024], BF16, "pT2")
            sq = P.sb([128, D], F32, "sq")
            ss16 = P.sb([128, 16], F32, "ss16")
            qn = [P.sb([128, D], BF16, "qn") for _ in range(2)]
            vb = [P.sb([128, D], BF16, "vb") for _ in range(2)]
            qT_s = [P.sb([128, 8, 128], BF16, "qTs") for _ in range(2)]
            n = 0
            for b in range(NB):
                for ti in range(NTILE):
                    t0 = ti * 128
                    r = b if t0 < SEQ else 2
                    xt = xts[n % 2]; vbt = vb[n % 2]; n += 1
                    S.dma("sp", xt[:, :], sc["xs"][b, t0:t0 + 128, :], reads=[sc["xs"]], writes=[xt])
                    self.norm_to_hT(xt, scp, shp, r, hT, 0, tmp)
                    for cg in range(6):
                        ps = psq[cg]
                        for j in range(8):
                            S.op("pe", lambda e, j=j, ps=ps, cg=cg: e.matmul(ps[:, :], hT[:, j, :], W[:, j, cg * 512:(cg + 1) * 512], start=(j == 0), stop=(j == 7)),
                                 reads=[hT, W], writes=[ps])
                    for which in range(2):
                        qnt = qn[which]
                        for half in range(2):
                            ps = psq[which * 2 + half]
                            S.op("act", lambda e, ps=ps, half=half: e.activation(sq[:, half * 512:(half + 1) * 512], ps[:, :], AF.Square), reads=[ps], writes=[sq])
                        S.op("dve", lambda e: e.tensor_reduce(ss16[:, :], sq.ap.rearrange("p (h d) -> p h d", h=16), AX.X, ALU.add), reads=[sq], writes=[ss16])
                        self.rsqrt(ss16, (slice(None), slice(None)), 1.0 / 64, EPS)
                        for half in range(2):
                            ps = psq[which * 2 + half]
                            S.op("dve", lambda e, ps=ps, half=half: e.tensor_tensor(sq[:, half * 512:(half + 1) * 512].rearrange("p (h d) -> p h d", h=8),
                                                                                    ps[:, :].rearrange("p (h d) -> p h d", h=8),
                                                                                    ss16[:, half * 8:(half + 1) * 8].unsqueeze(2).broadcast_to([128, 8, 64]), ALU.mult),
                                 reads=[ps, ss16], writes=[sq])
                        S.op("pool", lambda e, qnt=qnt, which=which: e.tensor_tensor(qnt.ap.rearrange("p (h d) -> p h d", h=16), sq.ap.rearrange("p (h d) -> p h d", h=16),
                                                                                    gn[:, which, :].unsqueeze(1).broadcast_to([128, 16, 64]), ALU.mult),
                             reads=[sq, gn], writes=[qnt])
                        for j in range(8):
                            S.op("pe", lambda e, j=j, qnt=qnt: e.transpose(pT2[:, j * 128:(j + 1) * 128], qnt[:, j * 128:(j + 1) * 128], self.ident_b[:, :]),
                                 reads=[qnt, self.ident_b], writes=[pT2])
                        qs = qT_s[which]
                        S.op("act", lambda e, qs=qs: e.copy(qs[:, :, :], pT2.ap.rearrange("p (j t) -> p j t", j=8)), reads=[pT2], writes=[qs])
                        dst = sc["oqT"] if which == 0 else sc["okT"]
                        S.dma("sp", dst[b, :, :, t0:t0 + 128].rearrange("h d t -> d h t"), qs[:, :, :], reads=[qs], writes=[dst])
                    S.op("dve", lambda e, vbt=vbt: e.tensor_copy(vbt[:, 0:512], psq[4][:, :]), reads=[psq[4]], writes=[vbt])
                    S.op("act", lambda e, vbt=vbt: e.copy(vbt[:, 512:1024], psq[5][:, :]), reads=[psq[5]], writes=[vbt])
                    S.dma("pool", sc["ov"][b, t0:t0 + 128, :, :].rearrange("p h d -> p (h d)"), vbt[:, :], reads=[vbt], writes=[sc["ov"]])

    def phase_natten(self, l, last=False):
        S, I, sc = self.S, self.inp, self.scr
        i = l // 2
        NR = SEQ // GRID_W
        with Phase(S, f"na{l}") as P:
            self.load_consts(P)
            cm = P.sb([64, 64], F32, "cm")
            S.dma("sp", cm[:, :], I["cmaskT"][:, :], writes=[cm])
            rp = P.sb([64, 2, 15, 64], F32, "rp")
            ebm = P.sb([64, 2, 15, 64], BF16, "ebm")
            qT = [P.sb([128, TT], BF16, "qT") for _ in range(2)]
            kT = [P.sb([128, TT], BF16, "kT") for _ in range(2)]
            v64 = [P.sb([64, 36, 2, 65], BF16, "v64") for _ in range(2)]
            psL = [P.ps([64, 8, 64], F32, "psL") for _ in range(2)]
            psX = [P.ps([64, 4, 64], F32, "psX") for _ in range(2)]
            psO = [P.ps([64, 65], F32, "psO") for _ in range(2)]
            pT = P.ps([128, 512], BF16, "pT")
            eL = [P.sb([64, 8, 64], BF16, "eL") for _ in range(2)]
            pL = [P.sb([64, 8, 64], BF16, "pL") for _ in range(2)]
            eX = [P.sb([64, 4, 64], BF16, "eX") for _ in range(2)]
            ao = [P.sb([64, 36, 128], BF16, "ao") for _ in range(2)]
            oT = [P.sb([128, 512], BF16, "oT") for _ in range(2)]
            rcs = [P.sb([64, 1], F32, "rc") for _ in range(2)]
            k = 0
            it = 0
            for hp in range(8):
                S.dma("sp", rp[:, :, :, :], I["rpbx"][i, 2 * hp:2 * hp + 2].rearrange("h k r q -> k h r q"), writes=[rp])
                S.op("act", lambda e: e.activation(rp[:, :, :, :], rp[:, :, :, :], AF.Exp), reads=[rp], writes=[rp])
                S.op("dve", lambda e: e.tensor_tensor(ebm.ap.rearrange("k h r q -> k (h r) q"), rp.ap.rearrange("k h r q -> k (h r) q"),
                                                     cm.ap.unsqueeze(1).broadcast_to([64, 30, 64]), ALU.mult), reads=[rp, cm], writes=[ebm])
                for b in range(NB):
                    q_, k_, v_, a_ = qT[it % 2], kT[it % 2], v64[it % 2], ao[it % 2]
                    it += 1
                    S.dma("sp", q_[:, :], sc["oqT"][b, hp, :, :], reads=[sc["oqT"]], writes=[q_])
                    S.dma("act", k_[:, :], sc["okT"][b, hp, :, :], reads=[sc["okT"]], writes=[k_])
                    S.op("pool", lambda e, v_=v_: e.memset(v_[:, :, :, :], 1.0), writes=[v_])
                    for hl_ in range(2):
                        S.dma("pool", v_[:, :, hl_, 0:64], sc["ov"][b, :, 2 * hp + hl_, :].rearrange("(n p) d -> p n d", p=64), reads=[sc["ov"]], writes=[v_])
                    nq = 32 if last else 36
                    for r in range(nq):
                        lat = r < NR
                        for hl in range(2):
                            a = k % 2; k += 1
                            hs = slice(hl * 64, (hl + 1) * 64)
                            pl_, px_, po_ = psL[a], psX[a], psO[a]
                            el_, pp_, ex_ = eL[a], pL[a], eX[a]
                            rc_ = rcs[a]
                            qs = slice(r * 64, (r + 1) * 64)
                            if lat:
                                rs = min(max(r - 4, 0), NR - 8)
                                for j in range(8):
                                    S.op("pe", lambda e, j=j: e.matmul(pl_[:, j, :], k_[hs, (rs + j) * 64:(rs + j + 1) * 64], q_[hs, qs], start=True, stop=True),
                                         reads=[k_, q_], writes=[pl_])
                            for j in range(4):
                                S.op("pe", lambda e, j=j: e.matmul(px_[:, j, :], k_[hs, SEQ + j * 64:SEQ + (j + 1) * 64], q_[hs, qs], start=True, stop=True),
                                     reads=[k_, q_], writes=[px_])
                            if lat:
                                S.op("act", lambda e: e.activation(el_[:, :, :], pl_[:, :, :], AF.Exp), reads=[pl_], writes=[el_])
                                d0 = rs - r + 7
                                S.op("dve", lambda e: e.tensor_tensor(pp_[:, :, :], el_[:, :, :], ebm[:, hl, d0:d0 + 8, :], ALU.mult), reads=[el_, ebm], writes=[pp_])
                            S.op("act", lambda e: e.activation(ex_[:, :, :], px_[:, :, :], AF.Exp), reads=[px_], writes=[ex_])
                            first = True
                            if lat:
                                for j in range(8):
                                    S.op("pe", lambda e, j=j, first=first: e.matmul(po_[:, :], pp_[:, j, :], v_[:, rs + j, hl, :], start=first, stop=False),
                                         reads=[pp_, v_], writes=[po_])
                                    first = False
                            for j in range(4):
                                S.op("pe", lambda e, j=j, first=first: e.matmul(po_[:, :], ex_[:, j, :], v_[:, 32 + j, hl, :], start=first, stop=(j == 3)),
                                     reads=[ex_, v_], writes=[po_])
                                first = False
                            S.op("dve", lambda e: e.reciprocal(rc_[:, :], po_[:, 64:65]), reads=[po_], writes=[rc_])
                            S.op("dve", lambda e: e.tensor_scalar(a_[:, r, hs], po_[:, 0:64], rc_[:, :], None, ALU.mult), reads=[po_, rc_], writes=[a_])
                    for g0 in range(0, nq, 8):
                        ng = min(8, nq - g0)
                        o_ = oT[(g0 // 8) % 2]
                        for j in range(ng):
                            S.op("pe", lambda e, j=j: e.transpose(pT[:, j * 64:(j + 1) * 64], a_[:, g0 + j, :], self.ident_b[0:64, 0:64]),
                                 reads=[a_, self.ident_b], writes=[pT])
                        S.op("act", lambda e: e.copy(o_[:, 0:ng * 64], pT[:, 0:ng * 64]), reads=[pT], writes=[o_])
                        S.dma("pool", sc["mixT"][b, hp * 128:(hp + 1) * 128, g0 * 64:(g0 + ng) * 64], o_[:, 0:ng * 64], reads=[o_], writes=[sc["mixT"]])

    def build(self):
        self.declare()
        stop = getattr(self, "stop_after", None)
        seq = [("init", self.phase_init), ("mod", self.phase_mod)]
        for l in range(self.depth):
            last = (l == self.depth - 1)
            if l % 2 == 0:
                seq += [(f"eproj{l}", lambda l=l: self.phase_even_proj(l)),
                        (f"filt{l}", lambda l=l: (self.phase_filters_pe(l) if self.hy_pe else self.phase_filters(l))),
                        (f"mlstm{l}", lambda l=l: self.phase_mlstm(l)),
                        (f"hyena{l}", lambda l=l: (self.phase_hyena_pe(l) if self.hy_pe else self.phase_hyena(l))),
                        (f"oproj{l}", lambda l=l, last=last: self.phase_outproj(l, self.inp["e_w_out"][l // 2], last))]
            else:
                seq += [(f"qproj{l}", lambda l=l: self.phase_odd_proj(l)),
                        (f"natten{l}", lambda l=l, last=last: self.phase_natten(l, last)),
                        (f"oproj{l}", lambda l=l, last=last: self.phase_outproj(l, self.inp["o_w_out"][l // 2], last))]
            seq.append((f"mlp{l}", lambda l=l, last=last: self.phase_mlp(l, last)))
        for name, fn in seq:
            if self.only is not None and name not in self.only:
                continue
            fn()
            if stop == name:
                break
        self.S.barrier()
        return self.nc


_CONSTS = None


def make_in_maps(inputs, needed=None):
    global _CONSTS
    if _CONSTS is None:
        _CONSTS = host_consts()
    f = lambda a: np.ascontiguousarray(np.asarray(a, dtype=np.float32))
    shared = {k: f(inputs[k]) for k in ("w_mod", "b_mod", "w_mlp_in", "w_mlp_out", "e_w_in", "e_gate_b", "e_hnorm", "e_conv_w", "e_conv_b",
                                        "e_f_w1", "e_f_b1", "e_f_w2", "e_f_b2", "e_f_w3", "e_f_freq", "e_hy_d", "e_w_out", "o_w_qkv", "o_qn", "o_kn", "o_w_out")}
    shared["rpbx"] = rpb_expand(f(inputs["o_rpb"]))
    shared.update(_CONSTS)
    x = f(inputs["x"]); c = f(inputs["c"]); ctx = f(inputs["ctx"]); cc = f(inputs["c_ctx"])
    maps = []
    for core in range(NCORES):
        m = dict(shared)
        b0 = core * NB
        m["x"] = np.ascontiguousarray(x[b0:b0 + NB])
        m["ctx"] = np.ascontiguousarray(ctx[b0:b0 + NB])
        c3 = np.stack([c[b0], c[b0 + 1], cc], 0)
        m["cT"] = np.ascontiguousarray(c3.reshape(3, 8, 128).transpose(2, 1, 0))
        if needed is not None:
            m = {k: v for k, v in m.items() if k in needed}
        maps.append(m)
    return maps


def kernel(**inputs):
    bld = Builder(depth=DEPTH)
    nc = bld.build()
    maps = make_in_maps(inputs, set(bld.inp.keys()))
    res = run_bass_kernel_spmd(nc, maps, core_ids=list(range(NCORES)))
    out = np.concatenate([np.asarray(r["out"], dtype=np.float32) for r in res.results], axis=0)
    return out
```
